# Optimizing a Trainium2 kernel written in Bass

```python
import jax, jax.numpy as jnp
from jax import lax
import numpy as np

D_MODEL = 1024
BATCH = 2
SEQ = 8192
DEPTH = 2

N_MIXERS = 2
N_HGRN_LAYERS = (DEPTH + 1) // 2
N_ATTN_LAYERS = DEPTH // 2

HGRN_EXPAND = 128
HGRN_HEADS = D_MODEL // HGRN_EXPAND
HGRN_KEY_DIM = HGRN_EXPAND
HGRN_VAL_DIM = D_MODEL // HGRN_HEADS
HGRN_QK_WIDTH = HGRN_HEADS * HGRN_KEY_DIM
HGRN_V_WIDTH = HGRN_HEADS * HGRN_VAL_DIM
HGRN_IN_WIDTH = 2 * HGRN_QK_WIDTH + 2 * HGRN_V_WIDTH
HGRN_CHUNK = 64

ATTN_HEAD_DIM = 128
DILATED_GROUPS = ((128, 1), (512, 4), (2048, 16))
HEADS_PER_GROUP = 4
N_GROUPS = len(DILATED_GROUPS)
ATTN_HEADS = HEADS_PER_GROUP * N_GROUPS
ATTN_WIDTH = ATTN_HEADS * ATTN_HEAD_DIM
ROPE_THETA = 10000.0

D_FF = ((8 * D_MODEL // 3 + 255) // 256) * 256

NORM_EPS = 1e-6

kernel_name = "hgrn2_dilated_swa_interleaved_trunk"


def rmsnorm(x, gain):
    xf = x.astype(jnp.float32)
    y = xf * lax.rsqrt(jnp.mean(xf * xf, axis=-1, keepdims=True) + NORM_EPS)
    return (y * gain.astype(jnp.float32)).astype(x.dtype)


def rope_tables(seq_len):
    inv_freq = 1.0 / (ROPE_THETA ** (jnp.arange(0, ATTN_HEAD_DIM, 2, dtype=jnp.float32) / ATTN_HEAD_DIM))
    pos = jnp.arange(seq_len, dtype=jnp.float32)
    ang = pos[:, None] * inv_freq[None, :]
    return jnp.cos(ang)[:, None, :], jnp.sin(ang)[:, None, :]


def apply_rope(x, cos, sin):
    xf = x.astype(jnp.float32)
    x1, x2 = jnp.split(xf, 2, axis=-1)
    out = jnp.concatenate([x1 * cos - x2 * sin, x2 * cos + x1 * sin], axis=-1)
    return out.astype(x.dtype)


def hgrn2_mixer(u, w_in, lower_bound, out_gain, w_out):
    B, S, _ = u.shape
    H, K, V, C = HGRN_HEADS, HGRN_KEY_DIM, HGRN_VAL_DIM, HGRN_CHUNK
    proj = u @ w_in
    q, f, i, g = jnp.split(proj, [HGRN_QK_WIDTH, 2 * HGRN_QK_WIDTH, 2 * HGRN_QK_WIDTH + HGRN_V_WIDTH], axis=-1)
    lb = lower_bound.astype(jnp.float32)
    forget = lb + (1.0 - lb) * jax.nn.sigmoid(f.astype(jnp.float32))
    key = 1.0 - forget
    log_f = jnp.log(forget)
    query = jax.nn.silu(q.astype(jnp.float32))
    value = i.astype(jnp.float32)

    def to_chunks(t, d):
        return t.reshape(B, S // C, C, H, d).transpose(1, 0, 3, 2, 4)

    xs = (to_chunks(query, K), to_chunks(key, K), to_chunks(value, V), to_chunks(log_f, K))
    causal = jnp.tril(jnp.ones((C, C), dtype=bool))[:, :, None]

    def chunk_step(state, inp):
        qc, kc, vc, gc = inp
        b = jnp.cumsum(gc, axis=2)
        o_inter = jnp.einsum('bhck,bhkv->bhcv', qc * jnp.exp(b), state)
        diff = b[:, :, :, None, :] - b[:, :, None, :, :]
        decay = jnp.exp(jnp.where(causal, diff, -jnp.inf))
        scores = jnp.einsum('bhtk,bhsk,bhtsk->bhts', qc, kc, decay)
        o_intra = jnp.einsum('bhts,bhsv->bhtv', scores, vc)
        b_last = b[:, :, -1, :]
        k_to_end = kc * jnp.exp(b_last[:, :, None, :] - b)
        new_state = jnp.exp(b_last)[..., None] * state + jnp.einsum('bhck,bhcv->bhkv', k_to_end, vc)
        return new_state, o_inter + o_intra

    state0 = jnp.zeros((B, H, K, V), dtype=jnp.float32)
    _, o = lax.scan(chunk_step, state0, xs)
    o = o.transpose(1, 0, 3, 2, 4).reshape(B, S, H, V)
    o = rmsnorm(o, out_gain)
    o = o * jax.nn.silu(g.astype(jnp.float32)).reshape(B, S, H, V)
    return o.reshape(B, S, H * V).astype(u.dtype) @ w_out


def dilated_window_group(q, k, v, window, dilation):
    B, S, Hg, D = q.shape
    span = window // dilation
    L = S // dilation
    nb = -(-L // span)
    Lp = nb * span

    def to_blocks(t):
        t = t.reshape(B, L, dilation, Hg, D).transpose(0, 2, 3, 1, 4)
        t = jnp.pad(t, ((0, 0), (0, 0), (0, 0), (0, Lp - L), (0, 0)))
        return t.reshape(B, dilation, Hg, nb, span, D)

    qb, kb, vb = to_blocks(q), to_blocks(k), to_blocks(v)

    def with_prev(t):
        prev = jnp.pad(t, ((0, 0), (0, 0), (0, 0), (1, 0), (0, 0), (0, 0)))[:, :, :, :-1]
        return jnp.concatenate([prev, t], axis=4)

    kw, vw = with_prev(kb), with_prev(vb)
    scores = jnp.einsum('bdhnqe,bdhnke->bdhnqk', qb, kw, preferred_element_type=jnp.float32) * (D ** -0.5)
    blk = jnp.arange(nb)[:, None, None] * span
    qpos = blk + jnp.arange(span)[None, :, None]
    kpos = blk - span + jnp.arange(2 * span)[None, None, :]
    mask = (kpos <= qpos) & (kpos >= qpos - span) & (kpos >= 0)
    scores = jnp.where(mask, scores, -jnp.inf)
    m = jnp.max(scores, axis=-1, keepdims=True)
    p = jnp.exp(scores - m)
    l = jnp.sum(p, axis=-1, keepdims=True)
    out = jnp.einsum('bdhnqk,bdhnke->bdhnqe', p, vw.astype(jnp.float32)) / l
    lse = (m + jnp.log(l))[..., 0]
    out = out.reshape(B, dilation, Hg, Lp, D)[:, :, :, :L].transpose(0, 3, 1, 2, 4).reshape(B, S, Hg, D)
    lse = lse.reshape(B, dilation, Hg, Lp)[:, :, :, :L].transpose(0, 3, 1, 2).reshape(B, S, Hg)
    return out, lse


def dilated_attention_mixer(u, w_qkv, w_out, cos, sin):
    B, S, _ = u.shape
    qkv = (u @ w_qkv).reshape(B, S, 3, ATTN_HEADS, ATTN_HEAD_DIM)
    q = apply_rope(qkv[:, :, 0], cos, sin)
    k = apply_rope(qkv[:, :, 1], cos, sin)
    v = qkv[:, :, 2]
    outs, lses = [], []
    for gi, (window, dilation) in enumerate(DILATED_GROUPS):
        hs = slice(gi * HEADS_PER_GROUP, (gi + 1) * HEADS_PER_GROUP)
        o_g, lse_g = dilated_window_group(q[:, :, hs], k[:, :, hs], v[:, :, hs], window, dilation)
        outs.append(o_g)
        lses.append(lse_g)
    o = jnp.stack(outs, axis=2)
    lse = jnp.stack(lses, axis=2)
    alpha = jax.nn.softmax(lse, axis=2)
    o = (o * alpha[..., None]).reshape(B, S, ATTN_WIDTH).astype(u.dtype)
    return o @ w_out


def swiglu_ffn(u, w_in, w_down):
    gate, up = jnp.split(u @ w_in, 2, axis=-1)
    return (jax.nn.silu(gate) * up) @ w_down


def setup_inputs(seed: int = 0) -> dict:
    key = jax.random.key(seed)
    ks = jax.random.split(key, 13)
    f32 = jnp.float32

    def dense(k, shape, fan_in):
        return jax.random.normal(k, shape, f32) * (fan_in ** -0.5)

    def gain(k, shape):
        return 1.0 + 0.02 * jax.random.normal(k, shape, f32)

    return {
        "x": jax.random.normal(ks[0], (BATCH, SEQ, D_MODEL), f32),
        "norm_mix": gain(ks[1], (DEPTH, D_MODEL)),
        "norm_ffn": gain(ks[2], (DEPTH, D_MODEL)),
        "hgrn_w_in": dense(ks[3], (N_HGRN_LAYERS, D_MODEL, HGRN_IN_WIDTH), D_MODEL),
        "hgrn_lb_logits": 0.5 * jax.random.normal(ks[4], (DEPTH + 1, HGRN_QK_WIDTH), f32),
        "hgrn_out_norm": gain(ks[5], (N_HGRN_LAYERS, HGRN_VAL_DIM)),
        "hgrn_w_out": dense(ks[6], (N_HGRN_LAYERS, HGRN_V_WIDTH, D_MODEL), HGRN_V_WIDTH),
        "attn_w_qkv": dense(ks[7], (N_ATTN_LAYERS, D_MODEL, 3 * ATTN_WIDTH), D_MODEL),
        "attn_w_out": dense(ks[8], (N_ATTN_LAYERS, ATTN_WIDTH, D_MODEL), ATTN_WIDTH),
        "ffn_w_in": dense(ks[9], (DEPTH, D_MODEL, 2 * D_FF), D_MODEL),
        "ffn_w_down": dense(ks[10], (DEPTH, D_FF, D_MODEL), D_FF),
        "final_norm": gain(ks[11], (D_MODEL,)),
    }


def reference(x, norm_mix, norm_ffn, hgrn_w_in, hgrn_lb_logits, hgrn_out_norm, hgrn_w_out,
              attn_w_qkv, attn_w_out, ffn_w_in, ffn_w_down, final_norm):
    lb_table = jnp.cumsum(jax.nn.softmax(hgrn_lb_logits.astype(jnp.float32), axis=0), axis=0)
    cos, sin = rope_tables(x.shape[1])
    h = x
    for layer in range(DEPTH):
        u = rmsnorm(h, norm_mix[layer])
        if layer % N_MIXERS == 0:
            a = layer // N_MIXERS
            mix = hgrn2_mixer(u, hgrn_w_in[a], lb_table[layer], hgrn_out_norm[a], hgrn_w_out[a])
        else:
            a = layer // N_MIXERS
            mix = dilated_attention_mixer(u, attn_w_qkv[a], attn_w_out[a], cos, sin)
        h = h + mix
        h = h + swiglu_ffn(rmsnorm(h, norm_ffn[layer]), ffn_w_in[layer], ffn_w_down[layer])
    return rmsnorm(h, final_norm)
```

```python
import numpy as np
from contextlib import ExitStack
import concourse.bass as bass
import concourse.mybir as mybir
from concourse.bass_utils import run_bass_kernel_spmd

F32 = mybir.dt.float32
BF16 = mybir.dt.bfloat16
AF = mybir.ActivationFunctionType
ALU = mybir.AluOpType

NCORES = 8
D = 1024
T = 2048
NT = T // 128
DFF = 2816
NJ = DFF // 128
EPS = 1e-6


import os as _os
MERGE_OFF = float(_os.environ.get("MOFF", "0.1"))
MERGE_SCALE = float(_os.environ.get("MSC", "0.9"))
SAME_ENGINE_INORDER = ("pe",)
STRICT_KEYS = ("ssq", "ssq1", "ssqo", "rstd", "rstd1", "rstdo", "st", "al", "ebl", "consts", "lmx", "lb_", "lbc")


class Prog:
    ENG = ("pe", "act", "dve", "pool", "sp")
    ATTR = {"pe": "tensor", "act": "scalar", "dve": "vector", "pool": "gpsimd", "sp": "sync"}

    def __init__(self, nc, es):
        self.nc = nc
        self.es = es
        self.streams = {e: [] for e in self.ENG}
        self.sem = {e: es.enter_context(nc.semaphore("s_" + e)) for e in ("pe", "act", "dve", "pool")}
        self.cnt = {e: 0 for e in ("pe", "act", "dve", "pool")}
        self.dsem = {}
        self.dcnt = {}
        self.known = {e: {} for e in self.ENG}
        self.lastw = {}
        self.readers = {}
        self._rec = None

    def _src_sem(self, src):
        return self.sem[src] if isinstance(src, str) else self.dsem[src[1]]

    def _deps(self, reads, writes):
        w = {}
        self._strict = {}

        def add(src, val, k):
            if val > w.get(src, 0):
                w[src] = val
            if (k if isinstance(k, str) else k[0]) in STRICT_KEYS and val > self._strict.get(src, 0):
                self._strict[src] = val

        for k in reads:
            lw = self.lastw.get(k)
            if lw:
                add(lw[0], lw[1], k)
        for k in writes:
            lw = self.lastw.get(k)
            if lw:
                add(lw[0], lw[1], k)
            for s, v in self.readers.get(k, {}).items():
                add(s, v, k)
        return w

    def _emit_waits(self, E, deps):
        for src, val in deps.items():
            if src == E and E in SAME_ENGINE_INORDER:
                if E == "pe" or src not in self._strict:
                    continue
                val = self._strict[src]
            if self.known[E].get(src, 0) >= val:
                continue
            self.known[E][src] = val
            sem = self._src_sem(src)
            self.streams[E].append(lambda eng, sem=sem, val=val: eng.wait_ge(sem, val))

    def record(self, emit):
        self._rec = []
        emit()
        r, self._rec = self._rec, None
        return r

    def group(self, emit):
        if self._rec is None:
            emit()
            return
        outer, self._rec = self._rec, []
        emit()
        inner, self._rec = self._rec, outer
        self._rec.append(("_replay", (inner,)))

    def _replay(self, items):
        for kind, args in items:
            getattr(self, kind)(*args)

    def replay_merged(self, la, lb):
        items = [((i + 0.5) / len(la), 0, i, it) for i, it in enumerate(la)] + [(MERGE_OFF + MERGE_SCALE * (j + 0.5) / len(lb), 1, j, it) for j, it in enumerate(lb)]
        import os
        if os.environ.get('MERGE') == 'seq':
            items.sort(key=lambda t: (t[1], t[2]))
        else:
            items.sort(key=lambda t: (t[0], t[1], t[2]))
        for _, _, _, (kind, args) in items:
            getattr(self, kind)(*args)

    def op(self, E, fn, reads=(), writes=()):
        if self._rec is not None:
            self._rec.append(("op", (E, fn, reads, writes)))
            return
        deps = self._deps(reads, writes)
        self._emit_waits(E, deps)
        self.cnt[E] += 1
        v = self.cnt[E]
        sem = self.sem[E]
        self.streams[E].append(lambda eng, fn=fn, sem=sem: fn(eng).then_inc(sem, 1))
        for k in reads:
            self.readers.setdefault(k, {})[E] = v
        for k in writes:
            self.lastw[k] = (E, v)
            self.readers[k] = {}

    def dma(self, Q, fn, semkey, reads=(), writes=()):
        if self._rec is not None:
            self._rec.append(("dma", (Q, fn, semkey, reads, writes)))
            return
        if semkey not in self.dsem:
            self.dsem[semkey] = self.es.enter_context(self.nc.semaphore("d_" + str(semkey)))
            self.dcnt[semkey] = 0
        deps = self._deps(reads, writes)
        self._emit_waits(Q, deps)
        self.dcnt[semkey] += 16
        v = self.dcnt[semkey]
        sem = self.dsem[semkey]
        self.streams[Q].append(lambda eng, fn=fn, sem=sem: fn(eng).then_inc(sem, 16))
        src = ("d", semkey)
        for k in reads:
            self.readers.setdefault(k, {})[src] = v
        for k in writes:
            self.lastw[k] = (src, v)
            self.readers[k] = {}

    def cc(self, fn, semkey, reads=(), writes=()):
        if semkey not in self.dsem:
            self.dsem[semkey] = self.es.enter_context(self.nc.semaphore("d_" + str(semkey)))
            self.dcnt[semkey] = 0
        deps = self._deps(reads, writes)
        self._emit_waits("pool", deps)
        self.dcnt[semkey] += 1
        v = self.dcnt[semkey]
        sem = self.dsem[semkey]
        self.streams["pool"].append(lambda eng, fn=fn, sem=sem: fn(eng).then_inc(sem, 1))
        src = ("d", semkey)
        for k in reads:
            self.readers.setdefault(k, {})[src] = v
        for k in writes:
            self.lastw[k] = (src, v)
            self.readers[k] = {}

    def wait_dma_done(self, E, semkeys):
        for sk in semkeys:
            sem, val = self.dsem[sk], self.dcnt[sk]
            self.streams[E].append(lambda eng, sem=sem, val=val: eng.wait_ge(sem, val))

    def barrier(self):
        for E in self.ENG:
            deps = {}
            for s in ("pe", "act", "dve", "pool"):
                if self.cnt[s] > 0:
                    deps[s] = self.cnt[s]
            for sk, v in self.dcnt.items():
                if v > 0:
                    deps[("d", sk)] = v
            for src, val in deps.items():
                if src == E:
                    continue
                if self.known[E].get(src, 0) >= val:
                    continue
                self.known[E][src] = val
                sem = self._src_sem(src)
                self.streams[E].append(lambda eng, sem=sem, val=val: eng.wait_ge(sem, val))

    def finalize(self):
        nc = self.nc
        with nc.Block() as block:
            for name in self.ENG:
                stream = self.streams[name]

                def body(eng, stream=stream):
                    for f in stream:
                        f(eng)

                getattr(block, self.ATTR[name])(body)


class Ctx:
    pass


def emit_rsqrt(P, c):
    P.op("act", lambda e: e.activation(out=c.rstd[:, :], in_=c.rstd[:, :], func=AF.Sqrt), reads=["rstd"], writes=["rstd"])
    P.op("dve", lambda e: e.reciprocal(out=c.rstd[:, :], in_=c.rstd[:, :]), reads=["rstd"], writes=["rstd"])


def emit_norm_T(P, c, src_key_fn, src_ap_fn, gain_ap, tag):
    nc = P.nc
    P.op("dve", lambda e: e.memset(c.ssq[:, :], 0.0), writes=["ssq"])
    for t in range(NT):
        P.op("act", lambda e, t=t: e.activation(out=c.junk[:, :], in_=src_ap_fn(t), func=AF.Square,
                                                accum_out=c.ssq[:, t:t + 1]),
             reads=[src_key_fn(t)], writes=["junk", "ssq"])
    P.op("dve", lambda e: e.tensor_scalar(out=c.rstd[:, :], in0=c.ssq[:, :], scalar1=1.0 / D, scalar2=EPS,
                                          op0=ALU.mult, op1=ALU.add), reads=["ssq"], writes=["rstd"])
    emit_rsqrt(P, c)
    for t in range(NT):
        ub = t % 2
        P.op("dve", lambda e, t=t, ub=ub: e.scalar_tensor_tensor(
            out=c.un[:, ub, :], in0=src_ap_fn(t), scalar=c.rstd[:, t:t + 1], in1=gain_ap,
            op0=ALU.mult, op1=ALU.mult), reads=[src_key_fn(t), "rstd", "consts"], writes=[("un", ub)])

        def tr(e, t=t, ub=ub):
            last = None
            for kc in range(8):
                last = e.transpose(out=c.psT[ub][:, kc * 128:(kc + 1) * 128], in_=c.un[:, ub, kc * 128:(kc + 1) * 128],
                                   identity=c.ident[:, :])
            return last

        P.op("pe", tr, reads=[("un", ub), "consts"], writes=[("psT", ub)])
        P.op("act", lambda e, t=t, ub=ub: e.copy(
            out=c.uT[:, :, t * 128:(t + 1) * 128], in_=c.psT[ub][:, :].rearrange("p (k n) -> p k n", k=8)),
            reads=[("psT", ub)], writes=[("uT", t // 4)])


def emit_ffn(P, c, layer, win_d, wd_d):
    groups = [list(range(0, 5)), list(range(5, 10)), list(range(10, 15)), list(range(15, 20)), list(range(20, 22))]
    def load_win(j):
        s = c.win_ctr % 3
        c.win_ctr += 1
        P.dma("pool", lambda e, j=j, s=s: e.dma_start(out=c.win[:, s, :, :], in_=win_d[layer, j]),
              ("win", s), writes=[("win", s)])
        return s

    def load_wd(g):
        s = c.wd_ctr % 2
        c.wd_ctr += 1
        js = groups[g]
        P.dma("pool", lambda e, js=js, s=s: e.dma_start(
            out=c.wd[:, s, 0:len(js), :], in_=wd_d[layer, js[0] * 128:(js[-1] + 1) * 128, :].rearrange("(j p) n -> p j n", p=128)),
            ("wd", s), writes=[("wd", s)])
        return s

    flat = [j for g in groups for j in g]
    win_slots = {}
    for j in flat[:2]:
        win_slots[j] = load_win(j)
    wd_slot = {0: load_wd(0)}
    idx = 0
    for g, js in enumerate(groups):
        if g + 1 < len(groups):
            wd_slot[g + 1] = load_wd(g + 1)
        for jl, j in enumerate(js):
            if idx + 2 < len(flat):
                win_slots[flat[idx + 2]] = load_win(flat[idx + 2])
            s = win_slots[j]
            for tb in range(4):
                pb = c.gu_ctr % 2
                c.gu_ctr += 1

                def mm(e, s=s, tb=tb, pb=pb):
                    last = None
                    for half, ps in ((0, c.psG[pb]), (1, c.psU[pb])):
                        for kc in range(8):
                            last = e.matmul(ps[:, :], lhsT=c.win[:, s, kc, half * 128:(half + 1) * 128],
                                            rhs=c.uT[:, kc, tb * 512:(tb + 1) * 512], start=(kc == 0), stop=(kc == 7))
                    return last

                P.op("pe", mm, reads=[("win", s), ("uT", tb)], writes=[("psG", pb), ("psU", pb)])
                P.op("act", lambda e, pb=pb: e.activation(out=c.sg[:, pb, :], in_=c.psG[pb][:, :], func=AF.Silu),
                     reads=[("psG", pb)], writes=[("sg", pb)])
                P.op("dve", lambda e, pb=pb, jl=jl, tb=tb: e.tensor_tensor(
                    out=c.aT[:, jl, tb * 512:(tb + 1) * 512], in0=c.sg[:, pb, :], in1=c.psU[pb][:, :], op=ALU.mult),
                    reads=[("sg", pb), ("psU", pb)], writes=[("aT", jl, tb)])
            idx += 1
        ws = wd_slot[g]
        for t in range(NT):
            for hf in range(2):
                pb = c.d_ctr % 2
                c.d_ctr += 1

                def mmd(e, t=t, hf=hf, pb=pb, ws=ws, n=len(js)):
                    last = None
                    for jl in range(n):
                        last = e.matmul(c.psD[pb][:, :], lhsT=c.aT[:, jl, t * 128:(t + 1) * 128],
                                        rhs=c.wd[:, ws, jl, hf * 512:(hf + 1) * 512], start=(jl == 0), stop=(jl == n - 1))
                    return last

                P.op("pe", mmd, reads=[("wd", ws)] + [("aT", jl, t // 4) for jl in range(len(js))], writes=[("psD", pb)])
                P.op("dve", lambda e, t=t, hf=hf, pb=pb: e.tensor_tensor(
                    out=c.h[:, t, hf * 512:(hf + 1) * 512], in0=c.h[:, t, hf * 512:(hf + 1) * 512], in1=c.psD[pb][:, :],
                    op=ALU.add), reads=[("psD", pb), ("h", t)], writes=[("h", t)])


NPREV = 48
HQ, HF, HI, HG = 0, 1024, 2048, 3072
NEG = -30000.0
DIL = (1, 4, 16)


def hgrn_F_stages(P, c, xs_ap, xs_key, full, z):
    sl2 = [slice(0, 512), slice(512, 1024)]

    def nb():
        b = c.pp_ctr % 2
        c.pp_ctr += 1
        return b

    def proj(col0):
        b = nb()

        def mm(e):
            last = None
            for kc in range(8):
                last = e.matmul(c.bank[b][:, :], lhsT=c.uTt[:, kc, :], rhs=c.W[:, kc, col0:col0 + 512], start=(kc == 0), stop=(kc == 7))
            return last

        P.op("pe", mm, reads=["uTt", "W"], writes=[("bank", b)])
        return b

    def F1():
        P.op("dve", lambda e: e.memset(c.ssq1[:, :], 0.0), writes=["ssq1"])
        P.op("act", lambda e: e.activation(out=c.un[:, 1, :], in_=xs_ap, func=AF.Square, accum_out=c.ssq1[:, 0:1]),
             reads=[xs_key], writes=[("un", 1), "ssq1"])
        P.op("dve", lambda e: e.tensor_scalar(out=c.rstd1[:, :], in0=c.ssq1[:, :], scalar1=1.0 / D, scalar2=EPS,
                                              op0=ALU.mult, op1=ALU.add), reads=["ssq1"], writes=["rstd1"])
        P.op("act", lambda e: e.activation(out=c.rstd1[:, :], in_=c.rstd1[:, :], func=AF.Ln), reads=["rstd1"], writes=["rstd1"])
        P.op("act", lambda e: e.activation(out=c.rstd1[:, :], in_=c.rstd1[:, :], func=AF.Exp, scale=-0.5), reads=["rstd1"], writes=["rstd1"])
        P.op("pool", lambda e: e.tensor_tensor(out=c.un[:, 0, :], in0=xs_ap, in1=c.gcur[:, :], op=ALU.mult),
             reads=[xs_key, "gcur"], writes=[("un", 0)])

    def F1b():
        def tr_un(e):
            last = None
            for kc in range(8):
                last = e.transpose(out=c.psTb[:, kc * 128:(kc + 1) * 128], in_=c.un[:, 0, kc * 128:(kc + 1) * 128], identity=c.ident[:, :])
            return last

        P.group(lambda: (
            P.op("pe", tr_un, reads=[("un", 0), "consts"], writes=[("bank", 7)]),
            P.op("act", lambda e: e.copy(out=c.uTt[:, :, :], in_=c.psTb[:, :].rearrange("p (k n) -> p k n", k=8)),
                 reads=[("bank", 7)], writes=["uTt"])))

    def F2():
        for hf in range(2):
            sl = sl2[hf]
            b = proj(HF + hf * 512)
            P.op("act", lambda e, b=b, sl=sl: e.activation(out=c.tt[:, sl], in_=c.bank[b][:, :], func=AF.Sigmoid, scale=c.rstd1[:, 0:1]),
                 reads=[("bank", b), "rstd1"], writes=[("tt", hf)])
            P.op("dve", lambda e, sl=sl: e.scalar_tensor_tensor(out=c.tt[:, sl], in0=c.tt[:, sl], scalar=-1.0, in1=c.oml[:, sl],
                                                                op0=ALU.add, op1=ALU.mult),
                 reads=[("tt", hf), "lbc"], writes=[("tt", hf)])

    def F3():
        for hf in range(2):
            sl = sl2[hf]
            b = proj(HI + hf * 512)
            P.op("dve", lambda e, b=b, sl=sl: e.tensor_scalar(out=c.v[z][:, sl], in0=c.bank[b][:, :], scalar1=c.rstd1[:, 0:1], scalar2=0.0,
                                                              op0=ALU.mult, op1=ALU.add), reads=[("bank", b), "rstd1"], writes=[("v", z, hf)])
        if full:
            for hf in range(2):
                sl = sl2[hf]
                b = proj(HQ + hf * 512)
                P.op("act", lambda e, b=b, sl=sl: e.activation(out=c.sq[:, sl], in_=c.bank[b][:, :], func=AF.Silu, scale=c.rstd1[:, 0:1]),
                     reads=[("bank", b), "rstd1"], writes=[("sq", hf)])
            for hf in range(2):
                sl = sl2[hf]
                b = proj(HG + hf * 512)
                P.op("act", lambda e, b=b, sl=sl: e.activation(out=c.gs[z][:, sl], in_=c.bank[b][:, :], func=AF.Silu, scale=c.rstd1[:, 0:1]),
                     reads=[("bank", b), "rstd1"], writes=[("gs", z, hf)])
        for hf in range(2):
            sl = sl2[hf]
            P.op("act", lambda e, sl=sl: e.activation(out=c.logf[:, sl], in_=c.tt[:, sl], func=AF.Ln, bias=c.one1[:, 0:1]),
                 reads=[("tt", hf), "consts"], writes=[("logf", hf)])

    def F4():
        if full:
            for hf in range(2):
                sl = sl2[hf]
                b = nb()
                P.op("pe", lambda e, b=b, sl=sl: e.matmul(c.bank[b][:, :], lhsT=c.prefT[:, :], rhs=c.logf[:, sl], start=True, stop=True),
                     reads=[("logf", hf), "consts"], writes=[("bank", b)])
                P.op("act", lambda e, b=b, sl=sl: e.activation(out=c.e1[:, sl], in_=c.bank[b][:, :], func=AF.Exp),
                     reads=[("bank", b)], writes=[("e1", hf)])
                P.op("dve", lambda e, sl=sl: e.scalar_tensor_tensor(out=c.qeA[:, sl], in0=c.sq[:, sl], scalar=c.ind2[:, 0:1], in1=c.e1[:, sl],
                                                                    op0=ALU.mult, op1=ALU.mult),
                     reads=[("sq", hf), ("e1", hf), "consts"], writes=[("qeA", hf)])
                P.op("dve", lambda e, sl=sl: e.scalar_tensor_tensor(out=c.qeB[:, sl], in0=c.sq[:, sl], scalar=c.ind2[:, 1:2], in1=c.e1[:, sl],
                                                                    op0=ALU.mult, op1=ALU.mult),
                     reads=[("sq", hf), ("e1", hf), "consts"], writes=[("qeB", hf)])
                P.op("act", lambda e, b=b, sl=sl: e.activation(out=c.e1[:, sl], in_=c.bank[b][:, :], func=AF.Exp, scale=-1.0),
                     reads=[("bank", b)], writes=[("e1", hf)])
                P.op("pool", lambda e, sl=sl: e.tensor_tensor(out=c.ke[:, sl], in0=c.tt[:, sl], in1=c.e1[:, sl], op=ALU.mult),
                     reads=[("tt", hf), ("e1", hf)], writes=[("ke", hf)])
        for hf in range(2):
            sl = sl2[hf]
            b = nb()
            P.op("pe", lambda e, b=b, sl=sl: e.matmul(c.bank[b][:, :], lhsT=c.sufT[:, :], rhs=c.logf[:, sl], start=True, stop=True),
                 reads=[("logf", hf), "consts"], writes=[("bank", b)])
            P.op("act", lambda e, b=b, sl=sl: e.activation(out=c.e1[:, sl], in_=c.bank[b][:, :], func=AF.Exp),
                 reads=[("bank", b)], writes=[("e1", hf)])
            P.op("pool", lambda e, sl=sl: e.tensor_tensor(out=c.kend[z][:, sl], in0=c.tt[:, sl], in1=c.e1[:, sl], op=ALU.mult),
                 reads=[("tt", hf), ("e1", hf)], writes=[("kend", z, hf)])

    def F5():
        bb = nb()

        def mm_bl(e):
            last = None
            for h in range(8):
                last = e.matmul(c.bank[bb][:, 2 * h:2 + 2 * h], lhsT=c.logf[:, h * 128:(h + 1) * 128], rhs=c.ind2[:, :],
                                start=True, stop=True)
            return last

        P.op("pe", mm_bl, reads=[("logf", 0), ("logf", 1), "consts"], writes=[("bank", bb)])
        P.op("act", lambda e: e.activation(out=c.ebl[z][:, :], in_=c.bank[bb][:, 0:16], func=AF.Exp), reads=[("bank", bb)], writes=[("ebl", z)])
        if full:
            for nm, src, dst in (("qeA", c.qeA, c.qeTA[z]), ("qeB", c.qeB, c.qeTB[z]), ("ke", c.ke, c.keT[z])):
                def tr_q(e, src=src):
                    last = None
                    for h in range(8):
                        last = e.transpose(out=c.psTb[:, h * 128:(h + 1) * 128], in_=src[:, h * 128:(h + 1) * 128], identity=c.ident[:, :])
                    return last

                P.group(lambda tr_q=tr_q, nm=nm, dst=dst: (
                    P.op("pe", tr_q, reads=[(nm, 0), (nm, 1), "consts"], writes=[("bank", 7)]),
                    P.op("act", lambda e, dst=dst: e.copy(out=dst[:, :, :], in_=c.psTb[:, :].rearrange("p (h n) -> p h n", h=8)),
                         reads=[("bank", 7)], writes=[(nm + "T", z)])))

    return [F1, F1b, F2, F3, F4, F5]


def hgrn_B_stages(P, c, full, z, par, ti):
    sl2 = [slice(0, 512), slice(512, 1024)]

    def kv_group(g4):
        def mmA(e):
            last = None
            for hl in range(4):
                hs = slice((g4 * 4 + hl) * 128, (g4 * 4 + hl + 1) * 128)
                last = e.matmul(c.bank[2][:, hl * 128:(hl + 1) * 128], lhsT=c.kend[z][0:64, hs], rhs=c.v[z][0:64, hs], start=True, stop=True)
            return last

        def mmB(e):
            last = None
            for hl in range(4):
                hs = slice((g4 * 4 + hl) * 128, (g4 * 4 + hl + 1) * 128)
                last = e.matmul(c.bank[6][:, hl * 128:(hl + 1) * 128], lhsT=c.kend[z][64:128, hs], rhs=c.v[z][64:128, hs], start=True, stop=True)
            return last

        rk = [("kend", z, g4), ("v", z, g4)]
        P.op("pe", mmA, reads=rk, writes=[("bank", 2)])
        P.op("pe", mmB, reads=rk, writes=[("bank", 6)])
        for hl in range(4):
            h = g4 * 4 + hl
            cs = slice(hl * 128, (hl + 1) * 128)
            P.op("dve", lambda e, h=h, cs=cs: e.scalar_tensor_tensor(
                out=c.S2[:, h, :], in0=c.S[:, h, :], scalar=c.ebl[z][:, 2 * h:2 * h + 1], in1=c.bank[2][:, cs],
                op0=ALU.mult, op1=ALU.subtract), reads=[("S", h), ("ebl", z), ("bank", 2)], writes=[("S2", h)])
            if full:
                P.op("pool", lambda e, h=h: e.tensor_copy(out=c.Sb[:, h, 2, :], in_=c.S2[:, h, :]), reads=[("S2", h)], writes=[("Sb", h, 2)])
            P.op("dve", lambda e, h=h, cs=cs: e.scalar_tensor_tensor(
                out=c.S[:, h, :], in0=c.S2[:, h, :], scalar=c.ebl[z][:, 2 * h + 1:2 * h + 2], in1=c.bank[6][:, cs],
                op0=ALU.mult, op1=ALU.subtract), reads=[("S2", h), ("ebl", z), ("bank", 6)], writes=[("S", h)])
            P.op("pool", lambda e, h=h: e.tensor_copy(out=c.Sb[:, h, 1 - par, :], in_=c.S[:, h, :]), reads=[("S", h)], writes=[("Sb", h, 1 - par)])

    def sc_group(g4):
        def mm_sc(e):
            last = None
            for hl in range(4):
                h = g4 * 4 + hl
                e.matmul(c.bank[3][:, hl * 128:(hl + 1) * 128], lhsT=c.keT[z][:, h, :], rhs=c.qeTA[z][:, h, :], start=True, stop=False)
                last = e.matmul(c.bank[3][:, hl * 128:(hl + 1) * 128], lhsT=c.keT[z][:, h, :], rhs=c.qeTB[z][:, h, :], start=False, stop=True)
            return last

        P.op("pe", mm_sc, reads=[("keT", z), ("qeAT", z), ("qeBT", z)], writes=[("bank", 3)])
        P.op("dve", lambda e: e.tensor_tensor(out=c.scm[:, :], in0=c.bank[3][:, :], in1=c.nmask4[:, :], op=ALU.mult),
             reads=[("bank", 3), "consts"], writes=["scm"])

    def o_group(g4):
        ob = 4 + g4

        def mm_o(e):
            last = None
            for hl in range(4):
                h = g4 * 4 + hl
                hs = slice(h * 128, (h + 1) * 128)
                oc = slice(hl * 128, (hl + 1) * 128)
                e.matmul(c.bank[ob][:, oc], lhsT=c.scm[:, hl * 128:(hl + 1) * 128], rhs=c.v[z][:, hs], start=True, stop=False)
                e.matmul(c.bank[ob][:, oc], lhsT=c.qeTA[z][:, h, :], rhs=c.Sb[:, h, par, :], start=False, stop=False)
                last = e.matmul(c.bank[ob][:, oc], lhsT=c.qeTB[z][:, h, :], rhs=c.Sb[:, h, 2, :], start=False, stop=True)
            return last

        P.op("pe", mm_o, reads=["scm", ("v", z, g4), ("qeAT", z), ("qeBT", z)] + [("Sb", g4 * 4 + hl, s) for hl in range(4) for s in (par, 2)],
             writes=[("bank", ob)])

    def post():
        P.op("dve", lambda e: e.memset(c.ssqo[:, :], 0.0), writes=["ssqo"])
        for h in range(8):
            ob = 4 + h // 4
            oc = slice((h % 4) * 128, (h % 4 + 1) * 128)
            P.op("act", lambda e, h=h, ob=ob, oc=oc: e.activation(out=c.un[:, 1, 0:128], in_=c.bank[ob][:, oc], func=AF.Square,
                                                                  accum_out=c.ssqo[:, h:h + 1]),
                 reads=[("bank", ob)], writes=[("un", 1), "ssqo"])
        P.op("dve", lambda e: e.tensor_scalar(out=c.rstdo[:, :], in0=c.ssqo[:, :], scalar1=1.0 / 128, scalar2=EPS,
                                              op0=ALU.mult, op1=ALU.add), reads=["ssqo"], writes=["rstdo"])
        P.op("act", lambda e: e.activation(out=c.rstdo[:, :], in_=c.rstdo[:, :], func=AF.Ln), reads=["rstdo"], writes=["rstdo"])
        P.op("act", lambda e: e.activation(out=c.rstdo[:, :], in_=c.rstdo[:, :], func=AF.Exp, scale=-0.5), reads=["rstdo"], writes=["rstdo"])
        for hf in range(2):
            sl = sl2[hf]
            P.op("dve", lambda e, hf=hf, sl=sl: e.tensor_tensor(out=c.og1[:, sl], in0=c.bank[4 + hf][:, :], in1=c.gs[z][:, sl], op=ALU.mult),
                 reads=[("bank", 4 + hf), ("gs", z, hf)] + [("S2", 4 * hf + i) for i in range(4)],
                 writes=[("og1", hf)] + [("S2", 4 * hf + i) for i in range(4)])
        for h in range(8):
            hs = slice(h * 128, (h + 1) * 128)
            P.op("dve", lambda e, h=h, hs=hs: e.scalar_tensor_tensor(
                out=c.og2[:, hs], in0=c.og1[:, hs], scalar=c.rstdo[:, h:h + 1], in1=c.gout[:, hs], op0=ALU.mult, op1=ALU.mult),
                reads=[("og1", h // 4), ("S2", h), "rstdo", "lbc"], writes=[("og2", h // 4)])

    def post_b():
        def tr_o(e):
            last = None
            for h in range(8):
                last = e.transpose(out=c.psTb[:, h * 128:(h + 1) * 128], in_=c.og2[:, h * 128:(h + 1) * 128], identity=c.ident[:, :])
            return last

        P.group(lambda: (
            P.op("pe", tr_o, reads=[("og2", 0), ("og2", 1), "consts"], writes=[("bank", 7)]),
            P.op("act", lambda e: e.copy(out=c.uT[:, :, ti * 128:(ti + 1) * 128], in_=c.psTb[:, :].rearrange("p (k n) -> p k n", k=8)),
                 reads=[("bank", 7)], writes=[("uT", ti // 4)])))

    if not full:
        return [lambda: kv_group(0), lambda: None, lambda: kv_group(1)]

    def B1():
        kv_group(0)

    def B2():
        sc_group(0)
        kv_group(1)

    def B3():
        o_group(0)
        sc_group(1)

    def B4():
        o_group(1)
        post()

    return [B1, lambda: None, B2, B3, B4, post_b]


def hgrn_run_tiles(P, c, tiles):
    def load_x(k):
        src, i, _ = tiles[k]
        s = k % 2
        P.dma("sp", lambda e, src=src, i=i, s=s: e.dma_start(out=c.xs[:, s, :], in_=src[:, i, :]), ("xs", s), writes=[("xs", s)])

    n = len(tiles)
    load_x(0)
    Fst = hgrn_F_stages(P, c, c.xs[:, 0, :], ("xs", 0), tiles[0][2], 0)
    for f in Fst:
        f()
    nfull = 0
    for k in range(n):
        if k + 1 < n:
            load_x(k + 1)
            Fn = hgrn_F_stages(P, c, c.xs[:, (k + 1) % 2, :], ("xs", (k + 1) % 2), tiles[k + 1][2], (k + 1) % 2)
        else:
            Fn = []
        Bk = hgrn_B_stages(P, c, tiles[k][2], k % 2, c.hpar, nfull)
        c.hpar = 1 - c.hpar
        nfull += 1 if tiles[k][2] else 0
        la = P.record(lambda: [f() for f in Fn]) if Fn else []
        lb = P.record(lambda: [b() for b in Bk])
        if la:
            P.replay_merged(la, lb)
        else:
            for kind, args in lb:
                getattr(P, kind)(*args)


NHALO = 4 * (1 + 4 + 16)
OROW = 132


def build_G(dbg=None):
    nc = bass.Bass("TRN2", target_bir_lowering=False)
    dr = lambda name, shape, kind="ExternalInput": nc.dram_tensor(name, list(shape), F32, kind=kind).ap()
    x_d = dr("x", [T, D])
    xp_d = dr("x_prev", [NPREV * 128, D])
    gains_d = dr("gains", [128, 5, D])
    cst_d = dr("cst", [128, 5, 128])
    lbl_d = dr("lb_logits", [128, 3, D])
    gout_d = dr("gout", [128, D])
    hw_in_d = dr("hgrn_win", [128, 8, 4096])
    hw_out_d = dr("hgrn_wout", [128, 8, D])
    win_d = dr("ffn_win", [2, NJ, 128, 8, 256])
    wd_d = dr("ffn_wd", [2, DFF, D])
    wqkv_d = dr("wqkv", [9, 128, 8, 512])
    rope_d = dr("rope", [2, 128, 2, NT, 256])
    wo_d = dr("attn_wout", [128, 12, D])
    cstb_d = dr("cstb", [128, 5, 128])
    out_d = dr("out", [T, D], kind="ExternalOutput")
    qkv_scr = nc.dram_tensor("qkv_scr", [2 * T * 12, 384], BF16, kind="Internal").ap()
    o_loc = nc.dram_tensor("o_loc", [T, 12 * OROW], F32, kind="Internal").ap()
    s_scr = nc.dram_tensor("s_scr", [128, 1024], F32, kind="Internal").ap()

    with ExitStack() as es:
        sb = lambda name, shape, dt=F32: es.enter_context(nc.sbuf_tensor("sb_" + name, list(shape), dt))
        ps = lambda name, shape, dt=F32: es.enter_context(nc.psum_tensor("ps_" + name, list(shape), dt))
        c = Ctx()
        arA = sb("arA", [128, 16384])
        arC = sb("arC", [128, 16960])
        c.h = arA[:, :].rearrange("p (t d) -> p t d", t=NT)
        c.W = arA[:, :].bitcast(BF16).rearrange("p (k n) -> p k n", k=8)
        c.uT = sb("uT", [128, 8, T], BF16)
        arD = sb("arD", [128, 8192])
        c.oml = arD[:, 0:1024]
        c.gout = arD[:, 1024:2048]
        c.xs = arD[:, 2048:4096].rearrange("p (s n) -> p s n", s=2)
        c.wout = arD[:, 4096:8192].bitcast(BF16).rearrange("p (k n) -> p k n", k=8)
        c.wo = arD[:, 0:6144].bitcast(BF16).rearrange("p (k n) -> p k n", k=12)
        c.gcur = sb("gcur", [128, D])
        c.un = sb("un", [128, 2, D], BF16)
        c.cstf = sb("cstf", [128, 5, 128])
        c.ident = sb("identb", [128, 128], BF16)
        c.small = sb("small", [128, 64])
        c.ssq = c.small[:, 0:16]
        c.rstd = c.small[:, 16:32]
        c.ssq1 = c.small[:, 32:33]
        c.rstd1 = c.small[:, 33:34]
        c.ssqo = c.small[:, 40:48]
        c.rstdo = c.small[:, 48:56]
        c.prefT = c.cstf[:, 1, :]
        c.sufT = c.cstf[:, 2, :]
        c.ind2 = c.cstf[:, 4, 0:2]
        c.one1 = c.small[:, 56:57]

        def cv(lo, n, dt=F32):
            a = arC[:, lo:lo + n]
            return a if dt == F32 else a.bitcast(dt)

        c.aT = cv(0, 5120, BF16).rearrange("p (j n) -> p j n", j=5)
        c.win = cv(5120, 3072, BF16).rearrange("p (s k n) -> p s k n", s=3, k=8)
        c.wd = cv(8192, 5120, BF16).rearrange("p (s j n) -> p s j n", s=2, j=5)
        c.sg = cv(13312, 1024).rearrange("p (s n) -> p s n", s=2)
        c.junk = cv(13312, 512, BF16)
        c.tt = cv(0, 1024); c.logf = cv(1024, 1024); c.e1 = cv(2048, 1024); c.sq = cv(3072, 1024)
        c.qeA = cv(4096, 512, BF16); c.qeB = cv(4608, 512, BF16); c.ke = cv(5120, 512, BF16); c.og2 = cv(16448, 512, BF16)
        c.uTt = cv(5632, 512, BF16).rearrange("p (k n) -> p k n", k=8)
        c.v, c.kend, c.qeTA, c.qeTB, c.keT, c.gs, c.ebl = [], [], [], [], [], [], []
        for z in range(2):
            zb = 6144 + z * 3104
            c.v.append(cv(zb, 512, BF16)); c.kend.append(cv(zb + 512, 512, BF16))
            c.qeTA.append(cv(zb + 1024, 512, BF16).rearrange("p (h n) -> p h n", h=8))
            c.qeTB.append(cv(zb + 1536, 512, BF16).rearrange("p (h n) -> p h n", h=8))
            c.keT.append(cv(zb + 2048, 512, BF16).rearrange("p (h n) -> p h n", h=8))
            c.gs.append(cv(zb + 2560, 512, BF16)); c.ebl.append(cv(zb + 3072, 16))
        c.Sflat = cv(12352, 1024)
        c.S = c.Sflat.rearrange("p (h n) -> p h n", h=8)
        c.og1 = cv(13376, 1024)
        c.S2 = c.og1.rearrange("p (h n) -> p h n", h=8)
        c.Sb = cv(14400, 1536, BF16).rearrange("p (h a n) -> p h a n", h=8, a=3)
        c.scm = cv(15936, 256, BF16)
        c.nmask4 = cv(16192, 256, BF16)
        c.hpar = 0
        c.rope = cv(0, 8192).rearrange("p (a t n) -> p a t n", a=2, t=NT)
        c.xsb = cv(8192, 1024).rearrange("p (s n) -> p s n", s=2)
        c.ost = cv(9216, 512, BF16).rearrange("p (s n) -> p s n", s=2)
        c.rt = cv(10240, 1024).rearrange("p (s n) -> p s n", s=4)
        c.wq = c.wout[:, :, :].rearrange("p k n -> p (k n)").rearrange("p (s k n) -> p s k n", s=2, k=8)

        c.bank = [ps("b%d" % i, [128, 512]) for i in range(8)]
        c.psTb = c.bank[7][:, :].bitcast(BF16)
        c.psG = [c.bank[0], c.bank[1]]
        c.psU = [c.bank[2], c.bank[3]]
        c.psD = [c.bank[4], c.bank[5]]
        c.psT = [c.bank[6][:, :].bitcast(BF16), c.bank[7][:, :].bitcast(BF16)]
        c.win_ctr = c.wd_ctr = c.gu_ctr = c.d_ctr = 0
        c.pp_ctr = c.sc_ctr = c.kv_ctr = 0

        P = Prog(nc, es)
        P.dma("sp", lambda e: e.dma_start(out=c.cstf[:, :, :], in_=cst_d), "cst1", writes=["cstf"])
        P.dma("sp", lambda e: e.dma_start(out=c.gcur[:, :], in_=gains_d[:, 0, :]), "cst2", writes=["gcur"])
        P.dma("sp", lambda e: e.dma_start(out=c.gout[:, :], in_=gout_d), "cst3", writes=["gout_l"])
        P.op("dve", lambda e: e.tensor_copy(out=c.ident[:, :], in_=c.cstf[:, 0, :]), reads=["cstf"], writes=["ident_"])
        for j in range(4):
            P.op("dve", lambda e, j=j: e.tensor_scalar(out=c.nmask4[:, j * 128:(j + 1) * 128], in0=c.cstf[:, 3, :], scalar1=-1.0, scalar2=0.0,
                                                       op0=ALU.mult, op1=ALU.add), reads=["cstf", "ident_"], writes=["nm4"])
        P.op("dve", lambda e: e.memset(c.one1, 1.0), reads=["nm4"], writes=["consts"])
        for q in range(4):
            P.dma("pool", lambda e, q=q: e.dma_start(out=c.W[:, 2 * q:2 * q + 2, :], in_=hw_in_d[:, 2 * q:2 * q + 2, :]),
                  "W", writes=["W"])
        P.dma("pool", lambda e: e.dma_start(out=c.wout[:, :, :], in_=hw_out_d), "wout", writes=["wout"])
        lg = arC[:, 4096:7168].rearrange("p (a n) -> p a n", a=3)
        P.dma("sp", lambda e: e.dma_start(out=lg, in_=lbl_d), "cst4", writes=["lg"])
        P.op("dve", lambda e: e.tensor_tensor(out=c.e1[:, :], in0=lg[:, 0, :], in1=lg[:, 1, :], op=ALU.max), reads=["lg"], writes=["lmx"])
        P.op("dve", lambda e: e.tensor_tensor(out=c.e1[:, :], in0=c.e1[:, :], in1=lg[:, 2, :], op=ALU.max), reads=["lg", "lmx"], writes=["lmx"])
        for a in range(3):
            P.op("dve", lambda e, a=a: e.tensor_tensor(out=lg[:, a, :], in0=lg[:, a, :], in1=c.e1[:, :], op=ALU.subtract),
                 reads=["lg", "lmx"], writes=["lg"])
        P.op("act", lambda e: e.activation(out=lg, in_=lg, func=AF.Exp), reads=["lg"], writes=["lg"])
        P.op("dve", lambda e: e.tensor_tensor(out=c.e1[:, :], in0=lg[:, 0, :], in1=lg[:, 1, :], op=ALU.add), reads=["lg"], writes=["lmx"])
        P.op("dve", lambda e: e.tensor_tensor(out=c.e1[:, :], in0=c.e1[:, :], in1=lg[:, 2, :], op=ALU.add), reads=["lg", "lmx"], writes=["lmx"])
        P.op("dve", lambda e: e.reciprocal(out=c.e1[:, :], in_=c.e1[:, :]), reads=["lmx"], writes=["lmx"])
        P.op("dve", lambda e: e.tensor_tensor(out=c.sq[:, :], in0=lg[:, 0, :], in1=c.e1[:, :], op=ALU.mult), reads=["lg", "lmx"], writes=["lb_"])
        P.op("dve", lambda e: e.tensor_scalar(out=c.oml[:, :], in0=c.sq[:, :], scalar1=-1.0, scalar2=1.0, op0=ALU.mult, op1=ALU.add),
             reads=["lb_", "gout_l"], writes=["lbc"])
        P.op("dve", lambda e: e.memset(c.Sflat, 0.0), reads=["lbc"], writes=[("S", h) for h in range(8)])
        P.op("dve", lambda e: e.memset(c.Sb[:, :, :, :], 0.0), writes=[("Sb", h, a) for h in range(8) for a in range(3)])
        P.barrier()

        xpv = xp_d.rearrange("(t p) d -> p t d", p=128)
        xv = x_d.rearrange("(t p) d -> p t d", p=128)
        qv = qkv_scr.rearrange("(t p h) (x n) -> p t h x n", p=128, h=12, x=3)
        Sflat = c.Sflat

        def layer0_pass(pz, tiles, xr_view, xr_base, cbs):
            hgrn_run_tiles(P, c, tiles)
            P.barrier()
            P.dma("sp", lambda e: e.dma_start(out=s_scr, in_=Sflat), "ssave", reads=[("S", h) for h in range(8)])
            def load_x2(t):
                s = t % 2
                P.dma("sp", lambda e, t=t, s=s: e.dma_start(out=c.xs[:, s, :], in_=xr_view[:, xr_base + t, :]), ("xs", s), writes=[("xs", s)])

            load_x2(0)
            for t in range(NT):
                if t + 1 < NT:
                    load_x2(t + 1)
                for hf in range(2):
                    b = c.pp_ctr % 2
                    c.pp_ctr += 1

                    def mmo(e, t=t, hf=hf, b=b):
                        last = None
                        for kc in range(8):
                            last = e.matmul(c.bank[b][:, :], lhsT=c.uT[:, kc, t * 128:(t + 1) * 128], rhs=c.wout[:, kc, hf * 512:(hf + 1) * 512],
                                            start=(kc == 0), stop=(kc == 7))
                        return last

                    P.op("pe", mmo, reads=[("uT", t // 4), "wout"], writes=[("bank", b)])
                    P.op("dve", lambda e, t=t, hf=hf, b=b: e.tensor_tensor(
                        out=c.h[:, t, hf * 512:(hf + 1) * 512], in0=c.xs[:, t % 2, hf * 512:(hf + 1) * 512], in1=c.bank[b][:, :], op=ALU.add),
                        reads=[("xs", t % 2), ("bank", b)], writes=[("h", t)])
            P.barrier()
            P.dma("sp", lambda e: e.dma_start(out=c.gcur[:, :], in_=gains_d[:, 1, :]), "g1_%d" % pz, writes=["consts", "gcur"])
            emit_norm_T(P, c, lambda t: ("h", t), lambda t: c.h[:, t, :], c.gcur[:, :], "f0")
            emit_ffn(P, c, 0, win_d, wd_d)
            P.barrier()
            P.dma("sp", lambda e: e.dma_start(out=c.gcur[:, :], in_=gains_d[:, 2, :]), "g2_%d" % pz, writes=["consts", "gcur"])
            P.dma("sp", lambda e: e.dma_start(out=c.rope[:, :, :, :], in_=rope_d[pz]), "rp_%d" % pz, writes=["rope"])
            emit_norm_T(P, c, lambda t: ("h", t), lambda t: c.h[:, t, :], c.gcur[:, :], "m1")

            def load_wq(j):
                s = j % 2
                P.dma("pool", lambda e, j=j, s=s: e.dma_start(out=c.wq[:, s, :, :], in_=wqkv_d[cbs[j]]), ("wq", s), writes=[("wq", s)])

            load_wq(0)
            cnt = 0
            for j, cb in enumerate(cbs):
                if j + 1 < len(cbs):
                    load_wq(j + 1)
                for t in range(NT):
                    if pz == 0 and t < NT - (1, 4, 16)[cb % 3]:
                        continue
                    b = c.pp_ctr % 2
                    c.pp_ctr += 1
                    s2 = cnt % 2
                    cnt += 1

                    def mmq(e, j=j, t=t, b=b):
                        last = None
                        for kc in range(8):
                            last = e.matmul(c.bank[b][:, :], lhsT=c.uT[:, kc, t * 128:(t + 1) * 128], rhs=c.wq[:, j % 2, kc, :],
                                            start=(kc == 0), stop=(kc == 7))
                        return last

                    P.op("pe", mmq, reads=[("uT", t // 4), ("wq", j % 2)], writes=[("bank", b)])
                    if cb < 6:
                        scale = (128.0 ** -0.5) if cb < 3 else 1.0
                        P.op("act", lambda e, b=b, s2=s2, scale=scale: e.activation(out=c.xsb[:, s2, :], in_=c.bank[b][:, :], func=AF.Copy, scale=scale),
                             reads=[("bank", b)], writes=[("xsb", s2)])
                        xh = c.xsb[:, s2, :].rearrange("p (h a n) -> p h a n", h=4, a=2)
                        oh = c.ost[:, s2, :].rearrange("p (h a n) -> p h a n", h=4, a=2)
                        cosv = c.rope[:, 0, t, :].rearrange("p (h n) -> p h n", h=4)
                        sinv = c.rope[:, 1, t, :].rearrange("p (h n) -> p h n", h=4)
                        r = [c.rt[:, i, 0:256].rearrange("p (h n) -> p h n", h=4) for i in range(4)]
                        rk = [("xsb", s2), "rope"]
                        P.op("dve", lambda e, xh=xh, cosv=cosv, r=r: e.tensor_tensor(out=r[0], in0=xh[:, :, 0, :], in1=cosv, op=ALU.mult), reads=rk, writes=["rt0"])
                        P.op("pool", lambda e, xh=xh, sinv=sinv, r=r: e.tensor_tensor(out=r[1], in0=xh[:, :, 1, :], in1=sinv, op=ALU.mult), reads=rk, writes=["rt1"])
                        P.op("dve", lambda e, xh=xh, cosv=cosv, r=r: e.tensor_tensor(out=r[2], in0=xh[:, :, 1, :], in1=cosv, op=ALU.mult), reads=rk, writes=["rt2"])
                        P.op("pool", lambda e, xh=xh, sinv=sinv, r=r: e.tensor_tensor(out=r[3], in0=xh[:, :, 0, :], in1=sinv, op=ALU.mult), reads=rk, writes=["rt3"])
                        P.op("dve", lambda e, oh=oh, r=r: e.tensor_tensor(out=oh[:, :, 0, :], in0=r[0], in1=r[1], op=ALU.subtract),
                             reads=["rt0", "rt1"], writes=[("ost", s2)])
                        P.op("pool", lambda e, oh=oh, r=r: e.tensor_tensor(out=oh[:, :, 1, :], in0=r[2], in1=r[3], op=ALU.add),
                             reads=["rt2", "rt3", ("ost", s2)], writes=[("ost", s2)])
                    else:
                        P.op("act", lambda e, b=b, s2=s2: e.copy(out=c.ost[:, s2, :], in_=c.bank[b][:, :]), reads=[("bank", b)], writes=[("ost", s2)])
                    P.dma("sp", lambda e, cb=cb, t=t, s2=s2: e.dma_start(
                        out=qv[:, pz * NT + t, 4 * (cb % 3):4 * (cb % 3) + 4, cb // 3, :], in_=c.ost[:, s2, :].rearrange("p (h n) -> p h n", h=4)),
                          ("qo", s2), reads=[("ost", s2)])
            P.barrier()

        if dbg is not None:
            npre_, nfull_ = dbg
            tiles = [(xpv, NPREV - npre_ + i, False) for i in range(npre_)] + [(xv, i, True) for i in range(nfull_)]
            hgrn_run_tiles(P, c, tiles)
            P.barrier()
            for t in range(nfull_):
                P.dma("sp", lambda e, t=t: e.dma_start(out=c.xs[:, t % 2, :], in_=xv[:, t, :]), ("xs", t % 2), writes=[("xs", t % 2)])
                for hf in range(2):
                    b = c.pp_ctr % 2
                    c.pp_ctr += 1

                    def mmo(e, t=t, hf=hf, b=b):
                        last = None
                        for kc in range(8):
                            last = e.matmul(c.bank[b][:, :], lhsT=c.uT[:, kc, t * 128:(t + 1) * 128], rhs=c.wout[:, kc, hf * 512:(hf + 1) * 512],
                                            start=(kc == 0), stop=(kc == 7))
                        return last

                    P.op("pe", mmo, reads=[("uT", t // 4), "wout"], writes=[("bank", b)])
                    P.op("dve", lambda e, t=t, hf=hf, b=b: e.tensor_tensor(
                        out=c.h[:, t, hf * 512:(hf + 1) * 512], in0=c.xs[:, t % 2, hf * 512:(hf + 1) * 512], in1=c.bank[b][:, :], op=ALU.add),
                        reads=[("xs", t % 2), ("bank", b)], writes=[("h", t)])
            ovd = out_d.rearrange("(t p) d -> p t d", p=128)
            for t in range(nfull_):
                P.dma("sp", lambda e, t=t: e.dma_start(out=ovd[:, t, :], in_=c.h[:, t, :]), ("o", t), reads=[("h", t)])
            P.wait_dma_done("sp", [("o", t) for t in range(nfull_)])
            P.finalize()
            return nc
        layer0_pass(0, [(xpv, i, False) for i in range(0, NPREV - NT)] + [(xpv, i, True) for i in range(NPREV - NT, NPREV)],
                    xpv, NPREV - NT, list(range(3, 9)))
        P.dma("sp", lambda e: e.dma_start(out=c.gcur[:, :], in_=gains_d[:, 0, :]), "g0_1", writes=["consts", "gcur"])
        for q in range(4):
            P.dma("pool", lambda e, q=q: e.dma_start(out=c.W[:, 2 * q:2 * q + 2, :], in_=hw_in_d[:, 2 * q:2 * q + 2, :]),
                  "W", writes=["W"])
        P.dma("pool", lambda e: e.dma_start(out=c.wout[:, :, :], in_=hw_out_d), "wout", writes=["wout"])
        P.dma("sp", lambda e: e.dma_start(out=Sflat, in_=s_scr), "srest", writes=[("S", h) for h in range(8)])
        for h in range(8):
            P.op("act", lambda e, h=h, pr=c.hpar: e.copy(out=c.Sb[:, h, pr, :], in_=c.S[:, h, :]), reads=[("S", h)], writes=[("Sb", h, c.hpar)])
        P.barrier()
        layer0_pass(1, [(xv, i, True) for i in range(NT)], xv, 0, list(range(9)))
        emit_B_g(P, c, arC, qkv_scr, o_loc, cstb_d)
        P.barrier()
        emit_C_g(P, c, arC, o_loc, gains_d, wo_d, win_d, wd_d, out_d)
        P.finalize()
    return nc


def emit_B_g(P, c, arC, qkv_all, o_loc, cstb_d):
    def cv(lo, n, dt=F32):
        a = arC[:, lo:lo + n]
        return a if dt == F32 else a.bitcast(dt)

    cstb = cv(0, 640).rearrange("p (a n) -> p a n", a=5)
    mask4 = cv(640, 1024).rearrange("p (v n) -> p v n", v=2)
    raw = cv(1664, 1920, BF16).rearrange("p (s h n) -> p s h n", s=5, h=2)
    qT = cv(3584, 256, BF16).rearrange("p (s n) -> p s n", s=2)
    kT = cv(3840, 640, BF16).rearrange("p (s n) -> p s n", s=5)
    sm = cv(4480, 1024).rearrange("p (s n) -> p s n", s=2)
    pp = cv(5504, 512, BF16).rearrange("p (s n) -> p s n", s=2)
    pT = cv(6016, 512, BF16).rearrange("p (s n) -> p s n", s=2)
    ost = cv(6528, 528).rearrange("p (g h n) -> p g h n", g=2, h=2)
    stt = cv(7056, 32).rearrange("p (s n) -> p s n", s=2)
    P.dma("sp", lambda e: e.dma_start(out=cstb, in_=cstb_d), "bi2", writes=["cstb"])
    for v in range(2):
        for hh in range(2):
            P.op("dve", lambda e, v=v, hh=hh: e.tensor_copy(
                out=mask4[:, v, hh * 256:(hh + 1) * 256], in_=cstb[:, 1 + 2 * v:3 + 2 * v, :].rearrange("p a n -> p (a n)")),
                reads=["cstb"], writes=["mask4"])
    P.op("dve", lambda e: e.memset(ost, 0.0), writes=[("ost", 0), ("ost", 1)])

    def unit_stages(it, u, blk, hslot):
        dd = DIL[u // 4]
        nbr = NT // dd
        r, n = blk // nbr, blk % nbr
        first = n == 0
        s2 = it % 2
        s3 = it % 3
        sp3 = hslot if first else (it - 1) % 3
        qsrc = qkv_all.rearrange("(a i d h) c -> d a i h c", i=128, d=dd, h=12)
        qb, tb, ob = 6 + s2, 2 + s2, 4 + s2
        psQ = c.bank[qb][:, :].bitcast(BF16)
        psT = c.bank[tb][:, :].bitcast(BF16)
        st = stt[:, s2, :]
        sk = [("st", s2)]

        def S1():
            if first:
                P.dma("pool", lambda e: e.dma_start(out=raw[:, hslot, :, 128:384], in_=qsrc[r, NT // dd - 1, :, u:u + 2, 128:384]),
                      ("raw", hslot), reads=["qkv_all"], writes=[("raw", hslot)])
            P.dma("pool", lambda e: e.dma_start(out=raw[:, s3, :, :], in_=qsrc[r, NT // dd + n, :, u:u + 2, :]),
                  ("raw", s3), reads=["qkv_all"], writes=[("raw", s3)])

            def tr_qk(e):
                last = None
                for hh in range(2):
                    e.transpose(out=psQ[:, hh * 128:(hh + 1) * 128], in_=raw[:, s3, hh, 0:128], identity=c.ident[:, :])
                    last = e.transpose(out=psQ[:, 256 + hh * 128:256 + (hh + 1) * 128], in_=raw[:, s3, hh, 128:256], identity=c.ident[:, :])
                    if first:
                        last = e.transpose(out=psQ[:, 512 + hh * 128:512 + (hh + 1) * 128], in_=raw[:, hslot, hh, 128:256],
                                           identity=c.ident[:, :])
                return last

            P.op("pe", tr_qk, reads=[("raw", s3), "consts"] + ([("raw", hslot)] if first else []), writes=[("bank", qb)])

        def S1b():
            P.op("act", lambda e: e.copy(out=qT[:, s2, :], in_=psQ[:, 0:256]), reads=[("bank", qb)], writes=[("qT", s2)])
            P.op("act", lambda e: e.copy(out=kT[:, s3, :], in_=psQ[:, 256:512]), reads=[("bank", qb)], writes=[("kT", s3)])
            if first:
                P.op("act", lambda e: e.copy(out=kT[:, hslot, :], in_=psQ[:, 512:768]), reads=[("bank", qb)], writes=[("kT", hslot)])

            def mm_s(e):
                last = None
                for hh in range(2):
                    hs = slice(hh * 128, (hh + 1) * 128)
                    e.matmul(c.bank[s2][:, hh * 256 + 128:hh * 256 + 256], lhsT=qT[:, s2, hs], rhs=kT[:, s3, hs], start=True, stop=True)
                    last = e.matmul(c.bank[s2][:, hh * 256:hh * 256 + 128], lhsT=qT[:, s2, hs], rhs=kT[:, sp3, hs], start=True, stop=True)
                return last

            P.op("pe", mm_s, reads=[("qT", s2), ("kT", s3), ("kT", sp3)], writes=[("bank", s2)])

        def S2():
            P.op("dve", lambda e: e.tensor_tensor(out=sm[:, s2, :], in0=c.bank[s2][:, :], in1=mask4[:, 1 if first else 0, :], op=ALU.add),
                 reads=[("bank", s2), "mask4"], writes=[("sm", s2)])
            P.op("dve", lambda e: e.reduce_max(out=st[:, 0:2], in_=sm[:, s2, :].rearrange("p (h n) -> p h n", h=2), axis=mybir.AxisListType.X),
                 reads=[("sm", s2)], writes=sk)
            P.op("dve", lambda e: e.tensor_scalar(out=st[:, 2:4], in0=st[:, 0:2], scalar1=-1.0, scalar2=0.0, op0=ALU.mult, op1=ALU.add),
                 reads=sk, writes=sk)
            P.op("dve", lambda e: e.memset(st[:, 4:6], 0.0), reads=sk, writes=sk)
            for hh in range(2):
                cs = slice(hh * 256, (hh + 1) * 256)
                P.op("act", lambda e, hh=hh, cs=cs: e.activation(out=pp[:, s2, cs], in_=sm[:, s2, cs], func=AF.Exp, bias=st[:, 2 + hh:3 + hh],
                                                                 accum_out=st[:, 4 + hh:5 + hh]),
                     reads=[("sm", s2)] + sk, writes=[("p", s2)] + sk)

        def S3():
            def tr_p(e):
                last = None
                for j in range(4):
                    last = e.transpose(out=psT[:, j * 128:(j + 1) * 128], in_=pp[:, s2, j * 128:(j + 1) * 128], identity=c.ident[:, :])
                return last

            P.op("pe", tr_p, reads=[("p", s2), "consts"], writes=[("bank", tb)])

        def S3b():
            P.op("act", lambda e: e.copy(out=pT[:, s2, :], in_=psT[:, 0:512]), reads=[("bank", tb)], writes=[("pT", s2)])

            def mm_o(e):
                last = None
                for hh in range(2):
                    oc = slice(hh * 128, (hh + 1) * 128)
                    e.matmul(c.bank[ob][:, oc], lhsT=pT[:, s2, hh * 256 + 128:hh * 256 + 256], rhs=raw[:, s3, hh, 256:384], start=True, stop=False)
                    last = e.matmul(c.bank[ob][:, oc], lhsT=pT[:, s2, hh * 256:hh * 256 + 128], rhs=raw[:, sp3, hh, 256:384], start=False, stop=True)
                return last

            P.op("pe", mm_o, reads=[("pT", s2), ("raw", s3), ("raw", sp3)], writes=[("bank", ob)])

        def S4():
            P.op("dve", lambda e: e.reciprocal(out=st[:, 6:8], in_=st[:, 4:6]), reads=sk, writes=sk)
            for hh in range(2):
                P.op("dve", lambda e, hh=hh: e.tensor_scalar(out=ost[:, s2, hh, 0:128], in0=c.bank[ob][:, hh * 128:(hh + 1) * 128],
                                                             scalar1=st[:, 6 + hh:7 + hh], scalar2=0.0, op0=ALU.mult, op1=ALU.add),
                     reads=[("bank", ob)] + sk, writes=[("ost", s2)])
            P.op("act", lambda e: e.activation(out=st[:, 8:10], in_=st[:, 4:6], func=AF.Ln), reads=sk, writes=sk)
            P.op("dve", lambda e: e.tensor_tensor(out=ost[:, s2, :, 128], in0=st[:, 8:10], in1=st[:, 0:2], op=ALU.add),
                 reads=sk, writes=[("ost", s2)])
            P.dma("sp", lambda e: e.dma_start(out=o_loc.rearrange("(n i d) (h c) -> d n i h c", i=128, d=dd, h=12)[r, n, :, u:u + 2, :],
                                              in_=ost[:, s2, :, :]), ("oo", s2), reads=[("ost", s2)])

        return [S1, S1b, S2, S3, S3b, S4]

    units = []
    it = 0
    nh = 0
    for u in range(0, 12, 2):
        for blk in range(NT):
            nbr = NT // DIL[u // 4]
            hs = 3 + nh % 2
            if blk % nbr == 0:
                nh += 1
            units.append(unit_stages(it, u, blk, hs))
            it += 1
    nu = len(units)
    units[0][0]()
    units[0][1]()
    units[0][2]()
    for k in range(nu):
        nxt = units[k + 1] if k + 1 < nu else None
        if nxt:
            nxt[0]()
        units[k][3]()
        if nxt:
            nxt[1]()
        units[k][4]()
        if nxt:
            nxt[2]()
        units[k][5]()


def emit_C_g(P, c, arC, o_all, gains_d, wo_d, win_d, wd_d, out_d):
    I32 = mybir.dt.int32

    def cv(lo, n, dt=F32):
        a = arC[:, lo:lo + n]
        return a if dt == F32 else a.bitcast(dt)

    idx2 = cv(0, 192, I32)
    oin = cv(192, 3168).rearrange("p (s h n) -> p s h n", s=2, h=12)
    og_ = cv(3360, 768, BF16)
    oT4 = cv(4128, 3072, BF16).rearrange("p (k n) -> p k n", k=12)
    alb = cv(7200, 96).rearrange("p (s n) -> p s n", s=2)
    P.dma("pool", lambda e: e.dma_start(out=c.wo[:, :, :], in_=wo_d), "wo", writes=["wo"])
    for t in range(NT):
        s = t % 2
        al = alb[:, s, :]
        P.dma("sp", lambda e, s=s, t=t: e.dma_start(out=oin[:, s, :, :], in_=o_all[t * 128:(t + 1) * 128, :].rearrange("p (h c) -> p h c", h=12)),
              ("oin", s), reads=["o_all"], writes=[("oin", s)])
        ak = [("al", s)]
        P.op("dve", lambda e, s=s, al=al: e.tensor_copy(out=al[:, 0:12], in_=oin[:, s, :, 128]), reads=[("oin", s)], writes=ak)
        l3 = al[:, 0:12].rearrange("p (g h) -> p g h", g=3)
        e3 = al[:, 12:24].rearrange("p (g h) -> p g h", g=3)
        a3 = al[:, 36:48].rearrange("p (g h) -> p g h", g=3)
        mx = al[:, 24:28]
        sm_ = al[:, 28:32]
        P.op("dve", lambda e, l3=l3, mx=mx: e.tensor_tensor(out=mx, in0=l3[:, 0, :], in1=l3[:, 1, :], op=ALU.max), reads=ak, writes=ak)
        P.op("dve", lambda e, l3=l3, mx=mx: e.tensor_tensor(out=mx, in0=mx, in1=l3[:, 2, :], op=ALU.max), reads=ak, writes=ak)
        for g in range(3):
            P.op("dve", lambda e, l3=l3, e3=e3, mx=mx, g=g: e.tensor_tensor(out=e3[:, g, :], in0=l3[:, g, :], in1=mx, op=ALU.subtract),
                 reads=ak, writes=ak)
        P.op("act", lambda e, al=al: e.activation(out=al[:, 12:24], in_=al[:, 12:24], func=AF.Exp), reads=ak, writes=ak)
        P.op("dve", lambda e, e3=e3, sm_=sm_: e.tensor_tensor(out=sm_, in0=e3[:, 0, :], in1=e3[:, 1, :], op=ALU.add), reads=ak, writes=ak)
        P.op("dve", lambda e, e3=e3, sm_=sm_: e.tensor_tensor(out=sm_, in0=sm_, in1=e3[:, 2, :], op=ALU.add), reads=ak, writes=ak)
        P.op("dve", lambda e, sm_=sm_: e.reciprocal(out=sm_, in_=sm_), reads=ak, writes=ak)
        for g in range(3):
            P.op("dve", lambda e, e3=e3, a3=a3, sm_=sm_, g=g: e.tensor_tensor(out=a3[:, g, :], in0=e3[:, g, :], in1=sm_, op=ALU.mult),
                 reads=ak, writes=ak)
        for hh in range(12):
            eng = "dve" if hh % 2 == 0 else "pool"
            P.op(eng, lambda e, s=s, hh=hh, al=al: e.tensor_scalar(
                out=og_[:, hh * 128:(hh + 1) * 128], in0=oin[:, s, hh, 0:128], scalar1=al[:, 36 + hh:37 + hh],
                scalar2=0.0, op0=ALU.mult, op1=ALU.add), reads=[("oin", s)] + ak, writes=["og"])
        for part, (k0, k1) in enumerate(((0, 8), (8, 12))):
            def tr(e, k0=k0, k1=k1, part=part):
                last = None
                for kc in range(k0, k1):
                    last = e.transpose(out=c.psT[part][:, (kc - k0) * 128:(kc - k0 + 1) * 128], in_=og_[:, kc * 128:(kc + 1) * 128],
                                       identity=c.ident[:, :])
                return last

            P.op("pe", tr, reads=["og", "consts"], writes=[("psT", part)])
            P.op("act", lambda e, k0=k0, k1=k1, part=part, t=t: e.copy(
                out=oT4[:, k0:k1, (t % 4) * 128:(t % 4 + 1) * 128],
                in_=c.psT[part][:, 0:(k1 - k0) * 128].rearrange("p (k n) -> p k n", k=k1 - k0)),
                reads=[("psT", part)], writes=[("oT4", t % 4)])
        if t % 4 == 3:
            for tt in range(t - 3, t + 1):
                for hf in range(2):
                    pb = c.d_ctr % 2
                    c.d_ctr += 1

                    def mmo(e, tt=tt, hf=hf, pb=pb):
                        last = None
                        for kc in range(12):
                            last = e.matmul(c.psD[pb][:, :], lhsT=oT4[:, kc, (tt % 4) * 128:(tt % 4 + 1) * 128],
                                            rhs=c.wo[:, kc, hf * 512:(hf + 1) * 512], start=(kc == 0), stop=(kc == 11))
                        return last

                    P.op("pe", mmo, reads=[("oT4", tt % 4), "wo"], writes=[("psD", pb)])
                    P.op("dve", lambda e, tt=tt, hf=hf, pb=pb: e.tensor_tensor(
                        out=c.h[:, tt, hf * 512:(hf + 1) * 512], in0=c.h[:, tt, hf * 512:(hf + 1) * 512], in1=c.psD[pb][:, :], op=ALU.add),
                        reads=[("psD", pb), ("h", tt)], writes=[("h", tt)])
    P.barrier()
    P.dma("sp", lambda e: e.dma_start(out=c.gcur[:, :], in_=gains_d[:, 3, :]), "cst8", writes=["consts"])
    emit_norm_T(P, c, lambda t: ("h", t), lambda t: c.h[:, t, :], c.gcur[:, :], "f1")
    emit_ffn(P, c, 1, win_d, wd_d)
    P.barrier()
    P.dma("sp", lambda e: e.dma_start(out=c.gcur[:, :], in_=gains_d[:, 4, :]), "cst9", writes=["consts"])
    P.op("dve", lambda e: e.memset(c.ssq[:, :], 0.0), writes=["ssq"])
    for t in range(NT):
        P.op("act", lambda e, t=t: e.activation(out=c.junk[:, :], in_=c.h[:, t, :], func=AF.Square,
                                                accum_out=c.ssq[:, t:t + 1]), reads=[("h", t)], writes=["junk", "ssq"])
    P.op("dve", lambda e: e.tensor_scalar(out=c.rstd[:, :], in0=c.ssq[:, :], scalar1=1.0 / D, scalar2=EPS,
                                          op0=ALU.mult, op1=ALU.add), reads=["ssq"], writes=["rstd"])
    emit_rsqrt(P, c)
    ov = out_d.rearrange("(t p) d -> p t d", p=128)
    for t in range(NT):
        P.op("dve", lambda e, t=t: e.scalar_tensor_tensor(
            out=c.h[:, t, :], in0=c.h[:, t, :], scalar=c.rstd[:, t:t + 1], in1=c.gcur[:, :],
            op0=ALU.mult, op1=ALU.mult), reads=[("h", t), "rstd", "consts"], writes=[("h", t)])
        if t % 4 == 3:
            q = t // 4
            P.dma("sp", lambda e, q=q: e.dma_start(out=ov[:, q * 4:(q + 1) * 4, :], in_=c.h[:, q * 4:(q + 1) * 4, :]),
                  ("o", q), reads=[("h", tt) for tt in range(q * 4, q * 4 + 4)])
    P.wait_dma_done("sp", [("o", q) for q in range(4)])


def _f(a):
    return np.ascontiguousarray(np.asarray(a, dtype=np.float32))


def _consts():
    s = np.arange(128)
    same = (s[:, None] // 64) == (s[None, :] // 64)
    ident = np.eye(128, dtype=np.float32)
    prefT = (same & (s[:, None] <= s[None, :])).astype(np.float32)
    sufT = (same & (s[:, None] > s[None, :])).astype(np.float32)
    ind2 = np.zeros((128, 128), np.float32)
    ind2[:64, 0] = 1.0
    ind2[64:, 1] = 1.0
    return np.ascontiguousarray(np.stack([ident, prefT, sufT, prefT, ind2], axis=1))


def _rope_tables(seg):
    inv_freq = (1.0 / (10000.0 ** (np.arange(0, 128, 2, dtype=np.float32) / np.float32(128)))).astype(np.float32)
    pos = (seg * T + np.arange(T)).astype(np.float32)
    ang = (pos[:, None] * inv_freq[None, :]).astype(np.float32)
    cs = np.stack([np.cos(ang), np.sin(ang)], axis=0).astype(np.float32)
    cs = np.tile(cs.reshape(2, NT, 128, 1, 64), (1, 1, 1, 4, 1)).reshape(2, NT, 128, 256)
    return np.ascontiguousarray(cs.transpose(2, 0, 1, 3))


def prep_common(inputs):
    gains = np.stack([_f(inputs["norm_mix"])[0], _f(inputs["norm_ffn"])[0], _f(inputs["norm_mix"])[1],
                      _f(inputs["norm_ffn"])[1], _f(inputs["final_norm"])], axis=0)
    gains = np.ascontiguousarray(np.broadcast_to(gains[None], (128, 5, D)))
    w_in = _f(inputs["ffn_w_in"])
    g = w_in[:, :, :DFF].reshape(2, 8, 128, NJ, 128)
    u = w_in[:, :, DFF:].reshape(2, 8, 128, NJ, 128)
    win = np.ascontiguousarray(np.concatenate([g, u], axis=-1).transpose(0, 3, 2, 1, 4))
    pk = lambda w: np.ascontiguousarray(w.reshape(w.shape[0] // 128, 128, w.shape[1]).transpose(1, 0, 2))
    wqkv = _f(inputs["attn_w_qkv"])[0].reshape(8, 128, 9, 512).transpose(2, 1, 0, 3)
    return {
        "gains": gains,
        "cst": _consts(),
        "lb_logits": np.ascontiguousarray(np.broadcast_to(_f(inputs["hgrn_lb_logits"])[None], (128, 3, D))),
        "gout": np.ascontiguousarray(np.broadcast_to(np.tile(_f(inputs["hgrn_out_norm"])[0], 8)[None], (128, D))),
        "hgrn_win": pk(_f(inputs["hgrn_w_in"])[0]),
        "hgrn_wout": pk(_f(inputs["hgrn_w_out"])[0]),
        "ffn_win": win,
        "ffn_wd": _f(inputs["ffn_w_down"]),
        "wqkv": np.ascontiguousarray(wqkv),
        "attn_wout": pk(_f(inputs["attn_w_out"])[0]),
    }


_NC_CACHE = {}


def _get(name, builder):
    if name not in _NC_CACHE:
        _NC_CACHE[name] = builder()
    return _NC_CACHE[name]


F_KEYS = ("gains", "cst", "lb_logits", "gout", "hgrn_win", "hgrn_wout", "ffn_win", "ffn_wd", "wqkv", "attn_wout")


def _consts_b():
    i = np.arange(128)
    ident = np.eye(128, dtype=np.float32)
    mprev = np.where(i[None, :] >= i[:, None], 0.0, NEG).astype(np.float32)
    mcur = np.where(i[None, :] <= i[:, None], 0.0, NEG).astype(np.float32)
    return np.ascontiguousarray(np.stack([ident, mprev, mcur], axis=1))


def run_G(inputs, common):
    nc = _get("G", build_G)
    x = _f(inputs["x"])
    cb = _consts_b()
    in_maps = []
    for cidx in range(NCORES):
        b, s = cidx // 4, cidx % 4
        xp = np.zeros((NPREV * 128, D), np.float32)
        n = min(s * T, NPREV * 128)
        if n:
            xp[NPREV * 128 - n:] = x[b, s * T - n:s * T]
        mh = cb[:, 1] if s > 0 else np.full((128, 128), NEG, np.float32)
        cstb = np.ascontiguousarray(np.stack([cb[:, 0], cb[:, 1], cb[:, 2], mh, cb[:, 2]], axis=1))
        rope = np.ascontiguousarray(np.stack([_rope_tables(max(s - 1, 0)), _rope_tables(s)], axis=0))
        m = {k: common[k] for k in F_KEYS}
        m.update(x=np.ascontiguousarray(x[b, s * T:(s + 1) * T]), x_prev=xp, rope=rope, cstb=cstb)
        in_maps.append(m)
    res = run_bass_kernel_spmd(nc, in_maps, core_ids=list(range(NCORES)))
    return np.stack([np.asarray(r["out"]) for r in res.results], axis=0).reshape(2, 4 * T, D).astype(np.float32)


def kernel(**inputs):
    common = prep_common(inputs)
    return run_G(inputs, common)
```

```python
import numpy as np
from contextlib import ExitStack
import concourse.bass as bass
import concourse.mybir as mybir
from concourse.bass_utils import run_bass_kernel_spmd

F32 = mybir.dt.float32
BF16 = mybir.dt.bfloat16
AF = mybir.ActivationFunctionType
ALU = mybir.AluOpType

NCORES = 8
D = 1024
T = 2048
NT = T // 128
DFF = 2816
NJ = DFF // 128
EPS = 1e-6


import os as _os
MERGE_OFF = float(_os.environ.get("MOFF", "0.1"))
MERGE_SCALE = float(_os.environ.get("MSC", "0.9"))
SAME_ENGINE_INORDER = ("pe",)
STRICT_KEYS = ("ssq", "ssq1", "ssqo", "rstd", "rstd1", "rstdo", "st", "al", "ebl", "consts", "lmx", "lb_", "lbc")


class Prog:
    ENG = ("pe", "act", "dve", "pool", "sp")
    ATTR = {"pe": "tensor", "act": "scalar", "dve": "vector", "pool": "gpsimd", "sp": "sync"}

    def __init__(self, nc, es):
        self.nc = nc
        self.es = es
        self.streams = {e: [] for e in self.ENG}
        self.sem = {e: es.enter_context(nc.semaphore("s_" + e)) for e in ("pe", "act", "dve", "pool")}
        self.cnt = {e: 0 for e in ("pe", "act", "dve", "pool")}
        self.dsem = {}
        self.dcnt = {}
        self.known = {e: {} for e in self.ENG}
        self.lastw = {}
        self.readers = {}
        self._rec = None

    def _src_sem(self, src):
        return self.sem[src] if isinstance(src, str) else self.dsem[src[1]]

    def _deps(self, reads, writes):
        w = {}
        self._strict = {}

        def add(src, val, k):
            if val > w.get(src, 0):
                w[src] = val
            if (k if isinstance(k, str) else k[0]) in STRICT_KEYS and val > self._strict.get(src, 0):
                self._strict[src] = val

        for k in reads:
            lw = self.lastw.get(k)
            if lw:
                add(lw[0], lw[1], k)
        for k in writes:
            lw = self.lastw.get(k)
            if lw:
                add(lw[0], lw[1], k)
            for s, v in self.readers.get(k, {}).items():
                add(s, v, k)
        return w

    def _emit_waits(self, E, deps):
        for src, val in deps.items():
            if src == E and E in SAME_ENGINE_INORDER:
                if E == "pe" or src not in self._strict:
                    continue
                val = self._strict[src]
            if self.known[E].get(src, 0) >= val:
                continue
            self.known[E][src] = val
            sem = self._src_sem(src)
            self.streams[E].append(lambda eng, sem=sem, val=val: eng.wait_ge(sem, val))

    def record(self, emit):
        self._rec = []
        emit()
        r, self._rec = self._rec, None
        return r

    def group(self, emit):
        if self._rec is None:
            emit()
            return
        outer, self._rec = self._rec, []
        emit()
        inner, self._rec = self._rec, outer
        self._rec.append(("_replay", (inner,)))

    def _replay(self, items):
        for kind, args in items:
            getattr(self, kind)(*args)

    def replay_merged(self, la, lb):
        items = [((i + 0.5) / len(la), 0, i, it) for i, it in enumerate(la)] + [(MERGE_OFF + MERGE_SCALE * (j + 0.5) / len(lb), 1, j, it) for j, it in enumerate(lb)]
        import os
        if os.environ.get('MERGE') == 'seq':
            items.sort(key=lambda t: (t[1], t[2]))
        else:
            items.sort(key=lambda t: (t[0], t[1], t[2]))
        for _, _, _, (kind, args) in items:
            getattr(self, kind)(*args)

    def op(self, E, fn, reads=(), writes=()):
        if self._rec is not None:
            self._rec.append(("op", (E, fn, reads, writes)))
            return
        deps = self._deps(reads, writes)
        self._emit_waits(E, deps)
        self.cnt[E] += 1
        v = self.cnt[E]
        sem = self.sem[E]
        self.streams[E].append(lambda eng, fn=fn, sem=sem: fn(eng).then_inc(sem, 1))
        for k in reads:
            self.readers.setdefault(k, {})[E] = v
        for k in writes:
            self.lastw[k] = (E, v)
            self.readers[k] = {}

    def dma(self, Q, fn, semkey, reads=(), writes=()):
        if self._rec is not None:
            self._rec.append(("dma", (Q, fn, semkey, reads, writes)))
            return
        if semkey not in self.dsem:
            self.dsem[semkey] = self.es.enter_context(self.nc.semaphore("d_" + str(semkey)))
            self.dcnt[semkey] = 0
        deps = self._deps(reads, writes)
        self._emit_waits(Q, deps)
        self.dcnt[semkey] += 16
        v = self.dcnt[semkey]
        sem = self.dsem[semkey]
        self.streams[Q].append(lambda eng, fn=fn, sem=sem: fn(eng).then_inc(sem, 16))
        src = ("d", semkey)
        for k in reads:
            self.readers.setdefault(k, {})[src] = v
        for k in writes:
            self.lastw[k] = (src, v)
            self.readers[k] = {}

    def cc(self, fn, semkey, reads=(), writes=()):
        if semkey not in self.dsem:
            self.dsem[semkey] = self.es.enter_context(self.nc.semaphore("d_" + str(semkey)))
            self.dcnt[semkey] = 0
        deps = self._deps(reads, writes)
        self._emit_waits("pool", deps)
        self.dcnt[semkey] += 1
        v = self.dcnt[semkey]
        sem = self.dsem[semkey]
        self.streams["pool"].append(lambda eng, fn=fn, sem=sem: fn(eng).then_inc(sem, 1))
        src = ("d", semkey)
        for k in reads:
            self.readers.setdefault(k, {})[src] = v
        for k in writes:
            self.lastw[k] = (src, v)
            self.readers[k] = {}

    def wait_dma_done(self, E, semkeys):
        for sk in semkeys:
            sem, val = self.dsem[sk], self.dcnt[sk]
            self.streams[E].append(lambda eng, sem=sem, val=val: eng.wait_ge(sem, val))

    def barrier(self):
        for E in self.ENG:
            deps = {}
            for s in ("pe", "act", "dve", "pool"):
                if self.cnt[s] > 0:
                    deps[s] = self.cnt[s]
            for sk, v in self.dcnt.items():
                if v > 0:
                    deps[("d", sk)] = v
            for src, val in deps.items():
                if src == E:
                    continue
                if self.known[E].get(src, 0) >= val:
                    continue
                self.known[E][src] = val
                sem = self._src_sem(src)
                self.streams[E].append(lambda eng, sem=sem, val=val: eng.wait_ge(sem, val))

    def finalize(self):
        nc = self.nc
        with nc.Block() as block:
            for name in self.ENG:
                stream = self.streams[name]

                def body(eng, stream=stream):
                    for f in stream:
                        f(eng)

                getattr(block, self.ATTR[name])(body)


class Ctx:
    pass


def emit_rsqrt(P, c):
    P.op("act", lambda e: e.activation(out=c.rstd[:, :], in_=c.rstd[:, :], func=AF.Sqrt), reads=["rstd"], writes=["rstd"])
    P.op("dve", lambda e: e.reciprocal(out=c.rstd[:, :], in_=c.rstd[:, :]), reads=["rstd"], writes=["rstd"])


def emit_norm_T(P, c, src_key_fn, src_ap_fn, gain_ap, tag):
    nc = P.nc
    P.op("dve", lambda e: e.memset(c.ssq[:, :], 0.0), writes=["ssq"])
    for t in range(NT):
        P.op("act", lambda e, t=t: e.activation(out=c.junk[:, :], in_=src_ap_fn(t), func=AF.Square,
                                                accum_out=c.ssq[:, t:t + 1]),
             reads=[src_key_fn(t)], writes=["junk", "ssq"])
    P.op("dve", lambda e: e.tensor_scalar(out=c.rstd[:, :], in0=c.ssq[:, :], scalar1=1.0 / D, scalar2=EPS,
                                          op0=ALU.mult, op1=ALU.add), reads=["ssq"], writes=["rstd"])
    emit_rsqrt(P, c)
    for t in range(NT):
        ub = t % 2
        P.op("dve", lambda e, t=t, ub=ub: e.scalar_tensor_tensor(
            out=c.un[:, ub, :], in0=src_ap_fn(t), scalar=c.rstd[:, t:t + 1], in1=gain_ap,
            op0=ALU.mult, op1=ALU.mult), reads=[src_key_fn(t), "rstd", "consts"], writes=[("un", ub)])

        def tr(e, t=t, ub=ub):
            last = None
            for kc in range(8):
                last = e.transpose(out=c.psT[ub][:, kc * 128:(kc + 1) * 128], in_=c.un[:, ub, kc * 128:(kc + 1) * 128],
                                   identity=c.ident[:, :])
            return last

        P.op("pe", tr, reads=[("un", ub), "consts"], writes=[("psT", ub)])
        P.op("act", lambda e, t=t, ub=ub: e.copy(
            out=c.uT[:, :, t * 128:(t + 1) * 128], in_=c.psT[ub][:, :].rearrange("p (k n) -> p k n", k=8)),
            reads=[("psT", ub)], writes=[("uT", t // 4)])


def emit_ffn(P, c, layer, win_d, wd_d):
    groups = [list(range(0, 5)), list(range(5, 10)), list(range(10, 15)), list(range(15, 20)), list(range(20, 22))]
    def load_win(j):
        s = c.win_ctr % 3
        c.win_ctr += 1
        P.dma("pool", lambda e, j=j, s=s: e.dma_start(out=c.win[:, s, :, :], in_=win_d[layer, j]),
              ("win", s), writes=[("win", s)])
        return s

    def load_wd(g):
        s = c.wd_ctr % 2
        c.wd_ctr += 1
        js = groups[g]
        P.dma("pool", lambda e, js=js, s=s: e.dma_start(
            out=c.wd[:, s, 0:len(js), :], in_=wd_d[layer, js[0] * 128:(js[-1] + 1) * 128, :].rearrange("(j p) n -> p j n", p=128)),
            ("wd", s), writes=[("wd", s)])
        return s

    flat = [j for g in groups for j in g]
    win_slots = {}
    for j in flat[:2]:
        win_slots[j] = load_win(j)
    wd_slot = {0: load_wd(0)}
    idx = 0
    for g, js in enumerate(groups):
        if g + 1 < len(groups):
            wd_slot[g + 1] = load_wd(g + 1)
        for jl, j in enumerate(js):
            if idx + 2 < len(flat):
                win_slots[flat[idx + 2]] = load_win(flat[idx + 2])
            s = win_slots[j]
            for tb in range(4):
                pb = c.gu_ctr % 2
                c.gu_ctr += 1

                def mm(e, s=s, tb=tb, pb=pb):
                    last = None
                    for half, ps in ((0, c.psG[pb]), (1, c.psU[pb])):
                        for kc in range(8):
                            last = e.matmul(ps[:, :], lhsT=c.win[:, s, kc, half * 128:(half + 1) * 128],
                                            rhs=c.uT[:, kc, tb * 512:(tb + 1) * 512], start=(kc == 0), stop=(kc == 7))
                    return last

                P.op("pe", mm, reads=[("win", s), ("uT", tb)], writes=[("psG", pb), ("psU", pb)])
                P.op("act", lambda e, pb=pb: e.activation(out=c.sg[:, pb, :], in_=c.psG[pb][:, :], func=AF.Silu),
                     reads=[("psG", pb)], writes=[("sg", pb)])
                P.op("dve", lambda e, pb=pb, jl=jl, tb=tb: e.tensor_tensor(
                    out=c.aT[:, jl, tb * 512:(tb + 1) * 512], in0=c.sg[:, pb, :], in1=c.psU[pb][:, :], op=ALU.mult),
                    reads=[("sg", pb), ("psU", pb)], writes=[("aT", jl, tb)])
            idx += 1
        ws = wd_slot[g]
        for t in range(NT):
            for hf in range(2):
                pb = c.d_ctr % 2
                c.d_ctr += 1

                def mmd(e, t=t, hf=hf, pb=pb, ws=ws, n=len(js)):
                    last = None
                    for jl in range(n):
                        last = e.matmul(c.psD[pb][:, :], lhsT=c.aT[:, jl, t * 128:(t + 1) * 128],
                                        rhs=c.wd[:, ws, jl, hf * 512:(hf + 1) * 512], start=(jl == 0), stop=(jl == n - 1))
                    return last

                P.op("pe", mmd, reads=[("wd", ws)] + [("aT", jl, t // 4) for jl in range(len(js))], writes=[("psD", pb)])
                P.op("dve", lambda e, t=t, hf=hf, pb=pb: e.tensor_tensor(
                    out=c.h[:, t, hf * 512:(hf + 1) * 512], in0=c.h[:, t, hf * 512:(hf + 1) * 512], in1=c.psD[pb][:, :],
                    op=ALU.add), reads=[("psD", pb), ("h", t)], writes=[("h", t)])


NPREV = 48
HQ, HF, HI, HG = 0, 1024, 2048, 3072
NEG = -30000.0
DIL = (1, 4, 16)


def hgrn_F_stages(P, c, xs_ap, xs_key, full, z):
    sl2 = [slice(0, 512), slice(512, 1024)]

    def nb():
        b = c.pp_ctr % 2
        c.pp_ctr += 1
        return b

    def proj(col0):
        b = nb()

        def mm(e):
            last = None
            for kc in range(8):
                last = e.matmul(c.bank[b][:, :], lhsT=c.uTt[:, kc, :], rhs=c.W[:, kc, col0:col0 + 512], start=(kc == 0), stop=(kc == 7))
            return last

        P.op("pe", mm, reads=["uTt", "W"], writes=[("bank", b)])
        return b

    def F1():
        P.op("dve", lambda e: e.memset(c.ssq1[:, :], 0.0), writes=["ssq1"])
        P.op("act", lambda e: e.activation(out=c.un[:, 1, :], in_=xs_ap, func=AF.Square, accum_out=c.ssq1[:, 0:1]),
             reads=[xs_key], writes=[("un", 1), "ssq1"])
        P.op("dve", lambda e: e.tensor_scalar(out=c.rstd1[:, :], in0=c.ssq1[:, :], scalar1=1.0 / D, scalar2=EPS,
                                              op0=ALU.mult, op1=ALU.add), reads=["ssq1"], writes=["rstd1"])
        P.op("act", lambda e: e.activation(out=c.rstd1[:, :], in_=c.rstd1[:, :], func=AF.Ln), reads=["rstd1"], writes=["rstd1"])
        P.op("act", lambda e: e.activation(out=c.rstd1[:, :], in_=c.rstd1[:, :], func=AF.Exp, scale=-0.5), reads=["rstd1"], writes=["rstd1"])
        P.op("pool", lambda e: e.tensor_tensor(out=c.un[:, 0, :], in0=xs_ap, in1=c.gcur[:, :], op=ALU.mult),
             reads=[xs_key, "gcur"], writes=[("un", 0)])

    def F1b():
        def tr_un(e):
            last = None
            for kc in range(8):
                last = e.transpose(out=c.psTb[:, kc * 128:(kc + 1) * 128], in_=c.un[:, 0, kc * 128:(kc + 1) * 128], identity=c.ident[:, :])
            return last

        P.group(lambda: (
            P.op("pe", tr_un, reads=[("un", 0), "consts"], writes=[("bank", 7)]),
            P.op("act", lambda e: e.copy(out=c.uTt[:, :, :], in_=c.psTb[:, :].rearrange("p (k n) -> p k n", k=8)),
                 reads=[("bank", 7)], writes=["uTt"])))

    def F2():
        for hf in range(2):
            sl = sl2[hf]
            b = proj(HF + hf * 512)
            P.op("act", lambda e, b=b, sl=sl: e.activation(out=c.tt[:, sl], in_=c.bank[b][:, :], func=AF.Sigmoid, scale=c.rstd1[:, 0:1]),
                 reads=[("bank", b), "rstd1"], writes=[("tt", hf)])
            P.op("dve", lambda e, sl=sl: e.scalar_tensor_tensor(out=c.tt[:, sl], in0=c.tt[:, sl], scalar=-1.0, in1=c.oml[:, sl],
                                                                op0=ALU.add, op1=ALU.mult),
                 reads=[("tt", hf), "lbc"], writes=[("tt", hf)])

    def F3():
        for hf in range(2):
            sl = sl2[hf]
            b = proj(HI + hf * 512)
            P.op("dve", lambda e, b=b, sl=sl: e.tensor_scalar(out=c.v[z][:, sl], in0=c.bank[b][:, :], scalar1=c.rstd1[:, 0:1], scalar2=0.0,
                                                              op0=ALU.mult, op1=ALU.add), reads=[("bank", b), "rstd1"], writes=[("v", z, hf)])
        if full:
            for hf in range(2):
                sl = sl2[hf]
                b = proj(HQ + hf * 512)
                P.op("act", lambda e, b=b, sl=sl: e.activation(out=c.sq[:, sl], in_=c.bank[b][:, :], func=AF.Silu, scale=c.rstd1[:, 0:1]),
                     reads=[("bank", b), "rstd1"], writes=[("sq", hf)])
            for hf in range(2):
                sl = sl2[hf]
                b = proj(HG + hf * 512)
                P.op("act", lambda e, b=b, sl=sl: e.activation(out=c.gs[z][:, sl], in_=c.bank[b][:, :], func=AF.Silu, scale=c.rstd1[:, 0:1]),
                     reads=[("bank", b), "rstd1"], writes=[("gs", z, hf)])
        for hf in range(2):
            sl = sl2[hf]
            P.op("act", lambda e, sl=sl: e.activation(out=c.logf[:, sl], in_=c.tt[:, sl], func=AF.Ln, bias=c.one1[:, 0:1]),
                 reads=[("tt", hf), "consts"], writes=[("logf", hf)])

    def F4():
        if full:
            for hf in range(2):
                sl = sl2[hf]
                b = nb()
                P.op("pe", lambda e, b=b, sl=sl: e.matmul(c.bank[b][:, :], lhsT=c.prefT[:, :], rhs=c.logf[:, sl], start=True, stop=True),
                     reads=[("logf", hf), "consts"], writes=[("bank", b)])
                P.op("act", lambda e, b=b, sl=sl: e.activation(out=c.e1[:, sl], in_=c.bank[b][:, :], func=AF.Exp),
                     reads=[("bank", b)], writes=[("e1", hf)])
                P.op("dve", lambda e, sl=sl: e.scalar_tensor_tensor(out=c.qeA[:, sl], in0=c.sq[:, sl], scalar=c.ind2[:, 0:1], in1=c.e1[:, sl],
                                                                    op0=ALU.mult, op1=ALU.mult),
                     reads=[("sq", hf), ("e1", hf), "consts"], writes=[("qeA", hf)])
                P.op("dve", lambda e, sl=sl: e.scalar_tensor_tensor(out=c.qeB[:, sl], in0=c.sq[:, sl], scalar=c.ind2[:, 1:2], in1=c.e1[:, sl],
                                                                    op0=ALU.mult, op1=ALU.mult),
                     reads=[("sq", hf), ("e1", hf), "consts"], writes=[("qeB", hf)])
                P.op("act", lambda e, b=b, sl=sl: e.activation(out=c.e1[:, sl], in_=c.bank[b][:, :], func=AF.Exp, scale=-1.0),
                     reads=[("bank", b)], writes=[("e1", hf)])
                P.op("pool", lambda e, sl=sl: e.tensor_tensor(out=c.ke[:, sl], in0=c.tt[:, sl], in1=c.e1[:, sl], op=ALU.mult),
                     reads=[("tt", hf), ("e1", hf)], writes=[("ke", hf)])
        for hf in range(2):
            sl = sl2[hf]
            b = nb()
            P.op("pe", lambda e, b=b, sl=sl: e.matmul(c.bank[b][:, :], lhsT=c.sufT[:, :], rhs=c.logf[:, sl], start=True, stop=True),
                 reads=[("logf", hf), "consts"], writes=[("bank", b)])
            P.op("act", lambda e, b=b, sl=sl: e.activation(out=c.e1[:, sl], in_=c.bank[b][:, :], func=AF.Exp),
                 reads=[("bank", b)], writes=[("e1", hf)])
            P.op("pool", lambda e, sl=sl: e.tensor_tensor(out=c.kend[z][:, sl], in0=c.tt[:, sl], in1=c.e1[:, sl], op=ALU.mult),
                 reads=[("tt", hf), ("e1", hf)], writes=[("kend", z, hf)])

    def F5():
        bb = nb()

        def mm_bl(e):
            last = None
            for h in range(8):
                last = e.matmul(c.bank[bb][:, 2 * h:2 + 2 * h], lhsT=c.logf[:, h * 128:(h + 1) * 128], rhs=c.ind2[:, :],
                                start=True, stop=True)
            return last

        P.op("pe", mm_bl, reads=[("logf", 0), ("logf", 1), "consts"], writes=[("bank", bb)])
        P.op("act", lambda e: e.activation(out=c.ebl[z][:, :], in_=c.bank[bb][:, 0:16], func=AF.Exp), reads=[("bank", bb)], writes=[("ebl", z)])
        if full:
            for nm, src, dst in (("qeA", c.qeA, c.qeTA[z]), ("qeB", c.qeB, c.qeTB[z]), ("ke", c.ke, c.keT[z])):
                def tr_q(e, src=src):
                    last = None
                    for h in range(8):
                        last = e.transpose(out=c.psTb[:, h * 128:(h + 1) * 128], in_=src[:, h * 128:(h + 1) * 128], identity=c.ident[:, :])
                    return last

                P.group(lambda tr_q=tr_q, nm=nm, dst=dst: (
                    P.op("pe", tr_q, reads=[(nm, 0), (nm, 1), "consts"], writes=[("bank", 7)]),
                    P.op("act", lambda e, dst=dst: e.copy(out=dst[:, :, :], in_=c.psTb[:, :].rearrange("p (h n) -> p h n", h=8)),
                         reads=[("bank", 7)], writes=[(nm + "T", z)])))

    return [F1, F1b, F2, F3, F4, F5]


def hgrn_B_stages(P, c, full, z, par, ti):
    sl2 = [slice(0, 512), slice(512, 1024)]

    def kv_group(g4):
        def mmA(e):
            last = None
            for hl in range(4):
                hs = slice((g4 * 4 + hl) * 128, (g4 * 4 + hl + 1) * 128)
                last = e.matmul(c.bank[2][:, hl * 128:(hl + 1) * 128], lhsT=c.kend[z][0:64, hs], rhs=c.v[z][0:64, hs], start=True, stop=True)
            return last

        def mmB(e):
            last = None
            for hl in range(4):
                hs = slice((g4 * 4 + hl) * 128, (g4 * 4 + hl + 1) * 128)
                last = e.matmul(c.bank[6][:, hl * 128:(hl + 1) * 128], lhsT=c.kend[z][64:128, hs], rhs=c.v[z][64:128, hs], start=True, stop=True)
            return last

        rk = [("kend", z, g4), ("v", z, g4)]
        P.op("pe", mmA, reads=rk, writes=[("bank", 2)])
        P.op("pe", mmB, reads=rk, writes=[("bank", 6)])
        for hl in range(4):
            h = g4 * 4 + hl
            cs = slice(hl * 128, (hl + 1) * 128)
            P.op("dve", lambda e, h=h, cs=cs: e.scalar_tensor_tensor(
                out=c.S2[:, h, :], in0=c.S[:, h, :], scalar=c.ebl[z][:, 2 * h:2 * h + 1], in1=c.bank[2][:, cs],
                op0=ALU.mult, op1=ALU.subtract), reads=[("S", h), ("ebl", z), ("bank", 2)], writes=[("S2", h)])
            if full:
                P.op("pool", lambda e, h=h: e.tensor_copy(out=c.Sb[:, h, 2, :], in_=c.S2[:, h, :]), reads=[("S2", h)], writes=[("Sb", h, 2)])
            P.op("dve", lambda e, h=h, cs=cs: e.scalar_tensor_tensor(
                out=c.S[:, h, :], in0=c.S2[:, h, :], scalar=c.ebl[z][:, 2 * h + 1:2 * h + 2], in1=c.bank[6][:, cs],
                op0=ALU.mult, op1=ALU.subtract), reads=[("S2", h), ("ebl", z), ("bank", 6)], writes=[("S", h)])
            P.op("pool", lambda e, h=h: e.tensor_copy(out=c.Sb[:, h, 1 - par, :], in_=c.S[:, h, :]), reads=[("S", h)], writes=[("Sb", h, 1 - par)])

    def sc_group(g4):
        def mm_sc(e):
            last = None
            for hl in range(4):
                h = g4 * 4 + hl
                e.matmul(c.bank[3][:, hl * 128:(hl + 1) * 128], lhsT=c.keT[z][:, h, :], rhs=c.qeTA[z][:, h, :], start=True, stop=False)
                last = e.matmul(c.bank[3][:, hl * 128:(hl + 1) * 128], lhsT=c.keT[z][:, h, :], rhs=c.qeTB[z][:, h, :], start=False, stop=True)
            return last

        P.op("pe", mm_sc, reads=[("keT", z), ("qeAT", z), ("qeBT", z)], writes=[("bank", 3)])
        P.op("dve", lambda e: e.tensor_tensor(out=c.scm[:, :], in0=c.bank[3][:, :], in1=c.nmask4[:, :], op=ALU.mult),
             reads=[("bank", 3), "consts"], writes=["scm"])

    def o_group(g4):
        ob = 4 + g4

        def mm_o(e):
            last = None
            for hl in range(4):
                h = g4 * 4 + hl
                hs = slice(h * 128, (h + 1) * 128)
                oc = slice(hl * 128, (hl + 1) * 128)
                e.matmul(c.bank[ob][:, oc], lhsT=c.scm[:, hl * 128:(hl + 1) * 128], rhs=c.v[z][:, hs], start=True, stop=False)
                e.matmul(c.bank[ob][:, oc], lhsT=c.qeTA[z][:, h, :], rhs=c.Sb[:, h, par, :], start=False, stop=False)
                last = e.matmul(c.bank[ob][:, oc], lhsT=c.qeTB[z][:, h, :], rhs=c.Sb[:, h, 2, :], start=False, stop=True)
            return last

        P.op("pe", mm_o, reads=["scm", ("v", z, g4), ("qeAT", z), ("qeBT", z)] + [("Sb", g4 * 4 + hl, s) for hl in range(4) for s in (par, 2)],
             writes=[("bank", ob)])

    def post():
        P.op("dve", lambda e: e.memset(c.ssqo[:, :], 0.0), writes=["ssqo"])
        for h in range(8):
            ob = 4 + h // 4
            oc = slice((h % 4) * 128, (h % 4 + 1) * 128)
            P.op("act", lambda e, h=h, ob=ob, oc=oc: e.activation(out=c.un[:, 1, 0:128], in_=c.bank[ob][:, oc], func=AF.Square,
                                                                  accum_out=c.ssqo[:, h:h + 1]),
                 reads=[("bank", ob)], writes=[("un", 1), "ssqo"])
        P.op("dve", lambda e: e.tensor_scalar(out=c.rstdo[:, :], in0=c.ssqo[:, :], scalar1=1.0 / 128, scalar2=EPS,
                                              op0=ALU.mult, op1=ALU.add), reads=["ssqo"], writes=["rstdo"])
        P.op("act", lambda e: e.activation(out=c.rstdo[:, :], in_=c.rstdo[:, :], func=AF.Ln), reads=["rstdo"], writes=["rstdo"])
        P.op("act", lambda e: e.activation(out=c.rstdo[:, :], in_=c.rstdo[:, :], func=AF.Exp, scale=-0.5), reads=["rstdo"], writes=["rstdo"])
        for hf in range(2):
            sl = sl2[hf]
            P.op("dve", lambda e, hf=hf, sl=sl: e.tensor_tensor(out=c.og1[:, sl], in0=c.bank[4 + hf][:, :], in1=c.gs[z][:, sl], op=ALU.mult),
                 reads=[("bank", 4 + hf), ("gs", z, hf)] + [("S2", 4 * hf + i) for i in range(4)],
                 writes=[("og1", hf)] + [("S2", 4 * hf + i) for i in range(4)])
        for h in range(8):
            hs = slice(h * 128, (h + 1) * 128)
            P.op("dve", lambda e, h=h, hs=hs: e.scalar_tensor_tensor(
                out=c.og2[:, hs], in0=c.og1[:, hs], scalar=c.rstdo[:, h:h + 1], in1=c.gout[:, hs], op0=ALU.mult, op1=ALU.mult),
                reads=[("og1", h // 4), ("S2", h), "rstdo", "lbc"], writes=[("og2", h // 4)])

    def post_b():
        def tr_o(e):
            last = None
            for h in range(8):
                last = e.transpose(out=c.psTb[:, h * 128:(h + 1) * 128], in_=c.og2[:, h * 128:(h + 1) * 128], identity=c.ident[:, :])
            return last

        P.group(lambda: (
            P.op("pe", tr_o, reads=[("og2", 0), ("og2", 1), "consts"], writes=[("bank", 7)]),
            P.op("act", lambda e: e.copy(out=c.uT[:, :, ti * 128:(ti + 1) * 128], in_=c.psTb[:, :].rearrange("p (k n) -> p k n", k=8)),
                 reads=[("bank", 7)], writes=[("uT", ti // 4)])))

    if not full:
        return [lambda: kv_group(0), lambda: None, lambda: kv_group(1)]

    def B1():
        kv_group(0)

    def B2():
        sc_group(0)
        kv_group(1)

    def B3():
        o_group(0)
        sc_group(1)

    def B4():
        o_group(1)
        post()

    return [B1, lambda: None, B2, B3, B4, post_b]


def hgrn_run_tiles(P, c, tiles):
    def load_x(k):
        src, i, _ = tiles[k]
        s = k % 2
        P.dma("sp", lambda e, src=src, i=i, s=s: e.dma_start(out=c.xs[:, s, :], in_=src[:, i, :]), ("xs", s), writes=[("xs", s)])

    n = len(tiles)
    load_x(0)
    Fst = hgrn_F_stages(P, c, c.xs[:, 0, :], ("xs", 0), tiles[0][2], 0)
    for f in Fst:
        f()
    nfull = 0
    for k in range(n):
        if k + 1 < n:
            load_x(k + 1)
            Fn = hgrn_F_stages(P, c, c.xs[:, (k + 1) % 2, :], ("xs", (k + 1) % 2), tiles[k + 1][2], (k + 1) % 2)
        else:
            Fn = []
        Bk = hgrn_B_stages(P, c, tiles[k][2], k % 2, c.hpar, nfull)
        c.hpar = 1 - c.hpar
        nfull += 1 if tiles[k][2] else 0
        la = P.record(lambda: [f() for f in Fn]) if Fn else []
        lb = P.record(lambda: [b() for b in Bk])
        if la:
            P.replay_merged(la, lb)
        else:
            for kind, args in lb:
                getattr(P, kind)(*args)


NHALO = 4 * (1 + 4 + 16)
OROW = 132


def build_G(dbg=None):
    nc = bass.Bass("TRN2", target_bir_lowering=False)
    dr = lambda name, shape, kind="ExternalInput": nc.dram_tensor(name, list(shape), F32, kind=kind).ap()
    x_d = dr("x", [T, D])
    xp_d = dr("x_prev", [NPREV * 128, D])
    gains_d = dr("gains", [128, 5, D])
    cst_d = dr("cst", [128, 5, 128])
    lbl_d = dr("lb_logits", [128, 3, D])
    gout_d = dr("gout", [128, D])
    hw_in_d = dr("hgrn_win", [128, 8, 4096])
    hw_out_d = dr("hgrn_wout", [128, 8, D])
    win_d = dr("ffn_win", [2, NJ, 128, 8, 256])
    wd_d = dr("ffn_wd", [2, DFF, D])
    wqkv_d = dr("wqkv", [9, 128, 8, 512])
    rope_d = dr("rope", [2, 128, 2, NT, 256])
    wo_d = dr("attn_wout", [128, 12, D])
    cstb_d = dr("cstb", [128, 5, 128])
    out_d = dr("out", [T, D], kind="ExternalOutput")
    qkv_scr = nc.dram_tensor("qkv_scr", [2 * T * 12, 384], BF16, kind="Internal").ap()
    o_loc = nc.dram_tensor("o_loc", [T, 12 * OROW], F32, kind="Internal").ap()
    s_scr = nc.dram_tensor("s_scr", [128, 1024], F32, kind="Internal").ap()

    with ExitStack() as es:
        sb = lambda name, shape, dt=F32: es.enter_context(nc.sbuf_tensor("sb_" + name, list(shape), dt))
        ps = lambda name, shape, dt=F32: es.enter_context(nc.psum_tensor("ps_" + name, list(shape), dt))
        c = Ctx()
        arA = sb("arA", [128, 16384])
        arC = sb("arC", [128, 16960])
        c.h = arA[:, :].rearrange("p (t d) -> p t d", t=NT)
        c.W = arA[:, :].bitcast(BF16).rearrange("p (k n) -> p k n", k=8)
        c.uT = sb("uT", [128, 8, T], BF16)
        arD = sb("arD", [128, 8192])
        c.oml = arD[:, 0:1024]
        c.gout = arD[:, 1024:2048]
        c.xs = arD[:, 2048:4096].rearrange("p (s n) -> p s n", s=2)
        c.wout = arD[:, 4096:8192].bitcast(BF16).rearrange("p (k n) -> p k n", k=8)
        c.wo = arD[:, 0:6144].bitcast(BF16).rearrange("p (k n) -> p k n", k=12)
        c.gcur = sb("gcur", [128, D])
        c.un = sb("un", [128, 2, D], BF16)
        c.cstf = sb("cstf", [128, 5, 128])
        c.ident = sb("identb", [128, 128], BF16)
        c.small = sb("small", [128, 64])
        c.ssq = c.small[:, 0:16]
        c.rstd = c.small[:, 16:32]
        c.ssq1 = c.small[:, 32:33]
        c.rstd1 = c.small[:, 33:34]
        c.ssqo = c.small[:, 40:48]
        c.rstdo = c.small[:, 48:56]
        c.prefT = c.cstf[:, 1, :]
        c.sufT = c.cstf[:, 2, :]
        c.ind2 = c.cstf[:, 4, 0:2]
        c.one1 = c.small[:, 56:57]

        def cv(lo, n, dt=F32):
            a = arC[:, lo:lo + n]
            return a if dt == F32 else a.bitcast(dt)

        c.aT = cv(0, 5120, BF16).rearrange("p (j n) -> p j n", j=5)
        c.win = cv(5120, 3072, BF16).rearrange("p (s k n) -> p s k n", s=3, k=8)
        c.wd = cv(8192, 5120, BF16).rearrange("p (s j n) -> p s j n", s=2, j=5)
        c.sg = cv(13312, 1024).rearrange("p (s n) -> p s n", s=2)
        c.junk = cv(13312, 512, BF16)
        c.tt = cv(0, 1024); c.logf = cv(1024, 1024); c.e1 = cv(2048, 1024); c.sq = cv(3072, 1024)
        c.qeA = cv(4096, 512, BF16); c.qeB = cv(4608, 512, BF16); c.ke = cv(5120, 512, BF16); c.og2 = cv(16448, 512, BF16)
        c.uTt = cv(5632, 512, BF16).rearrange("p (k n) -> p k n", k=8)
        c.v, c.kend, c.qeTA, c.qeTB, c.keT, c.gs, c.ebl = [], [], [], [], [], [], []
        for z in range(2):
            zb = 6144 + z * 3104
            c.v.append(cv(zb, 512, BF16)); c.kend.append(cv(zb + 512, 512, BF16))
            c.qeTA.append(cv(zb + 1024, 512, BF16).rearrange("p (h n) -> p h n", h=8))
            c.qeTB.append(cv(zb + 1536, 512, BF16).rearrange("p (h n) -> p h n", h=8))
            c.keT.append(cv(zb + 2048, 512, BF16).rearrange("p (h n) -> p h n", h=8))
            c.gs.append(cv(zb + 2560, 512, BF16)); c.ebl.append(cv(zb + 3072, 16))
        c.Sflat = cv(12352, 1024)
        c.S = c.Sflat.rearrange("p (h n) -> p h n", h=8)
        c.og1 = cv(13376, 1024)
        c.S2 = c.og1.rearrange("p (h n) -> p h n", h=8)
        c.Sb = cv(14400, 1536, BF16).rearrange("p (h a n) -> p h a n", h=8, a=3)
        c.scm = cv(15936, 256, BF16)
        c.nmask4 = cv(16192, 256, BF16)
        c.hpar = 0
        c.rope = cv(0, 8192).rearrange("p (a t n) -> p a t n", a=2, t=NT)
        c.xsb = cv(8192, 1024).rearrange("p (s n) -> p s n", s=2)
        c.ost = cv(9216, 512, BF16).rearrange("p (s n) -> p s n", s=2)
        c.rt = cv(10240, 1024).rearrange("p (s n) -> p s n", s=4)
        c.rtp = [c.rt, cv(11264, 1024).rearrange("p (s n) -> p s n", s=4)]
        c.wq = c.wout[:, :, :].rearrange("p k n -> p (k n)").rearrange("p (s k n) -> p s k n", s=2, k=8)

        c.bank = [ps("b%d" % i, [128, 512]) for i in range(8)]
        c.psTb = c.bank[7][:, :].bitcast(BF16)
        c.psG = [c.bank[0], c.bank[1]]
        c.psU = [c.bank[2], c.bank[3]]
        c.psD = [c.bank[4], c.bank[5]]
        c.psT = [c.bank[6][:, :].bitcast(BF16), c.bank[7][:, :].bitcast(BF16)]
        c.win_ctr = c.wd_ctr = c.gu_ctr = c.d_ctr = 0
        c.pp_ctr = c.sc_ctr = c.kv_ctr = 0

        P = Prog(nc, es)
        P.dma("sp", lambda e: e.dma_start(out=c.cstf[:, :, :], in_=cst_d), "cst1", writes=["cstf"])
        P.dma("sp", lambda e: e.dma_start(out=c.gcur[:, :], in_=gains_d[:, 0, :]), "cst2", writes=["gcur"])
        P.dma("sp", lambda e: e.dma_start(out=c.gout[:, :], in_=gout_d), "cst3", writes=["gout_l"])
        P.op("dve", lambda e: e.tensor_copy(out=c.ident[:, :], in_=c.cstf[:, 0, :]), reads=["cstf"], writes=["ident_"])
        for j in range(4):
            P.op("dve", lambda e, j=j: e.tensor_scalar(out=c.nmask4[:, j * 128:(j + 1) * 128], in0=c.cstf[:, 3, :], scalar1=-1.0, scalar2=0.0,
                                                       op0=ALU.mult, op1=ALU.add), reads=["cstf", "ident_"], writes=["nm4"])
        P.op("dve", lambda e: e.memset(c.one1, 1.0), reads=["nm4"], writes=["consts"])
        for q in range(4):
            P.dma("pool", lambda e, q=q: e.dma_start(out=c.W[:, 2 * q:2 * q + 2, :], in_=hw_in_d[:, 2 * q:2 * q + 2, :]),
                  "W", writes=["W"])
        P.dma("pool", lambda e: e.dma_start(out=c.wout[:, :, :], in_=hw_out_d), "wout", writes=["wout"])
        lg = arC[:, 4096:7168].rearrange("p (a n) -> p a n", a=3)
        P.dma("sp", lambda e: e.dma_start(out=lg, in_=lbl_d), "cst4", writes=["lg"])
        P.op("dve", lambda e: e.tensor_tensor(out=c.e1[:, :], in0=lg[:, 0, :], in1=lg[:, 1, :], op=ALU.max), reads=["lg"], writes=["lmx"])
        P.op("dve", lambda e: e.tensor_tensor(out=c.e1[:, :], in0=c.e1[:, :], in1=lg[:, 2, :], op=ALU.max), reads=["lg", "lmx"], writes=["lmx"])
        for a in range(3):
            P.op("dve", lambda e, a=a: e.tensor_tensor(out=lg[:, a, :], in0=lg[:, a, :], in1=c.e1[:, :], op=ALU.subtract),
                 reads=["lg", "lmx"], writes=["lg"])
        P.op("act", lambda e: e.activation(out=lg, in_=lg, func=AF.Exp), reads=["lg"], writes=["lg"])
        P.op("dve", lambda e: e.tensor_tensor(out=c.e1[:, :], in0=lg[:, 0, :], in1=lg[:, 1, :], op=ALU.add), reads=["lg"], writes=["lmx"])
        P.op("dve", lambda e: e.tensor_tensor(out=c.e1[:, :], in0=c.e1[:, :], in1=lg[:, 2, :], op=ALU.add), reads=["lg", "lmx"], writes=["lmx"])
        P.op("dve", lambda e: e.reciprocal(out=c.e1[:, :], in_=c.e1[:, :]), reads=["lmx"], writes=["lmx"])
        P.op("dve", lambda e: e.tensor_tensor(out=c.sq[:, :], in0=lg[:, 0, :], in1=c.e1[:, :], op=ALU.mult), reads=["lg", "lmx"], writes=["lb_"])
        P.op("dve", lambda e: e.tensor_scalar(out=c.oml[:, :], in0=c.sq[:, :], scalar1=-1.0, scalar2=1.0, op0=ALU.mult, op1=ALU.add),
             reads=["lb_", "gout_l"], writes=["lbc"])
        P.op("dve", lambda e: e.memset(c.Sflat, 0.0), reads=["lbc"], writes=[("S", h) for h in range(8)])
        P.op("dve", lambda e: e.memset(c.Sb[:, :, :, :], 0.0), writes=[("Sb", h, a) for h in range(8) for a in range(3)])
        P.barrier()

        xpv = xp_d.rearrange("(t p) d -> p t d", p=128)
        xv = x_d.rearrange("(t p) d -> p t d", p=128)
        qv = qkv_scr.rearrange("(t p h) (x n) -> p t h x n", p=128, h=12, x=3)
        Sflat = c.Sflat

        def layer0_pass(pz, tiles, xr_view, xr_base, cbs):
            hgrn_run_tiles(P, c, tiles)
            P.barrier()
            P.dma("sp", lambda e: e.dma_start(out=s_scr, in_=Sflat), "ssave", reads=[("S", h) for h in range(8)])
            def load_x2(t):
                s = t % 2
                P.dma("sp", lambda e, t=t, s=s: e.dma_start(out=c.xs[:, s, :], in_=xr_view[:, xr_base + t, :]), ("xs", s), writes=[("xs", s)])

            load_x2(0)
            for t in range(NT):
                if t + 1 < NT:
                    load_x2(t + 1)
                for hf in range(2):
                    b = c.pp_ctr % 2
                    c.pp_ctr += 1

                    def mmo(e, t=t, hf=hf, b=b):
                        last = None
                        for kc in range(8):
                            last = e.matmul(c.bank[b][:, :], lhsT=c.uT[:, kc, t * 128:(t + 1) * 128], rhs=c.wout[:, kc, hf * 512:(hf + 1) * 512],
                                            start=(kc == 0), stop=(kc == 7))
                        return last

                    P.op("pe", mmo, reads=[("uT", t // 4), "wout"], writes=[("bank", b)])
                    P.op("dve", lambda e, t=t, hf=hf, b=b: e.tensor_tensor(
                        out=c.h[:, t, hf * 512:(hf + 1) * 512], in0=c.xs[:, t % 2, hf * 512:(hf + 1) * 512], in1=c.bank[b][:, :], op=ALU.add),
                        reads=[("xs", t % 2), ("bank", b)], writes=[("h", t)])
            P.barrier()
            P.dma("sp", lambda e: e.dma_start(out=c.gcur[:, :], in_=gains_d[:, 1, :]), "g1_%d" % pz, writes=["consts", "gcur"])
            emit_norm_T(P, c, lambda t: ("h", t), lambda t: c.h[:, t, :], c.gcur[:, :], "f0")
            emit_ffn(P, c, 0, win_d, wd_d)
            P.barrier()
            P.dma("sp", lambda e: e.dma_start(out=c.gcur[:, :], in_=gains_d[:, 2, :]), "g2_%d" % pz, writes=["consts", "gcur"])
            P.dma("sp", lambda e: e.dma_start(out=c.rope[:, :, :, :], in_=rope_d[pz]), "rp_%d" % pz, writes=["rope"])
            emit_norm_T(P, c, lambda t: ("h", t), lambda t: c.h[:, t, :], c.gcur[:, :], "m1")

            def load_wq(j):
                s = j % 2
                P.dma("pool", lambda e, j=j, s=s: e.dma_start(out=c.wq[:, s, :, :], in_=wqkv_d[cbs[j]]), ("wq", s), writes=[("wq", s)])

            load_wq(0)
            cnt = 0
            for j, cb in enumerate(cbs):
                if j + 1 < len(cbs):
                    load_wq(j + 1)
                for t in range(NT):
                    if pz == 0 and t < NT - (1, 4, 16)[cb % 3]:
                        continue
                    b = c.pp_ctr % 2
                    c.pp_ctr += 1
                    s2 = cnt % 2
                    cnt += 1

                    def mmq(e, j=j, t=t, b=b):
                        last = None
                        for kc in range(8):
                            last = e.matmul(c.bank[b][:, :], lhsT=c.uT[:, kc, t * 128:(t + 1) * 128], rhs=c.wq[:, j % 2, kc, :],
                                            start=(kc == 0), stop=(kc == 7))
                        return last

                    P.op("pe", mmq, reads=[("uT", t // 4), ("wq", j % 2)], writes=[("bank", b)])
                    if cb < 6:
                        scale = (128.0 ** -0.5) if cb < 3 else 1.0
                        P.op("act", lambda e, b=b, s2=s2, scale=scale: e.activation(out=c.xsb[:, s2, :], in_=c.bank[b][:, :], func=AF.Copy, scale=scale),
                             reads=[("bank", b)], writes=[("xsb", s2)])
                        xh = c.xsb[:, s2, :].rearrange("p (h a n) -> p h a n", h=4, a=2)
                        oh = c.ost[:, s2, :].rearrange("p (h a n) -> p h a n", h=4, a=2)
                        cosv = c.rope[:, 0, t, :].rearrange("p (h n) -> p h n", h=4)
                        sinv = c.rope[:, 1, t, :].rearrange("p (h n) -> p h n", h=4)
                        r = [c.rtp[s2][:, i, 0:256].rearrange("p (h n) -> p h n", h=4) for i in range(4)]
                        rk = [("xsb", s2), "rope"]
                        P.op("dve", lambda e, xh=xh, cosv=cosv, r=r: e.tensor_tensor(out=r[0], in0=xh[:, :, 0, :], in1=cosv, op=ALU.mult), reads=rk, writes=[("rt0", s2)])
                        P.op("pool", lambda e, xh=xh, sinv=sinv, r=r: e.tensor_tensor(out=r[1], in0=xh[:, :, 1, :], in1=sinv, op=ALU.mult), reads=rk, writes=[("rt1", s2)])
                        P.op("dve", lambda e, xh=xh, cosv=cosv, r=r: e.tensor_tensor(out=r[2], in0=xh[:, :, 1, :], in1=cosv, op=ALU.mult), reads=rk, writes=[("rt2", s2)])
                        P.op("pool", lambda e, xh=xh, sinv=sinv, r=r: e.tensor_tensor(out=r[3], in0=xh[:, :, 0, :], in1=sinv, op=ALU.mult), reads=rk, writes=[("rt3", s2)])
                        P.op("dve", lambda e, oh=oh, r=r: e.tensor_tensor(out=oh[:, :, 0, :], in0=r[0], in1=r[1], op=ALU.subtract),
                             reads=[("rt0", s2), ("rt1", s2)], writes=[("ost", s2)])
                        P.op("pool", lambda e, oh=oh, r=r: e.tensor_tensor(out=oh[:, :, 1, :], in0=r[2], in1=r[3], op=ALU.add),
                             reads=[("rt2", s2), ("rt3", s2), ("ost", s2)], writes=[("ost", s2)])
                    else:
                        P.op("act", lambda e, b=b, s2=s2: e.copy(out=c.ost[:, s2, :], in_=c.bank[b][:, :]), reads=[("bank", b)], writes=[("ost", s2)])
                    P.dma("sp", lambda e, cb=cb, t=t, s2=s2: e.dma_start(
                        out=qv[:, pz * NT + t, 4 * (cb % 3):4 * (cb % 3) + 4, cb // 3, :], in_=c.ost[:, s2, :].rearrange("p (h n) -> p h n", h=4)),
                          ("qo", s2), reads=[("ost", s2)])
            P.barrier()

        if dbg is not None:
            npre_, nfull_ = dbg
            tiles = [(xpv, NPREV - npre_ + i, False) for i in range(npre_)] + [(xv, i, True) for i in range(nfull_)]
            hgrn_run_tiles(P, c, tiles)
            P.barrier()
            for t in range(nfull_):
                P.dma("sp", lambda e, t=t: e.dma_start(out=c.xs[:, t % 2, :], in_=xv[:, t, :]), ("xs", t % 2), writes=[("xs", t % 2)])
                for hf in range(2):
                    b = c.pp_ctr % 2
                    c.pp_ctr += 1

                    def mmo(e, t=t, hf=hf, b=b):
                        last = None
                        for kc in range(8):
                            last = e.matmul(c.bank[b][:, :], lhsT=c.uT[:, kc, t * 128:(t + 1) * 128], rhs=c.wout[:, kc, hf * 512:(hf + 1) * 512],
                                            start=(kc == 0), stop=(kc == 7))
                        return last

                    P.op("pe", mmo, reads=[("uT", t // 4), "wout"], writes=[("bank", b)])
                    P.op("dve", lambda e, t=t, hf=hf, b=b: e.tensor_tensor(
                        out=c.h[:, t, hf * 512:(hf + 1) * 512], in0=c.xs[:, t % 2, hf * 512:(hf + 1) * 512], in1=c.bank[b][:, :], op=ALU.add),
                        reads=[("xs", t % 2), ("bank", b)], writes=[("h", t)])
            ovd = out_d.rearrange("(t p) d -> p t d", p=128)
            for t in range(nfull_):
                P.dma("sp", lambda e, t=t: e.dma_start(out=ovd[:, t, :], in_=c.h[:, t, :]), ("o", t), reads=[("h", t)])
            P.wait_dma_done("sp", [("o", t) for t in range(nfull_)])
            P.finalize()
            return nc
        layer0_pass(0, [(xpv, i, False) for i in range(0, NPREV - NT)] + [(xpv, i, True) for i in range(NPREV - NT, NPREV)],
                    xpv, NPREV - NT, list(range(3, 9)))
        P.dma("sp", lambda e: e.dma_start(out=c.gcur[:, :], in_=gains_d[:, 0, :]), "g0_1", writes=["consts", "gcur"])
        for q in range(4):
            P.dma("pool", lambda e, q=q: e.dma_start(out=c.W[:, 2 * q:2 * q + 2, :], in_=hw_in_d[:, 2 * q:2 * q + 2, :]),
                  "W", writes=["W"])
        P.dma("pool", lambda e: e.dma_start(out=c.wout[:, :, :], in_=hw_out_d), "wout", writes=["wout"])
        P.dma("sp", lambda e: e.dma_start(out=Sflat, in_=s_scr), "srest", writes=[("S", h) for h in range(8)])
        for h in range(8):
            P.op("act", lambda e, h=h, pr=c.hpar: e.copy(out=c.Sb[:, h, pr, :], in_=c.S[:, h, :]), reads=[("S", h)], writes=[("Sb", h, c.hpar)])
        P.barrier()
        layer0_pass(1, [(xv, i, True) for i in range(NT)], xv, 0, list(range(9)))
        emit_B_g(P, c, arC, qkv_scr, o_loc, cstb_d)
        P.barrier()
        emit_C_g(P, c, arC, o_loc, gains_d, wo_d, win_d, wd_d, out_d)
        P.finalize()
    return nc


def emit_B_g(P, c, arC, qkv_all, o_loc, cstb_d):
    def cv(lo, n, dt=F32):
        a = arC[:, lo:lo + n]
        return a if dt == F32 else a.bitcast(dt)

    cstb = cv(0, 640).rearrange("p (a n) -> p a n", a=5)
    mask4 = cv(640, 1024).rearrange("p (v n) -> p v n", v=2)
    raw = cv(1664, 1920, BF16).rearrange("p (s h n) -> p s h n", s=5, h=2)
    qT = cv(3584, 256, BF16).rearrange("p (s n) -> p s n", s=2)
    kT = cv(3840, 640, BF16).rearrange("p (s n) -> p s n", s=5)
    sm = cv(4480, 1024).rearrange("p (s n) -> p s n", s=2)
    pp = cv(5504, 512, BF16).rearrange("p (s n) -> p s n", s=2)
    pT = cv(6016, 512, BF16).rearrange("p (s n) -> p s n", s=2)
    ost = cv(6528, 528).rearrange("p (g h n) -> p g h n", g=2, h=2)
    stt = cv(7056, 32).rearrange("p (s n) -> p s n", s=2)
    P.dma("sp", lambda e: e.dma_start(out=cstb, in_=cstb_d), "bi2", writes=["cstb"])
    for v in range(2):
        for hh in range(2):
            P.op("dve", lambda e, v=v, hh=hh: e.tensor_copy(
                out=mask4[:, v, hh * 256:(hh + 1) * 256], in_=cstb[:, 1 + 2 * v:3 + 2 * v, :].rearrange("p a n -> p (a n)")),
                reads=["cstb"], writes=["mask4"])
    P.op("dve", lambda e: e.memset(ost, 0.0), writes=[("ost", 0), ("ost", 1)])

    def unit_stages(it, u, blk, hslot):
        dd = DIL[u // 4]
        nbr = NT // dd
        r, n = blk // nbr, blk % nbr
        first = n == 0
        s2 = it % 2
        s3 = it % 3
        sp3 = hslot if first else (it - 1) % 3
        qsrc = qkv_all.rearrange("(a i d h) c -> d a i h c", i=128, d=dd, h=12)
        qb, tb, ob = 6 + s2, 2 + s2, 4 + s2
        psQ = c.bank[qb][:, :].bitcast(BF16)
        psT = c.bank[tb][:, :].bitcast(BF16)
        st = stt[:, s2, :]
        sk = [("st", s2)]

        def S1():
            if first:
                P.dma("pool", lambda e: e.dma_start(out=raw[:, hslot, :, 128:384], in_=qsrc[r, NT // dd - 1, :, u:u + 2, 128:384]),
                      ("raw", hslot), reads=["qkv_all"], writes=[("raw", hslot)])
            P.dma("pool", lambda e: e.dma_start(out=raw[:, s3, :, :], in_=qsrc[r, NT // dd + n, :, u:u + 2, :]),
                  ("raw", s3), reads=["qkv_all"], writes=[("raw", s3)])

            def tr_qk(e):
                last = None
                for hh in range(2):
                    e.transpose(out=psQ[:, hh * 128:(hh + 1) * 128], in_=raw[:, s3, hh, 0:128], identity=c.ident[:, :])
                    last = e.transpose(out=psQ[:, 256 + hh * 128:256 + (hh + 1) * 128], in_=raw[:, s3, hh, 128:256], identity=c.ident[:, :])
                    if first:
                        last = e.transpose(out=psQ[:, 512 + hh * 128:512 + (hh + 1) * 128], in_=raw[:, hslot, hh, 128:256],
                                           identity=c.ident[:, :])
                return last

            P.op("pe", tr_qk, reads=[("raw", s3), "consts"] + ([("raw", hslot)] if first else []), writes=[("bank", qb)])

        def S1b():
            P.op("act", lambda e: e.copy(out=qT[:, s2, :], in_=psQ[:, 0:256]), reads=[("bank", qb)], writes=[("qT", s2)])
            P.op("act", lambda e: e.copy(out=kT[:, s3, :], in_=psQ[:, 256:512]), reads=[("bank", qb)], writes=[("kT", s3)])
            if first:
                P.op("act", lambda e: e.copy(out=kT[:, hslot, :], in_=psQ[:, 512:768]), reads=[("bank", qb)], writes=[("kT", hslot)])

            def mm_s(e):
                last = None
                for hh in range(2):
                    hs = slice(hh * 128, (hh + 1) * 128)
                    e.matmul(c.bank[s2][:, hh * 256 + 128:hh * 256 + 256], lhsT=qT[:, s2, hs], rhs=kT[:, s3, hs], start=True, stop=True)
                    last = e.matmul(c.bank[s2][:, hh * 256:hh * 256 + 128], lhsT=qT[:, s2, hs], rhs=kT[:, sp3, hs], start=True, stop=True)
                return last

            P.op("pe", mm_s, reads=[("qT", s2), ("kT", s3), ("kT", sp3)], writes=[("bank", s2)])

        def S2():
            P.op("dve", lambda e: e.tensor_tensor(out=sm[:, s2, :], in0=c.bank[s2][:, :], in1=mask4[:, 1 if first else 0, :], op=ALU.add),
                 reads=[("bank", s2), "mask4"], writes=[("sm", s2)])
            P.op("dve", lambda e: e.reduce_max(out=st[:, 0:2], in_=sm[:, s2, :].rearrange("p (h n) -> p h n", h=2), axis=mybir.AxisListType.X),
                 reads=[("sm", s2)], writes=sk)
            P.op("dve", lambda e: e.tensor_scalar(out=st[:, 2:4], in0=st[:, 0:2], scalar1=-1.0, scalar2=0.0, op0=ALU.mult, op1=ALU.add),
                 reads=sk, writes=sk)
            P.op("dve", lambda e: e.memset(st[:, 4:6], 0.0), reads=sk, writes=sk)
            for hh in range(2):
                cs = slice(hh * 256, (hh + 1) * 256)
                P.op("act", lambda e, hh=hh, cs=cs: e.activation(out=pp[:, s2, cs], in_=sm[:, s2, cs], func=AF.Exp, bias=st[:, 2 + hh:3 + hh],
                                                                 accum_out=st[:, 4 + hh:5 + hh]),
                     reads=[("sm", s2)] + sk, writes=[("p", s2)] + sk)

        def S3():
            def tr_p(e):
                last = None
                for j in range(4):
                    last = e.transpose(out=psT[:, j * 128:(j + 1) * 128], in_=pp[:, s2, j * 128:(j + 1) * 128], identity=c.ident[:, :])
                return last

            P.op("pe", tr_p, reads=[("p", s2), "consts"], writes=[("bank", tb)])

        def S3b():
            P.op("act", lambda e: e.copy(out=pT[:, s2, :], in_=psT[:, 0:512]), reads=[("bank", tb)], writes=[("pT", s2)])

            def mm_o(e):
                last = None
                for hh in range(2):
                    oc = slice(hh * 128, (hh + 1) * 128)
                    e.matmul(c.bank[ob][:, oc], lhsT=pT[:, s2, hh * 256 + 128:hh * 256 + 256], rhs=raw[:, s3, hh, 256:384], start=True, stop=False)
                    last = e.matmul(c.bank[ob][:, oc], lhsT=pT[:, s2, hh * 256:hh * 256 + 128], rhs=raw[:, sp3, hh, 256:384], start=False, stop=True)
                return last

            P.op("pe", mm_o, reads=[("pT", s2), ("raw", s3), ("raw", sp3)], writes=[("bank", ob)])

        def S4():
            P.op("dve", lambda e: e.reciprocal(out=st[:, 6:8], in_=st[:, 4:6]), reads=sk, writes=sk)
            for hh in range(2):
                P.op("dve", lambda e, hh=hh: e.tensor_scalar(out=ost[:, s2, hh, 0:128], in0=c.bank[ob][:, hh * 128:(hh + 1) * 128],
                                                             scalar1=st[:, 6 + hh:7 + hh], scalar2=0.0, op0=ALU.mult, op1=ALU.add),
                     reads=[("bank", ob)] + sk, writes=[("ost", s2)])
            P.op("act", lambda e: e.activation(out=st[:, 8:10], in_=st[:, 4:6], func=AF.Ln), reads=sk, writes=sk)
            P.op("dve", lambda e: e.tensor_tensor(out=ost[:, s2, :, 128], in0=st[:, 8:10], in1=st[:, 0:2], op=ALU.add),
                 reads=sk, writes=[("ost", s2)])
            P.dma("sp", lambda e: e.dma_start(out=o_loc.rearrange("(n i d) (h c) -> d n i h c", i=128, d=dd, h=12)[r, n, :, u:u + 2, :],
                                              in_=ost[:, s2, :, :]), ("oo", s2), reads=[("ost", s2)])

        return [S1, S1b, S2, S3, S3b, S4]

    units = []
    it = 0
    nh = 0
    for u in range(0, 12, 2):
        for blk in range(NT):
            nbr = NT // DIL[u // 4]
            hs = 3 + nh % 2
            if blk % nbr == 0:
                nh += 1
            units.append(unit_stages(it, u, blk, hs))
            it += 1
    nu = len(units)
    units[0][0]()
    units[0][1]()
    units[0][2]()
    for k in range(nu):
        nxt = units[k + 1] if k + 1 < nu else None
        if nxt:
            nxt[0]()
        units[k][3]()
        if nxt:
            nxt[1]()
        units[k][4]()
        if nxt:
            nxt[2]()
        units[k][5]()


def emit_C_g(P, c, arC, o_all, gains_d, wo_d, win_d, wd_d, out_d):
    I32 = mybir.dt.int32

    def cv(lo, n, dt=F32):
        a = arC[:, lo:lo + n]
        return a if dt == F32 else a.bitcast(dt)

    idx2 = cv(0, 192, I32)
    oin = cv(192, 3168).rearrange("p (s h n) -> p s h n", s=2, h=12)
    og_ = cv(3360, 768, BF16)
    oT4 = cv(4128, 3072, BF16).rearrange("p (k n) -> p k n", k=12)
    alb = cv(7200, 96).rearrange("p (s n) -> p s n", s=2)
    P.dma("pool", lambda e: e.dma_start(out=c.wo[:, :, :], in_=wo_d), "wo", writes=["wo"])
    for t in range(NT):
        s = t % 2
        al = alb[:, s, :]
        P.dma("sp", lambda e, s=s, t=t: e.dma_start(out=oin[:, s, :, :], in_=o_all[t * 128:(t + 1) * 128, :].rearrange("p (h c) -> p h c", h=12)),
              ("oin", s), reads=["o_all"], writes=[("oin", s)])
        ak = [("al", s)]
        P.op("dve", lambda e, s=s, al=al: e.tensor_copy(out=al[:, 0:12], in_=oin[:, s, :, 128]), reads=[("oin", s)], writes=ak)
        l3 = al[:, 0:12].rearrange("p (g h) -> p g h", g=3)
        e3 = al[:, 12:24].rearrange("p (g h) -> p g h", g=3)
        a3 = al[:, 36:48].rearrange("p (g h) -> p g h", g=3)
        mx = al[:, 24:28]
        sm_ = al[:, 28:32]
        P.op("dve", lambda e, l3=l3, mx=mx: e.tensor_tensor(out=mx, in0=l3[:, 0, :], in1=l3[:, 1, :], op=ALU.max), reads=ak, writes=ak)
        P.op("dve", lambda e, l3=l3, mx=mx: e.tensor_tensor(out=mx, in0=mx, in1=l3[:, 2, :], op=ALU.max), reads=ak, writes=ak)
        for g in range(3):
            P.op("dve", lambda e, l3=l3, e3=e3, mx=mx, g=g: e.tensor_tensor(out=e3[:, g, :], in0=l3[:, g, :], in1=mx, op=ALU.subtract),
                 reads=ak, writes=ak)
        P.op("act", lambda e, al=al: e.activation(out=al[:, 12:24], in_=al[:, 12:24], func=AF.Exp), reads=ak, writes=ak)
        P.op("dve", lambda e, e3=e3, sm_=sm_: e.tensor_tensor(out=sm_, in0=e3[:, 0, :], in1=e3[:, 1, :], op=ALU.add), reads=ak, writes=ak)
        P.op("dve", lambda e, e3=e3, sm_=sm_: e.tensor_tensor(out=sm_, in0=sm_, in1=e3[:, 2, :], op=ALU.add), reads=ak, writes=ak)
        P.op("dve", lambda e, sm_=sm_: e.reciprocal(out=sm_, in_=sm_), reads=ak, writes=ak)
        for g in range(3):
            P.op("dve", lambda e, e3=e3, a3=a3, sm_=sm_, g=g: e.tensor_tensor(out=a3[:, g, :], in0=e3[:, g, :], in1=sm_, op=ALU.mult),
                 reads=ak, writes=ak)
        for hh in range(12):
            eng = "dve" if hh % 2 == 0 else "pool"
            P.op(eng, lambda e, s=s, hh=hh, al=al: e.tensor_scalar(
                out=og_[:, hh * 128:(hh + 1) * 128], in0=oin[:, s, hh, 0:128], scalar1=al[:, 36 + hh:37 + hh],
                scalar2=0.0, op0=ALU.mult, op1=ALU.add), reads=[("oin", s)] + ak, writes=["og"])
        for part, (k0, k1) in enumerate(((0, 8), (8, 12))):
            def tr(e, k0=k0, k1=k1, part=part):
                last = None
                for kc in range(k0, k1):
                    last = e.transpose(out=c.psT[part][:, (kc - k0) * 128:(kc - k0 + 1) * 128], in_=og_[:, kc * 128:(kc + 1) * 128],
                                       identity=c.ident[:, :])
                return last

            P.op("pe", tr, reads=["og", "consts"], writes=[("psT", part)])
            P.op("act", lambda e, k0=k0, k1=k1, part=part, t=t: e.copy(
                out=oT4[:, k0:k1, (t % 4) * 128:(t % 4 + 1) * 128],
                in_=c.psT[part][:, 0:(k1 - k0) * 128].rearrange("p (k n) -> p k n", k=k1 - k0)),
                reads=[("psT", part)], writes=[("oT4", t % 4)])
        if t % 4 == 3:
            for tt in range(t - 3, t + 1):
                for hf in range(2):
                    pb = c.d_ctr % 2
                    c.d_ctr += 1

                    def mmo(e, tt=tt, hf=hf, pb=pb):
                        last = None
                        for kc in range(12):
                            last = e.matmul(c.psD[pb][:, :], lhsT=oT4[:, kc, (tt % 4) * 128:(tt % 4 + 1) * 128],
                                            rhs=c.wo[:, kc, hf * 512:(hf + 1) * 512], start=(kc == 0), stop=(kc == 11))
                        return last

                    P.op("pe", mmo, reads=[("oT4", tt % 4), "wo"], writes=[("psD", pb)])
                    P.op("dve", lambda e, tt=tt, hf=hf, pb=pb: e.tensor_tensor(
                        out=c.h[:, tt, hf * 512:(hf + 1) * 512], in0=c.h[:, tt, hf * 512:(hf + 1) * 512], in1=c.psD[pb][:, :], op=ALU.add),
                        reads=[("psD", pb), ("h", tt)], writes=[("h", tt)])
    P.barrier()
    P.dma("sp", lambda e: e.dma_start(out=c.gcur[:, :], in_=gains_d[:, 3, :]), "cst8", writes=["consts"])
    emit_norm_T(P, c, lambda t: ("h", t), lambda t: c.h[:, t, :], c.gcur[:, :], "f1")
    emit_ffn(P, c, 1, win_d, wd_d)
    P.barrier()
    P.dma("sp", lambda e: e.dma_start(out=c.gcur[:, :], in_=gains_d[:, 4, :]), "cst9", writes=["consts"])
    P.op("dve", lambda e: e.memset(c.ssq[:, :], 0.0), writes=["ssq"])
    for t in range(NT):
        P.op("act", lambda e, t=t: e.activation(out=c.junk[:, :], in_=c.h[:, t, :], func=AF.Square,
                                                accum_out=c.ssq[:, t:t + 1]), reads=[("h", t)], writes=["junk", "ssq"])
    P.op("dve", lambda e: e.tensor_scalar(out=c.rstd[:, :], in0=c.ssq[:, :], scalar1=1.0 / D, scalar2=EPS,
                                          op0=ALU.mult, op1=ALU.add), reads=["ssq"], writes=["rstd"])
    emit_rsqrt(P, c)
    ov = out_d.rearrange("(t p) d -> p t d", p=128)
    for t in range(NT):
        P.op("dve", lambda e, t=t: e.scalar_tensor_tensor(
            out=c.h[:, t, :], in0=c.h[:, t, :], scalar=c.rstd[:, t:t + 1], in1=c.gcur[:, :],
            op0=ALU.mult, op1=ALU.mult), reads=[("h", t), "rstd", "consts"], writes=[("h", t)])
        if t % 4 == 3:
            q = t // 4
            P.dma("sp", lambda e, q=q: e.dma_start(out=ov[:, q * 4:(q + 1) * 4, :], in_=c.h[:, q * 4:(q + 1) * 4, :]),
                  ("o", q), reads=[("h", tt) for tt in range(q * 4, q * 4 + 4)])
    P.wait_dma_done("sp", [("o", q) for q in range(4)])


def _f(a):
    return np.ascontiguousarray(np.asarray(a, dtype=np.float32))


def _consts():
    s = np.arange(128)
    same = (s[:, None] // 64) == (s[None, :] // 64)
    ident = np.eye(128, dtype=np.float32)
    prefT = (same & (s[:, None] <= s[None, :])).astype(np.float32)
    sufT = (same & (s[:, None] > s[None, :])).astype(np.float32)
    ind2 = np.zeros((128, 128), np.float32)
    ind2[:64, 0] = 1.0
    ind2[64:, 1] = 1.0
    return np.ascontiguousarray(np.stack([ident, prefT, sufT, prefT, ind2], axis=1))


def _rope_tables(seg):
    inv_freq = (1.0 / (10000.0 ** (np.arange(0, 128, 2, dtype=np.float32) / np.float32(128)))).astype(np.float32)
    pos = (seg * T + np.arange(T)).astype(np.float32)
    ang = (pos[:, None] * inv_freq[None, :]).astype(np.float32)
    cs = np.stack([np.cos(ang), np.sin(ang)], axis=0).astype(np.float32)
    cs = np.tile(cs.reshape(2, NT, 128, 1, 64), (1, 1, 1, 4, 1)).reshape(2, NT, 128, 256)
    return np.ascontiguousarray(cs.transpose(2, 0, 1, 3))


def prep_common(inputs):
    gains = np.stack([_f(inputs["norm_mix"])[0], _f(inputs["norm_ffn"])[0], _f(inputs["norm_mix"])[1],
                      _f(inputs["norm_ffn"])[1], _f(inputs["final_norm"])], axis=0)
    gains = np.ascontiguousarray(np.broadcast_to(gains[None], (128, 5, D)))
    w_in = _f(inputs["ffn_w_in"])
    g = w_in[:, :, :DFF].reshape(2, 8, 128, NJ, 128)
    u = w_in[:, :, DFF:].reshape(2, 8, 128, NJ, 128)
    win = np.ascontiguousarray(np.concatenate([g, u], axis=-1).transpose(0, 3, 2, 1, 4))
    pk = lambda w: np.ascontiguousarray(w.reshape(w.shape[0] // 128, 128, w.shape[1]).transpose(1, 0, 2))
    wqkv = _f(inputs["attn_w_qkv"])[0].reshape(8, 128, 9, 512).transpose(2, 1, 0, 3)
    return {
        "gains": gains,
        "cst": _consts(),
        "lb_logits": np.ascontiguousarray(np.broadcast_to(_f(inputs["hgrn_lb_logits"])[None], (128, 3, D))),
        "gout": np.ascontiguousarray(np.broadcast_to(np.tile(_f(inputs["hgrn_out_norm"])[0], 8)[None], (128, D))),
        "hgrn_win": pk(_f(inputs["hgrn_w_in"])[0]),
        "hgrn_wout": pk(_f(inputs["hgrn_w_out"])[0]),
        "ffn_win": win,
        "ffn_wd": _f(inputs["ffn_w_down"]),
        "wqkv": np.ascontiguousarray(wqkv),
        "attn_wout": pk(_f(inputs["attn_w_out"])[0]),
    }


_NC_CACHE = {}


def _get(name, builder):
    if name not in _NC_CACHE:
        _NC_CACHE[name] = builder()
    return _NC_CACHE[name]


F_KEYS = ("gains", "cst", "lb_logits", "gout", "hgrn_win", "hgrn_wout", "ffn_win", "ffn_wd", "wqkv", "attn_wout")


def _consts_b():
    i = np.arange(128)
    ident = np.eye(128, dtype=np.float32)
    mprev = np.where(i[None, :] >= i[:, None], 0.0, NEG).astype(np.float32)
    mcur = np.where(i[None, :] <= i[:, None], 0.0, NEG).astype(np.float32)
    return np.ascontiguousarray(np.stack([ident, mprev, mcur], axis=1))


def run_G(inputs, common):
    nc = _get("G", build_G)
    x = _f(inputs["x"])
    cb = _consts_b()
    in_maps = []
    for cidx in range(NCORES):
        b, s = cidx // 4, cidx % 4
        xp = np.zeros((NPREV * 128, D), np.float32)
        n = min(s * T, NPREV * 128)
        if n:
            xp[NPREV * 128 - n:] = x[b, s * T - n:s * T]
        mh = cb[:, 1] if s > 0 else np.full((128, 128), NEG, np.float32)
        cstb = np.ascontiguousarray(np.stack([cb[:, 0], cb[:, 1], cb[:, 2], mh, cb[:, 2]], axis=1))
        rope = np.ascontiguousarray(np.stack([_rope_tables(max(s - 1, 0)), _rope_tables(s)], axis=0))
        m = {k: common[k] for k in F_KEYS}
        m.update(x=np.ascontiguousarray(x[b, s * T:(s + 1) * T]), x_prev=xp, rope=rope, cstb=cstb)
        in_maps.append(m)
    res = run_bass_kernel_spmd(nc, in_maps, core_ids=list(range(NCORES)))
    return np.stack([np.asarray(r["out"]) for r in res.results], axis=0).reshape(2, 4 * T, D).astype(np.float32)


def kernel(**inputs):
    common = prep_common(inputs)
    return run_G(inputs, common)
```

```python
import numpy as np
from contextlib import ExitStack
import concourse.bass as bass
import concourse.mybir as mybir
from concourse.bass_utils import run_bass_kernel_spmd

F32 = mybir.dt.float32
BF16 = mybir.dt.bfloat16
AF = mybir.ActivationFunctionType
ALU = mybir.AluOpType

NCORES = 8
D = 1024
T = 2048
NT = T // 128
DFF = 2816
NJ = DFF // 128
EPS = 1e-6


import os as _os
MERGE_OFF = float(_os.environ.get("MOFF", "0.1"))
MERGE_SCALE = float(_os.environ.get("MSC", "0.9"))
SAME_ENGINE_INORDER = ("pe",)
STRICT_KEYS = ("ssq", "ssq1", "ssqo", "rstd", "rstd1", "rstdo", "st", "al", "ebl", "consts", "lmx", "lb_", "lbc")


class Prog:
    ENG = ("pe", "act", "dve", "pool", "sp")
    ATTR = {"pe": "tensor", "act": "scalar", "dve": "vector", "pool": "gpsimd", "sp": "sync"}

    def __init__(self, nc, es):
        self.nc = nc
        self.es = es
        self.streams = {e: [] for e in self.ENG}
        self.sem = {e: es.enter_context(nc.semaphore("s_" + e)) for e in ("pe", "act", "dve", "pool")}
        self.cnt = {e: 0 for e in ("pe", "act", "dve", "pool")}
        self.dsem = {}
        self.dcnt = {}
        self.known = {e: {} for e in self.ENG}
        self.lastw = {}
        self.readers = {}
        self._rec = None

    def _src_sem(self, src):
        return self.sem[src] if isinstance(src, str) else self.dsem[src[1]]

    def _deps(self, reads, writes):
        w = {}
        self._strict = {}

        def add(src, val, k):
            if val > w.get(src, 0):
                w[src] = val
            if (k if isinstance(k, str) else k[0]) in STRICT_KEYS and val > self._strict.get(src, 0):
                self._strict[src] = val

        for k in reads:
            lw = self.lastw.get(k)
            if lw:
                add(lw[0], lw[1], k)
        for k in writes:
            lw = self.lastw.get(k)
            if lw:
                add(lw[0], lw[1], k)
            for s, v in self.readers.get(k, {}).items():
                add(s, v, k)
        return w

    def _emit_waits(self, E, deps):
        for src, val in deps.items():
            if src == E and E in SAME_ENGINE_INORDER:
                if E == "pe" or src not in self._strict:
                    continue
                val = self._strict[src]
            if self.known[E].get(src, 0) >= val:
                continue
            self.known[E][src] = val
            sem = self._src_sem(src)
            self.streams[E].append(lambda eng, sem=sem, val=val: eng.wait_ge(sem, val))

    def record(self, emit):
        self._rec = []
        emit()
        r, self._rec = self._rec, None
        return r

    def group(self, emit):
        if self._rec is None:
            emit()
            return
        outer, self._rec = self._rec, []
        emit()
        inner, self._rec = self._rec, outer
        self._rec.append(("_replay", (inner,)))

    def _replay(self, items):
        for kind, args in items:
            getattr(self, kind)(*args)

    def replay_merged(self, la, lb):
        items = [((i + 0.5) / len(la), 0, i, it) for i, it in enumerate(la)] + [(MERGE_OFF + MERGE_SCALE * (j + 0.5) / len(lb), 1, j, it) for j, it in enumerate(lb)]
        import os
        if os.environ.get('MERGE') == 'seq':
            items.sort(key=lambda t: (t[1], t[2]))
        else:
            items.sort(key=lambda t: (t[0], t[1], t[2]))
        for _, _, _, (kind, args) in items:
            getattr(self, kind)(*args)

    def op(self, E, fn, reads=(), writes=()):
        if self._rec is not None:
            self._rec.append(("op", (E, fn, reads, writes)))
            return
        deps = self._deps(reads, writes)
        self._emit_waits(E, deps)
        self.cnt[E] += 1
        v = self.cnt[E]
        sem = self.sem[E]
        self.streams[E].append(lambda eng, fn=fn, sem=sem: fn(eng).then_inc(sem, 1))
        for k in reads:
            self.readers.setdefault(k, {})[E] = v
        for k in writes:
            self.lastw[k] = (E, v)
            self.readers[k] = {}

    def dma(self, Q, fn, semkey, reads=(), writes=()):
        if self._rec is not None:
            self._rec.append(("dma", (Q, fn, semkey, reads, writes)))
            return
        if semkey not in self.dsem:
            self.dsem[semkey] = self.es.enter_context(self.nc.semaphore("d_" + str(semkey)))
            self.dcnt[semkey] = 0
        deps = self._deps(reads, writes)
        self._emit_waits(Q, deps)
        self.dcnt[semkey] += 16
        v = self.dcnt[semkey]
        sem = self.dsem[semkey]
        self.streams[Q].append(lambda eng, fn=fn, sem=sem: fn(eng).then_inc(sem, 16))
        src = ("d", semkey)
        for k in reads:
            self.readers.setdefault(k, {})[src] = v
        for k in writes:
            self.lastw[k] = (src, v)
            self.readers[k] = {}

    def cc(self, fn, semkey, reads=(), writes=()):
        if semkey not in self.dsem:
            self.dsem[semkey] = self.es.enter_context(self.nc.semaphore("d_" + str(semkey)))
            self.dcnt[semkey] = 0
        deps = self._deps(reads, writes)
        self._emit_waits("pool", deps)
        self.dcnt[semkey] += 1
        v = self.dcnt[semkey]
        sem = self.dsem[semkey]
        self.streams["pool"].append(lambda eng, fn=fn, sem=sem: fn(eng).then_inc(sem, 1))
        src = ("d", semkey)
        for k in reads:
            self.readers.setdefault(k, {})[src] = v
        for k in writes:
            self.lastw[k] = (src, v)
            self.readers[k] = {}

    def wait_dma_done(self, E, semkeys):
        for sk in semkeys:
            sem, val = self.dsem[sk], self.dcnt[sk]
            self.streams[E].append(lambda eng, sem=sem, val=val: eng.wait_ge(sem, val))

    def barrier(self):
        for E in self.ENG:
            deps = {}
            for s in ("pe", "act", "dve", "pool"):
                if self.cnt[s] > 0:
                    deps[s] = self.cnt[s]
            for sk, v in self.dcnt.items():
                if v > 0:
                    deps[("d", sk)] = v
            for src, val in deps.items():
                if src == E:
                    continue
                if self.known[E].get(src, 0) >= val:
                    continue
                self.known[E][src] = val
                sem = self._src_sem(src)
                self.streams[E].append(lambda eng, sem=sem, val=val: eng.wait_ge(sem, val))

    def finalize(self):
        nc = self.nc
        with nc.Block() as block:
            for name in self.ENG:
                stream = self.streams[name]

                def body(eng, stream=stream):
                    for f in stream:
                        f(eng)

                getattr(block, self.ATTR[name])(body)


class Ctx:
    pass


def emit_rsqrt(P, c):
    P.op("act", lambda e: e.activation(out=c.rstd[:, :], in_=c.rstd[:, :], func=AF.Sqrt), reads=["rstd"], writes=["rstd"])
    P.op("dve", lambda e: e.reciprocal(out=c.rstd[:, :], in_=c.rstd[:, :]), reads=["rstd"], writes=["rstd"])


def emit_norm_T(P, c, src_key_fn, src_ap_fn, gain_ap, tag):
    nc = P.nc
    P.op("dve", lambda e: e.memset(c.ssq[:, :], 0.0), writes=["ssq"])
    for t in range(NT):
        P.op("act", lambda e, t=t: e.activation(out=c.junk[:, :], in_=src_ap_fn(t), func=AF.Square,
                                                accum_out=c.ssq[:, t:t + 1]),
             reads=[src_key_fn(t)], writes=["junk", "ssq"])
    P.op("dve", lambda e: e.tensor_scalar(out=c.rstd[:, :], in0=c.ssq[:, :], scalar1=1.0 / D, scalar2=EPS,
                                          op0=ALU.mult, op1=ALU.add), reads=["ssq"], writes=["rstd"])
    emit_rsqrt(P, c)
    for t in range(NT):
        ub = t % 2
        P.op("dve", lambda e, t=t, ub=ub: e.scalar_tensor_tensor(
            out=c.un[:, ub, :], in0=src_ap_fn(t), scalar=c.rstd[:, t:t + 1], in1=gain_ap,
            op0=ALU.mult, op1=ALU.mult), reads=[src_key_fn(t), "rstd", "consts"], writes=[("un", ub)])

        def tr(e, t=t, ub=ub):
            last = None
            for kc in range(8):
                last = e.transpose(out=c.psT[ub][:, kc * 128:(kc + 1) * 128], in_=c.un[:, ub, kc * 128:(kc + 1) * 128],
                                   identity=c.ident[:, :])
            return last

        P.op("pe", tr, reads=[("un", ub), "consts"], writes=[("psT", ub)])
        P.op("act", lambda e, t=t, ub=ub: e.copy(
            out=c.uT[:, :, t * 128:(t + 1) * 128], in_=c.psT[ub][:, :].rearrange("p (k n) -> p k n", k=8)),
            reads=[("psT", ub)], writes=[("uT", t // 4)])


def emit_ffn(P, c, layer, win_d, wd_d):
    groups = [list(range(0, 5)), list(range(5, 10)), list(range(10, 14)), list(range(14, 18)), list(range(18, 22))]
    def load_win(j):
        s = c.win_ctr % 3
        c.win_ctr += 1
        P.dma("pool", lambda e, j=j, s=s: e.dma_start(out=c.win[:, s, :, :], in_=win_d[layer, j]),
              ("win", s), writes=[("win", s)])
        return s

    def load_wd(g):
        s = c.wd_ctr % 2
        c.wd_ctr += 1
        js = groups[g]
        P.dma("pool", lambda e, js=js, s=s: e.dma_start(
            out=c.wd[:, s, 0:len(js), :], in_=wd_d[layer, js[0] * 128:(js[-1] + 1) * 128, :].rearrange("(j p) n -> p j n", p=128)),
            ("wd", s), writes=[("wd", s)])
        return s

    flat = [j for g in groups for j in g]
    win_slots = {}
    for j in flat[:2]:
        win_slots[j] = load_win(j)
    wd_slot = {0: load_wd(0)}
    idx = 0
    for g, js in enumerate(groups):
        if g + 1 < len(groups):
            wd_slot[g + 1] = load_wd(g + 1)
        for jl, j in enumerate(js):
            if idx + 2 < len(flat):
                win_slots[flat[idx + 2]] = load_win(flat[idx + 2])
            s = win_slots[j]
            for tb in range(4):
                pb = c.gu_ctr % 2
                c.gu_ctr += 1

                def mm(e, s=s, tb=tb, pb=pb):
                    last = None
                    for half, ps in ((0, c.psG[pb]), (1, c.psU[pb])):
                        for kc in range(8):
                            last = e.matmul(ps[:, :], lhsT=c.win[:, s, kc, half * 128:(half + 1) * 128],
                                            rhs=c.uT[:, kc, tb * 512:(tb + 1) * 512], start=(kc == 0), stop=(kc == 7))
                    return last

                P.op("pe", mm, reads=[("win", s), ("uT", tb)], writes=[("psG", pb), ("psU", pb)])
                P.op("act", lambda e, pb=pb: e.activation(out=c.sg[:, pb, :], in_=c.psG[pb][:, :], func=AF.Silu),
                     reads=[("psG", pb)], writes=[("sg", pb)])
                P.op("dve", lambda e, pb=pb, jl=jl, tb=tb: e.tensor_tensor(
                    out=c.aT[:, jl, tb * 512:(tb + 1) * 512], in0=c.sg[:, pb, :], in1=c.psU[pb][:, :], op=ALU.mult),
                    reads=[("sg", pb), ("psU", pb)], writes=[("aT", jl, tb)])
            idx += 1
        ws = wd_slot[g]
        for t in range(NT):
            for hf in range(2):
                pb = c.d_ctr % 2
                c.d_ctr += 1

                def mmd(e, t=t, hf=hf, pb=pb, ws=ws, n=len(js)):
                    last = None
                    for jl in range(n):
                        last = e.matmul(c.psD[pb][:, :], lhsT=c.aT[:, jl, t * 128:(t + 1) * 128],
                                        rhs=c.wd[:, ws, jl, hf * 512:(hf + 1) * 512], start=(jl == 0), stop=(jl == n - 1))
                    return last

                P.op("pe", mmd, reads=[("wd", ws)] + [("aT", jl, t // 4) for jl in range(len(js))], writes=[("psD", pb)])
                P.op("dve", lambda e, t=t, hf=hf, pb=pb: e.tensor_tensor(
                    out=c.h[:, t, hf * 512:(hf + 1) * 512], in0=c.h[:, t, hf * 512:(hf + 1) * 512], in1=c.psD[pb][:, :],
                    op=ALU.add), reads=[("psD", pb), ("h", t)], writes=[("h", t)])


NPREV = 48
HQ, HF, HI, HG = 0, 1024, 2048, 3072
NEG = -30000.0
DIL = (1, 4, 16)


def hgrn_F_stages(P, c, xs_ap, xs_key, full, z):
    sl2 = [slice(0, 512), slice(512, 1024)]

    def nb():
        b = c.pp_ctr % 2
        c.pp_ctr += 1
        return b

    def proj(col0):
        b = nb()

        def mm(e):
            last = None
            for kc in range(8):
                last = e.matmul(c.bank[b][:, :], lhsT=c.uTt[:, kc, :], rhs=c.W[:, kc, col0:col0 + 512], start=(kc == 0), stop=(kc == 7))
            return last

        P.op("pe", mm, reads=["uTt", "W"], writes=[("bank", b)])
        return b

    def F1():
        P.op("dve", lambda e: e.memset(c.ssq1[:, :], 0.0), writes=["ssq1"])
        P.op("act", lambda e: e.activation(out=c.un[:, 1, :], in_=xs_ap, func=AF.Square, accum_out=c.ssq1[:, 0:1]),
             reads=[xs_key], writes=[("un", 1), "ssq1"])
        P.op("dve", lambda e: e.tensor_scalar(out=c.rstd1[:, :], in0=c.ssq1[:, :], scalar1=1.0 / D, scalar2=EPS,
                                              op0=ALU.mult, op1=ALU.add), reads=["ssq1"], writes=["rstd1"])
        P.op("act", lambda e: e.activation(out=c.rstd1[:, :], in_=c.rstd1[:, :], func=AF.Ln), reads=["rstd1"], writes=["rstd1"])
        P.op("act", lambda e: e.activation(out=c.rstd1[:, :], in_=c.rstd1[:, :], func=AF.Exp, scale=-0.5), reads=["rstd1"], writes=["rstd1"])
        P.op("pool", lambda e: e.tensor_tensor(out=c.un[:, 0, :], in0=xs_ap, in1=c.gcur[:, :], op=ALU.mult),
             reads=[xs_key, "gcur"], writes=[("un", 0)])

    def F1b():
        def tr_un(e):
            last = None
            for kc in range(8):
                last = e.transpose(out=c.psTb[:, kc * 128:(kc + 1) * 128], in_=c.un[:, 0, kc * 128:(kc + 1) * 128], identity=c.ident[:, :])
            return last

        P.group(lambda: (
            P.op("pe", tr_un, reads=[("un", 0), "consts"], writes=[("bank", 7)]),
            P.op("act", lambda e: e.copy(out=c.uTt[:, :, :], in_=c.psTb[:, :].rearrange("p (k n) -> p k n", k=8)),
                 reads=[("bank", 7)], writes=["uTt"])))

    def F2():
        for hf in range(2):
            sl = sl2[hf]
            b = proj(HF + hf * 512)
            P.op("act", lambda e, b=b, sl=sl: e.activation(out=c.tt[:, sl], in_=c.bank[b][:, :], func=AF.Sigmoid, scale=c.rstd1[:, 0:1]),
                 reads=[("bank", b), "rstd1"], writes=[("tt", hf)])
            P.op("dve", lambda e, sl=sl: e.scalar_tensor_tensor(out=c.tt[:, sl], in0=c.tt[:, sl], scalar=-1.0, in1=c.oml[:, sl],
                                                                op0=ALU.add, op1=ALU.mult),
                 reads=[("tt", hf), "lbc"], writes=[("tt", hf)])

    def F3():
        for hf in range(2):
            sl = sl2[hf]
            b = proj(HI + hf * 512)
            P.op("dve", lambda e, b=b, sl=sl: e.tensor_scalar(out=c.v[z][:, sl], in0=c.bank[b][:, :], scalar1=c.rstd1[:, 0:1], scalar2=0.0,
                                                              op0=ALU.mult, op1=ALU.add), reads=[("bank", b), "rstd1"], writes=[("v", z, hf)])
        if full:
            for hf in range(2):
                sl = sl2[hf]
                b = proj(HQ + hf * 512)
                P.op("act", lambda e, b=b, sl=sl: e.activation(out=c.sq[:, sl], in_=c.bank[b][:, :], func=AF.Silu, scale=c.rstd1[:, 0:1]),
                     reads=[("bank", b), "rstd1"], writes=[("sq", hf)])
            for hf in range(2):
                sl = sl2[hf]
                b = proj(HG + hf * 512)
                P.op("act", lambda e, b=b, sl=sl: e.activation(out=c.gs[z][:, sl], in_=c.bank[b][:, :], func=AF.Silu, scale=c.rstd1[:, 0:1]),
                     reads=[("bank", b), "rstd1"], writes=[("gs", z, hf)])
        for hf in range(2):
            sl = sl2[hf]
            P.op("act", lambda e, sl=sl: e.activation(out=c.logf[:, sl], in_=c.tt[:, sl], func=AF.Ln, bias=c.one1[:, 0:1]),
                 reads=[("tt", hf), "consts"], writes=[("logf", hf)])

    def F4():
        if full:
            for hf in range(2):
                sl = sl2[hf]
                b = nb()
                P.op("pe", lambda e, b=b, sl=sl: e.matmul(c.bank[b][:, :], lhsT=c.prefT[:, :], rhs=c.logf[:, sl], start=True, stop=True),
                     reads=[("logf", hf), "consts"], writes=[("bank", b)])
                P.op("act", lambda e, b=b, sl=sl: e.activation(out=c.e1[:, sl], in_=c.bank[b][:, :], func=AF.Exp),
                     reads=[("bank", b)], writes=[("e1", hf)])
                P.op("dve", lambda e, sl=sl: e.scalar_tensor_tensor(out=c.qeA[:, sl], in0=c.sq[:, sl], scalar=c.ind2[:, 0:1], in1=c.e1[:, sl],
                                                                    op0=ALU.mult, op1=ALU.mult),
                     reads=[("sq", hf), ("e1", hf), "consts"], writes=[("qeA", hf)])
                P.op("dve", lambda e, sl=sl: e.scalar_tensor_tensor(out=c.qeB[:, sl], in0=c.sq[:, sl], scalar=c.ind2[:, 1:2], in1=c.e1[:, sl],
                                                                    op0=ALU.mult, op1=ALU.mult),
                     reads=[("sq", hf), ("e1", hf), "consts"], writes=[("qeB", hf)])
                P.op("act", lambda e, b=b, sl=sl: e.activation(out=c.e1[:, sl], in_=c.bank[b][:, :], func=AF.Exp, scale=-1.0),
                     reads=[("bank", b)], writes=[("e1", hf)])
                P.op("pool", lambda e, sl=sl: e.tensor_tensor(out=c.ke[:, sl], in0=c.tt[:, sl], in1=c.e1[:, sl], op=ALU.mult),
                     reads=[("tt", hf), ("e1", hf)], writes=[("ke", hf)])
        for hf in range(2):
            sl = sl2[hf]
            b = nb()
            P.op("pe", lambda e, b=b, sl=sl: e.matmul(c.bank[b][:, :], lhsT=c.sufT[:, :], rhs=c.logf[:, sl], start=True, stop=True),
                 reads=[("logf", hf), "consts"], writes=[("bank", b)])
            P.op("act", lambda e, b=b, sl=sl: e.activation(out=c.e1[:, sl], in_=c.bank[b][:, :], func=AF.Exp),
                 reads=[("bank", b)], writes=[("e1", hf)])
            P.op("pool", lambda e, sl=sl: e.tensor_tensor(out=c.kend[z][:, sl], in0=c.tt[:, sl], in1=c.e1[:, sl], op=ALU.mult),
                 reads=[("tt", hf), ("e1", hf)], writes=[("kend", z, hf)])

    def F5():
        bb = nb()

        def mm_bl(e):
            last = None
            for h in range(8):
                last = e.matmul(c.bank[bb][:, 2 * h:2 + 2 * h], lhsT=c.logf[:, h * 128:(h + 1) * 128], rhs=c.ind2[:, :],
                                start=True, stop=True)
            return last

        P.op("pe", mm_bl, reads=[("logf", 0), ("logf", 1), "consts"], writes=[("bank", bb)])
        P.op("act", lambda e: e.activation(out=c.ebl[z][:, :], in_=c.bank[bb][:, 0:16], func=AF.Exp), reads=[("bank", bb)], writes=[("ebl", z)])
        if full:
            for nm, src, dst in (("qeA", c.qeA, c.qeTA[z]), ("qeB", c.qeB, c.qeTB[z]), ("ke", c.ke, c.keT[z])):
                def tr_q(e, src=src):
                    last = None
                    for h in range(8):
                        last = e.transpose(out=c.psTb[:, h * 128:(h + 1) * 128], in_=src[:, h * 128:(h + 1) * 128], identity=c.ident[:, :])
                    return last

                P.group(lambda tr_q=tr_q, nm=nm, dst=dst: (
                    P.op("pe", tr_q, reads=[(nm, 0), (nm, 1), "consts"], writes=[("bank", 7)]),
                    P.op("act", lambda e, dst=dst: e.copy(out=dst[:, :, :], in_=c.psTb[:, :].rearrange("p (h n) -> p h n", h=8)),
                         reads=[("bank", 7)], writes=[(nm + "T", z)])))

    return [F1, F1b, F2, F3, F4, F5]


def hgrn_B_stages(P, c, full, z, par, ti):
    sl2 = [slice(0, 512), slice(512, 1024)]

    def kv_group(g4):
        def mmA(e):
            last = None
            for hl in range(4):
                hs = slice((g4 * 4 + hl) * 128, (g4 * 4 + hl + 1) * 128)
                last = e.matmul(c.bank[2][:, hl * 128:(hl + 1) * 128], lhsT=c.kend[z][0:64, hs], rhs=c.v[z][0:64, hs], start=True, stop=True)
            return last

        def mmB(e):
            last = None
            for hl in range(4):
                hs = slice((g4 * 4 + hl) * 128, (g4 * 4 + hl + 1) * 128)
                last = e.matmul(c.bank[6][:, hl * 128:(hl + 1) * 128], lhsT=c.kend[z][64:128, hs], rhs=c.v[z][64:128, hs], start=True, stop=True)
            return last

        rk = [("kend", z, g4), ("v", z, g4)]
        P.op("pe", mmA, reads=rk, writes=[("bank", 2)])
        P.op("pe", mmB, reads=rk, writes=[("bank", 6)])
        for hl in range(4):
            h = g4 * 4 + hl
            cs = slice(hl * 128, (hl + 1) * 128)
            P.op("dve", lambda e, h=h, cs=cs: e.scalar_tensor_tensor(
                out=c.S2[:, h, :], in0=c.S[:, h, :], scalar=c.ebl[z][:, 2 * h:2 * h + 1], in1=c.bank[2][:, cs],
                op0=ALU.mult, op1=ALU.subtract), reads=[("S", h), ("ebl", z), ("bank", 2)], writes=[("S2", h)])
            if full:
                P.op("pool", lambda e, h=h: e.tensor_copy(out=c.Sb[:, h, 2, :], in_=c.S2[:, h, :]), reads=[("S2", h)], writes=[("Sb", h, 2)])
            P.op("dve", lambda e, h=h, cs=cs: e.scalar_tensor_tensor(
                out=c.S[:, h, :], in0=c.S2[:, h, :], scalar=c.ebl[z][:, 2 * h + 1:2 * h + 2], in1=c.bank[6][:, cs],
                op0=ALU.mult, op1=ALU.subtract), reads=[("S2", h), ("ebl", z), ("bank", 6)], writes=[("S", h)])
            P.op("pool", lambda e, h=h: e.tensor_copy(out=c.Sb[:, h, 1 - par, :], in_=c.S[:, h, :]), reads=[("S", h)], writes=[("Sb", h, 1 - par)])

    def sc_group(g4):
        def mm_sc(e):
            last = None
            for hl in range(4):
                h = g4 * 4 + hl
                e.matmul(c.bank[3][:, hl * 128:(hl + 1) * 128], lhsT=c.keT[z][:, h, :], rhs=c.qeTA[z][:, h, :], start=True, stop=False)
                last = e.matmul(c.bank[3][:, hl * 128:(hl + 1) * 128], lhsT=c.keT[z][:, h, :], rhs=c.qeTB[z][:, h, :], start=False, stop=True)
            return last

        P.op("pe", mm_sc, reads=[("keT", z), ("qeAT", z), ("qeBT", z)], writes=[("bank", 3)])
        P.op("dve", lambda e: e.tensor_tensor(out=c.scm[:, :], in0=c.bank[3][:, :], in1=c.nmask4[:, :], op=ALU.mult),
             reads=[("bank", 3), "consts"], writes=["scm"])

    def o_group(g4):
        ob = 4 + g4

        def mm_o(e):
            last = None
            for hl in range(4):
                h = g4 * 4 + hl
                hs = slice(h * 128, (h + 1) * 128)
                oc = slice(hl * 128, (hl + 1) * 128)
                e.matmul(c.bank[ob][:, oc], lhsT=c.scm[:, hl * 128:(hl + 1) * 128], rhs=c.v[z][:, hs], start=True, stop=False)
                e.matmul(c.bank[ob][:, oc], lhsT=c.qeTA[z][:, h, :], rhs=c.Sb[:, h, par, :], start=False, stop=False)
                last = e.matmul(c.bank[ob][:, oc], lhsT=c.qeTB[z][:, h, :], rhs=c.Sb[:, h, 2, :], start=False, stop=True)
            return last

        P.op("pe", mm_o, reads=["scm", ("v", z, g4), ("qeAT", z), ("qeBT", z)] + [("Sb", g4 * 4 + hl, s) for hl in range(4) for s in (par, 2)],
             writes=[("bank", ob)])

    def post():
        P.op("dve", lambda e: e.memset(c.ssqo[:, :], 0.0), writes=["ssqo"])
        for h in range(8):
            ob = 4 + h // 4
            oc = slice((h % 4) * 128, (h % 4 + 1) * 128)
            P.op("act", lambda e, h=h, ob=ob, oc=oc: e.activation(out=c.un[:, 1, 0:128], in_=c.bank[ob][:, oc], func=AF.Square,
                                                                  accum_out=c.ssqo[:, h:h + 1]),
                 reads=[("bank", ob)], writes=[("un", 1), "ssqo"])
        P.op("dve", lambda e: e.tensor_scalar(out=c.rstdo[:, :], in0=c.ssqo[:, :], scalar1=1.0 / 128, scalar2=EPS,
                                              op0=ALU.mult, op1=ALU.add), reads=["ssqo"], writes=["rstdo"])
        P.op("act", lambda e: e.activation(out=c.rstdo[:, :], in_=c.rstdo[:, :], func=AF.Ln), reads=["rstdo"], writes=["rstdo"])
        P.op("act", lambda e: e.activation(out=c.rstdo[:, :], in_=c.rstdo[:, :], func=AF.Exp, scale=-0.5), reads=["rstdo"], writes=["rstdo"])
        for hf in range(2):
            sl = sl2[hf]
            P.op("dve", lambda e, hf=hf, sl=sl: e.tensor_tensor(out=c.og1[:, sl], in0=c.bank[4 + hf][:, :], in1=c.gs[z][:, sl], op=ALU.mult),
                 reads=[("bank", 4 + hf), ("gs", z, hf)] + [("S2", 4 * hf + i) for i in range(4)],
                 writes=[("og1", hf)] + [("S2", 4 * hf + i) for i in range(4)])
        for h in range(8):
            hs = slice(h * 128, (h + 1) * 128)
            P.op("dve", lambda e, h=h, hs=hs: e.scalar_tensor_tensor(
                out=c.og2[:, hs], in0=c.og1[:, hs], scalar=c.rstdo[:, h:h + 1], in1=c.gout[:, hs], op0=ALU.mult, op1=ALU.mult),
                reads=[("og1", h // 4), ("S2", h), "rstdo", "lbc"], writes=[("og2", h // 4)])

    def post_b():
        def tr_o(e):
            last = None
            for h in range(8):
                last = e.transpose(out=c.psTb[:, h * 128:(h + 1) * 128], in_=c.og2[:, h * 128:(h + 1) * 128], identity=c.ident[:, :])
            return last

        P.group(lambda: (
            P.op("pe", tr_o, reads=[("og2", 0), ("og2", 1), "consts"], writes=[("bank", 7)]),
            P.op("act", lambda e: e.copy(out=c.uT[:, :, ti * 128:(ti + 1) * 128], in_=c.psTb[:, :].rearrange("p (k n) -> p k n", k=8)),
                 reads=[("bank", 7)], writes=[("uT", ti // 4)])))

    if not full:
        return [lambda: kv_group(0), lambda: None, lambda: kv_group(1)]

    def B1():
        kv_group(0)

    def B2():
        sc_group(0)
        kv_group(1)

    def B3():
        o_group(0)
        sc_group(1)

    def B4():
        o_group(1)
        post()

    return [B1, lambda: None, B2, B3, B4, post_b]


def hgrn_run_tiles(P, c, tiles):
    def load_x(k):
        src, i, _ = tiles[k]
        s = k % 2
        P.dma("sp", lambda e, src=src, i=i, s=s: e.dma_start(out=c.xs[:, s, :], in_=src[:, i, :]), ("xs", s), writes=[("xs", s)])

    n = len(tiles)
    load_x(0)
    Fst = hgrn_F_stages(P, c, c.xs[:, 0, :], ("xs", 0), tiles[0][2], 0)
    for f in Fst:
        f()
    nfull = 0
    for k in range(n):
        if k + 1 < n:
            load_x(k + 1)
            Fn = hgrn_F_stages(P, c, c.xs[:, (k + 1) % 2, :], ("xs", (k + 1) % 2), tiles[k + 1][2], (k + 1) % 2)
        else:
            Fn = []
        Bk = hgrn_B_stages(P, c, tiles[k][2], k % 2, c.hpar, nfull)
        c.hpar = 1 - c.hpar
        nfull += 1 if tiles[k][2] else 0
        la = P.record(lambda: [f() for f in Fn]) if Fn else []
        lb = P.record(lambda: [b() for b in Bk])
        if la:
            P.replay_merged(la, lb)
        else:
            for kind, args in lb:
                getattr(P, kind)(*args)


NHALO = 4 * (1 + 4 + 16)
OROW = 132


def build_G(dbg=None):
    nc = bass.Bass("TRN2", target_bir_lowering=False)
    dr = lambda name, shape, kind="ExternalInput": nc.dram_tensor(name, list(shape), F32, kind=kind).ap()
    x_d = dr("x", [T, D])
    xp_d = dr("x_prev", [NPREV * 128, D])
    gains_d = dr("gains", [128, 5, D])
    cst_d = dr("cst", [128, 5, 128])
    lbl_d = dr("lb_logits", [128, 3, D])
    gout_d = dr("gout", [128, D])
    hw_in_d = dr("hgrn_win", [128, 8, 4096])
    hw_out_d = dr("hgrn_wout", [128, 8, D])
    win_d = dr("ffn_win", [2, NJ, 128, 8, 256])
    wd_d = dr("ffn_wd", [2, DFF, D])
    wqkv_d = dr("wqkv", [9, 128, 8, 512])
    rope_d = dr("rope", [2, 128, 2, NT, 256])
    wo_d = dr("attn_wout", [128, 12, D])
    cstb_d = dr("cstb", [128, 5, 128])
    out_d = dr("out", [T, D], kind="ExternalOutput")
    qkv_scr = nc.dram_tensor("qkv_scr", [2 * T * 12, 384], BF16, kind="Internal").ap()
    o_loc = nc.dram_tensor("o_loc", [T, 12 * OROW], F32, kind="Internal").ap()
    s_scr = nc.dram_tensor("s_scr", [128, 1024], F32, kind="Internal").ap()

    with ExitStack() as es:
        sb = lambda name, shape, dt=F32: es.enter_context(nc.sbuf_tensor("sb_" + name, list(shape), dt))
        ps = lambda name, shape, dt=F32: es.enter_context(nc.psum_tensor("ps_" + name, list(shape), dt))
        c = Ctx()
        arA = sb("arA", [128, 16384])
        arC = sb("arC", [128, 16960])
        c.h = arA[:, :].rearrange("p (t d) -> p t d", t=NT)
        c.W = arA[:, :].bitcast(BF16).rearrange("p (k n) -> p k n", k=8)
        c.uT = sb("uT", [128, 8, T], BF16)
        arD = sb("arD", [128, 8192])
        c.oml = arD[:, 0:1024]
        c.gout = arD[:, 1024:2048]
        c.xs = arD[:, 2048:4096].rearrange("p (s n) -> p s n", s=2)
        c.wout = arD[:, 4096:8192].bitcast(BF16).rearrange("p (k n) -> p k n", k=8)
        c.wo = arD[:, 0:6144].bitcast(BF16).rearrange("p (k n) -> p k n", k=12)
        c.gcur = sb("gcur", [128, D])
        c.un = sb("un", [128, 2, D], BF16)
        c.cstf = sb("cstf", [128, 5, 128])
        c.ident = sb("identb", [128, 128], BF16)
        c.small = sb("small", [128, 64])
        c.ssq = c.small[:, 0:16]
        c.rstd = c.small[:, 16:32]
        c.ssq1 = c.small[:, 32:33]
        c.rstd1 = c.small[:, 33:34]
        c.ssqo = c.small[:, 40:48]
        c.rstdo = c.small[:, 48:56]
        c.prefT = c.cstf[:, 1, :]
        c.sufT = c.cstf[:, 2, :]
        c.ind2 = c.cstf[:, 4, 0:2]
        c.one1 = c.small[:, 56:57]

        def cv(lo, n, dt=F32):
            a = arC[:, lo:lo + n]
            return a if dt == F32 else a.bitcast(dt)

        c.aT = cv(0, 5120, BF16).rearrange("p (j n) -> p j n", j=5)
        c.win = cv(5120, 3072, BF16).rearrange("p (s k n) -> p s k n", s=3, k=8)
        c.wd = cv(8192, 5120, BF16).rearrange("p (s j n) -> p s j n", s=2, j=5)
        c.sg = cv(13312, 1024).rearrange("p (s n) -> p s n", s=2)
        c.junk = cv(13312, 512, BF16)
        c.tt = cv(0, 1024); c.logf = cv(1024, 1024); c.e1 = cv(2048, 1024); c.sq = cv(3072, 1024)
        c.qeA = cv(4096, 512, BF16); c.qeB = cv(4608, 512, BF16); c.ke = cv(5120, 512, BF16); c.og2 = cv(16448, 512, BF16)
        c.uTt = cv(5632, 512, BF16).rearrange("p (k n) -> p k n", k=8)
        c.v, c.kend, c.qeTA, c.qeTB, c.keT, c.gs, c.ebl = [], [], [], [], [], [], []
        for z in range(2):
            zb = 6144 + z * 3104
            c.v.append(cv(zb, 512, BF16)); c.kend.append(cv(zb + 512, 512, BF16))
            c.qeTA.append(cv(zb + 1024, 512, BF16).rearrange("p (h n) -> p h n", h=8))
            c.qeTB.append(cv(zb + 1536, 512, BF16).rearrange("p (h n) -> p h n", h=8))
            c.keT.append(cv(zb + 2048, 512, BF16).rearrange("p (h n) -> p h n", h=8))
            c.gs.append(cv(zb + 2560, 512, BF16)); c.ebl.append(cv(zb + 3072, 16))
        c.Sflat = cv(12352, 1024)
        c.S = c.Sflat.rearrange("p (h n) -> p h n", h=8)
        c.og1 = cv(13376, 1024)
        c.S2 = c.og1.rearrange("p (h n) -> p h n", h=8)
        c.Sb = cv(14400, 1536, BF16).rearrange("p (h a n) -> p h a n", h=8, a=3)
        c.scm = cv(15936, 256, BF16)
        c.nmask4 = cv(16192, 256, BF16)
        c.hpar = 0
        c.rope = cv(0, 8192).rearrange("p (a t n) -> p a t n", a=2, t=NT)
        c.xsb = cv(8192, 1024).rearrange("p (s n) -> p s n", s=2)
        c.ost = cv(9216, 512, BF16).rearrange("p (s n) -> p s n", s=2)
        c.rt = cv(10240, 1024).rearrange("p (s n) -> p s n", s=4)
        c.rtp = [c.rt, cv(11264, 1024).rearrange("p (s n) -> p s n", s=4)]
        c.wq = c.wout[:, :, :].rearrange("p k n -> p (k n)").rearrange("p (s k n) -> p s k n", s=2, k=8)

        c.bank = [ps("b%d" % i, [128, 512]) for i in range(8)]
        c.psTb = c.bank[7][:, :].bitcast(BF16)
        c.psG = [c.bank[0], c.bank[1]]
        c.psU = [c.bank[2], c.bank[3]]
        c.psD = [c.bank[4], c.bank[5]]
        c.psT = [c.bank[6][:, :].bitcast(BF16), c.bank[7][:, :].bitcast(BF16)]
        c.win_ctr = c.wd_ctr = c.gu_ctr = c.d_ctr = 0
        c.pp_ctr = c.sc_ctr = c.kv_ctr = 0

        P = Prog(nc, es)
        P.dma("sp", lambda e: e.dma_start(out=c.cstf[:, :, :], in_=cst_d), "cst1", writes=["cstf"])
        P.dma("sp", lambda e: e.dma_start(out=c.gcur[:, :], in_=gains_d[:, 0, :]), "cst2", writes=["gcur"])
        P.dma("sp", lambda e: e.dma_start(out=c.gout[:, :], in_=gout_d), "cst3", writes=["gout_l"])
        P.op("dve", lambda e: e.tensor_copy(out=c.ident[:, :], in_=c.cstf[:, 0, :]), reads=["cstf"], writes=["ident_"])
        for j in range(4):
            P.op("dve", lambda e, j=j: e.tensor_scalar(out=c.nmask4[:, j * 128:(j + 1) * 128], in0=c.cstf[:, 3, :], scalar1=-1.0, scalar2=0.0,
                                                       op0=ALU.mult, op1=ALU.add), reads=["cstf", "ident_"], writes=["nm4"])
        P.op("dve", lambda e: e.memset(c.one1, 1.0), reads=["nm4"], writes=["consts"])
        for q in range(4):
            P.dma("pool", lambda e, q=q: e.dma_start(out=c.W[:, 2 * q:2 * q + 2, :], in_=hw_in_d[:, 2 * q:2 * q + 2, :]),
                  "W", writes=["W"])
        P.dma("pool", lambda e: e.dma_start(out=c.wout[:, :, :], in_=hw_out_d), "wout", writes=["wout"])
        lg = arC[:, 4096:7168].rearrange("p (a n) -> p a n", a=3)
        P.dma("sp", lambda e: e.dma_start(out=lg, in_=lbl_d), "cst4", writes=["lg"])
        P.op("dve", lambda e: e.tensor_tensor(out=c.e1[:, :], in0=lg[:, 0, :], in1=lg[:, 1, :], op=ALU.max), reads=["lg"], writes=["lmx"])
        P.op("dve", lambda e: e.tensor_tensor(out=c.e1[:, :], in0=c.e1[:, :], in1=lg[:, 2, :], op=ALU.max), reads=["lg", "lmx"], writes=["lmx"])
        for a in range(3):
            P.op("dve", lambda e, a=a: e.tensor_tensor(out=lg[:, a, :], in0=lg[:, a, :], in1=c.e1[:, :], op=ALU.subtract),
                 reads=["lg", "lmx"], writes=["lg"])
        P.op("act", lambda e: e.activation(out=lg, in_=lg, func=AF.Exp), reads=["lg"], writes=["lg"])
        P.op("dve", lambda e: e.tensor_tensor(out=c.e1[:, :], in0=lg[:, 0, :], in1=lg[:, 1, :], op=ALU.add), reads=["lg"], writes=["lmx"])
        P.op("dve", lambda e: e.tensor_tensor(out=c.e1[:, :], in0=c.e1[:, :], in1=lg[:, 2, :], op=ALU.add), reads=["lg", "lmx"], writes=["lmx"])
        P.op("dve", lambda e: e.reciprocal(out=c.e1[:, :], in_=c.e1[:, :]), reads=["lmx"], writes=["lmx"])
        P.op("dve", lambda e: e.tensor_tensor(out=c.sq[:, :], in0=lg[:, 0, :], in1=c.e1[:, :], op=ALU.mult), reads=["lg", "lmx"], writes=["lb_"])
        P.op("dve", lambda e: e.tensor_scalar(out=c.oml[:, :], in0=c.sq[:, :], scalar1=-1.0, scalar2=1.0, op0=ALU.mult, op1=ALU.add),
             reads=["lb_", "gout_l"], writes=["lbc"])
        P.op("dve", lambda e: e.memset(c.Sflat, 0.0), reads=["lbc"], writes=[("S", h) for h in range(8)])
        P.op("dve", lambda e: e.memset(c.Sb[:, :, :, :], 0.0), writes=[("Sb", h, a) for h in range(8) for a in range(3)])
        P.barrier()

        xpv = xp_d.rearrange("(t p) d -> p t d", p=128)
        xv = x_d.rearrange("(t p) d -> p t d", p=128)
        qv = qkv_scr.rearrange("(t p h) (x n) -> p t h x n", p=128, h=12, x=3)
        Sflat = c.Sflat

        def layer0_pass(pz, tiles, xr_view, xr_base, cbs):
            hgrn_run_tiles(P, c, tiles)
            P.barrier()
            P.dma("sp", lambda e: e.dma_start(out=s_scr, in_=Sflat), "ssave", reads=[("S", h) for h in range(8)])
            def load_x2(t):
                s = t % 2
                P.dma("sp", lambda e, t=t, s=s: e.dma_start(out=c.xs[:, s, :], in_=xr_view[:, xr_base + t, :]), ("xs", s), writes=[("xs", s)])

            load_x2(0)
            for t in range(NT):
                if t + 1 < NT:
                    load_x2(t + 1)
                for hf in range(2):
                    b = c.pp_ctr % 2
                    c.pp_ctr += 1

                    def mmo(e, t=t, hf=hf, b=b):
                        last = None
                        for kc in range(8):
                            last = e.matmul(c.bank[b][:, :], lhsT=c.uT[:, kc, t * 128:(t + 1) * 128], rhs=c.wout[:, kc, hf * 512:(hf + 1) * 512],
                                            start=(kc == 0), stop=(kc == 7))
                        return last

                    P.op("pe", mmo, reads=[("uT", t // 4), "wout"], writes=[("bank", b)])
                    P.op("dve", lambda e, t=t, hf=hf, b=b: e.tensor_tensor(
                        out=c.h[:, t, hf * 512:(hf + 1) * 512], in0=c.xs[:, t % 2, hf * 512:(hf + 1) * 512], in1=c.bank[b][:, :], op=ALU.add),
                        reads=[("xs", t % 2), ("bank", b)], writes=[("h", t)])
            P.barrier()
            P.dma("sp", lambda e: e.dma_start(out=c.gcur[:, :], in_=gains_d[:, 1, :]), "g1_%d" % pz, writes=["consts", "gcur"])
            emit_norm_T(P, c, lambda t: ("h", t), lambda t: c.h[:, t, :], c.gcur[:, :], "f0")
            emit_ffn(P, c, 0, win_d, wd_d)
            P.barrier()
            P.dma("sp", lambda e: e.dma_start(out=c.gcur[:, :], in_=gains_d[:, 2, :]), "g2_%d" % pz, writes=["consts", "gcur"])
            P.dma("sp", lambda e: e.dma_start(out=c.rope[:, :, :, :], in_=rope_d[pz]), "rp_%d" % pz, writes=["rope"])
            emit_norm_T(P, c, lambda t: ("h", t), lambda t: c.h[:, t, :], c.gcur[:, :], "m1")

            def load_wq(j):
                s = j % 2
                P.dma("pool", lambda e, j=j, s=s: e.dma_start(out=c.wq[:, s, :, :], in_=wqkv_d[cbs[j]]), ("wq", s), writes=[("wq", s)])

            load_wq(0)
            cnt = 0
            for j, cb in enumerate(cbs):
                if j + 1 < len(cbs):
                    load_wq(j + 1)
                for t in range(NT):
                    if pz == 0 and t < NT - (1, 4, 16)[cb % 3]:
                        continue
                    b = c.pp_ctr % 2
                    c.pp_ctr += 1
                    s2 = cnt % 2
                    cnt += 1

                    def mmq(e, j=j, t=t, b=b):
                        last = None
                        for kc in range(8):
                            last = e.matmul(c.bank[b][:, :], lhsT=c.uT[:, kc, t * 128:(t + 1) * 128], rhs=c.wq[:, j % 2, kc, :],
                                            start=(kc == 0), stop=(kc == 7))
                        return last

                    P.op("pe", mmq, reads=[("uT", t // 4), ("wq", j % 2)], writes=[("bank", b)])
                    if cb < 6:
                        scale = (128.0 ** -0.5) if cb < 3 else 1.0
                        P.op("act", lambda e, b=b, s2=s2, scale=scale: e.activation(out=c.xsb[:, s2, :], in_=c.bank[b][:, :], func=AF.Copy, scale=scale),
                             reads=[("bank", b)], writes=[("xsb", s2)])
                        xh = c.xsb[:, s2, :].rearrange("p (h a n) -> p h a n", h=4, a=2)
                        oh = c.ost[:, s2, :].rearrange("p (h a n) -> p h a n", h=4, a=2)
                        cosv = c.rope[:, 0, t, :].rearrange("p (h n) -> p h n", h=4)
                        sinv = c.rope[:, 1, t, :].rearrange("p (h n) -> p h n", h=4)
                        r = [c.rtp[s2][:, i, 0:256].rearrange("p (h n) -> p h n", h=4) for i in range(4)]
                        rk = [("xsb", s2), "rope"]
                        P.op("dve", lambda e, xh=xh, cosv=cosv, r=r: e.tensor_tensor(out=r[0], in0=xh[:, :, 0, :], in1=cosv, op=ALU.mult), reads=rk, writes=[("rt0", s2)])
                        P.op("pool", lambda e, xh=xh, sinv=sinv, r=r: e.tensor_tensor(out=r[1], in0=xh[:, :, 1, :], in1=sinv, op=ALU.mult), reads=rk, writes=[("rt1", s2)])
                        P.op("dve", lambda e, xh=xh, cosv=cosv, r=r: e.tensor_tensor(out=r[2], in0=xh[:, :, 1, :], in1=cosv, op=ALU.mult), reads=rk, writes=[("rt2", s2)])
                        P.op("pool", lambda e, xh=xh, sinv=sinv, r=r: e.tensor_tensor(out=r[3], in0=xh[:, :, 0, :], in1=sinv, op=ALU.mult), reads=rk, writes=[("rt3", s2)])
                        P.op("dve", lambda e, oh=oh, r=r: e.tensor_tensor(out=oh[:, :, 0, :], in0=r[0], in1=r[1], op=ALU.subtract),
                             reads=[("rt0", s2), ("rt1", s2)], writes=[("ost", s2)])
                        P.op("pool", lambda e, oh=oh, r=r: e.tensor_tensor(out=oh[:, :, 1, :], in0=r[2], in1=r[3], op=ALU.add),
                             reads=[("rt2", s2), ("rt3", s2), ("ost", s2)], writes=[("ost", s2)])
                    else:
                        P.op("act", lambda e, b=b, s2=s2: e.copy(out=c.ost[:, s2, :], in_=c.bank[b][:, :]), reads=[("bank", b)], writes=[("ost", s2)])
                    P.dma("sp", lambda e, cb=cb, t=t, s2=s2: e.dma_start(
                        out=qv[:, pz * NT + t, 4 * (cb % 3):4 * (cb % 3) + 4, cb // 3, :], in_=c.ost[:, s2, :].rearrange("p (h n) -> p h n", h=4)),
                          ("qo", s2), reads=[("ost", s2)])
            P.barrier()

        if dbg is not None:
            npre_, nfull_ = dbg
            tiles = [(xpv, NPREV - npre_ + i, False) for i in range(npre_)] + [(xv, i, True) for i in range(nfull_)]
            hgrn_run_tiles(P, c, tiles)
            P.barrier()
            for t in range(nfull_):
                P.dma("sp", lambda e, t=t: e.dma_start(out=c.xs[:, t % 2, :], in_=xv[:, t, :]), ("xs", t % 2), writes=[("xs", t % 2)])
                for hf in range(2):
                    b = c.pp_ctr % 2
                    c.pp_ctr += 1

                    def mmo(e, t=t, hf=hf, b=b):
                        last = None
                        for kc in range(8):
                            last = e.matmul(c.bank[b][:, :], lhsT=c.uT[:, kc, t * 128:(t + 1) * 128], rhs=c.wout[:, kc, hf * 512:(hf + 1) * 512],
                                            start=(kc == 0), stop=(kc == 7))
                        return last

                    P.op("pe", mmo, reads=[("uT", t // 4), "wout"], writes=[("bank", b)])
                    P.op("dve", lambda e, t=t, hf=hf, b=b: e.tensor_tensor(
                        out=c.h[:, t, hf * 512:(hf + 1) * 512], in0=c.xs[:, t % 2, hf * 512:(hf + 1) * 512], in1=c.bank[b][:, :], op=ALU.add),
                        reads=[("xs", t % 2), ("bank", b)], writes=[("h", t)])
            ovd = out_d.rearrange("(t p) d -> p t d", p=128)
            for t in range(nfull_):
                P.dma("sp", lambda e, t=t: e.dma_start(out=ovd[:, t, :], in_=c.h[:, t, :]), ("o", t), reads=[("h", t)])
            P.wait_dma_done("sp", [("o", t) for t in range(nfull_)])
            P.finalize()
            return nc
        layer0_pass(0, [(xpv, i, False) for i in range(0, NPREV - NT)] + [(xpv, i, True) for i in range(NPREV - NT, NPREV)],
                    xpv, NPREV - NT, list(range(3, 9)))
        P.dma("sp", lambda e: e.dma_start(out=c.gcur[:, :], in_=gains_d[:, 0, :]), "g0_1", writes=["consts", "gcur"])
        for q in range(4):
            P.dma("pool", lambda e, q=q: e.dma_start(out=c.W[:, 2 * q:2 * q + 2, :], in_=hw_in_d[:, 2 * q:2 * q + 2, :]),
                  "W", writes=["W"])
        P.dma("pool", lambda e: e.dma_start(out=c.wout[:, :, :], in_=hw_out_d), "wout", writes=["wout"])
        P.dma("sp", lambda e: e.dma_start(out=Sflat, in_=s_scr), "srest", writes=[("S", h) for h in range(8)])
        for h in range(8):
            P.op("act", lambda e, h=h, pr=c.hpar: e.copy(out=c.Sb[:, h, pr, :], in_=c.S[:, h, :]), reads=[("S", h)], writes=[("Sb", h, c.hpar)])
        P.barrier()
        layer0_pass(1, [(xv, i, True) for i in range(NT)], xv, 0, list(range(9)))
        emit_B_g(P, c, arC, qkv_scr, o_loc, cstb_d)
        P.barrier()
        emit_C_g(P, c, arC, o_loc, gains_d, wo_d, win_d, wd_d, out_d)
        P.finalize()
    return nc


def emit_B_g(P, c, arC, qkv_all, o_loc, cstb_d):
    def cv(lo, n, dt=F32):
        a = arC[:, lo:lo + n]
        return a if dt == F32 else a.bitcast(dt)

    cstb = cv(0, 640).rearrange("p (a n) -> p a n", a=5)
    mask4 = cv(640, 1024).rearrange("p (v n) -> p v n", v=2)
    raw = cv(1664, 1920, BF16).rearrange("p (s h n) -> p s h n", s=5, h=2)
    qT = cv(3584, 256, BF16).rearrange("p (s n) -> p s n", s=2)
    kT = cv(3840, 640, BF16).rearrange("p (s n) -> p s n", s=5)
    sm = cv(4480, 1024).rearrange("p (s n) -> p s n", s=2)
    pp = cv(5504, 512, BF16).rearrange("p (s n) -> p s n", s=2)
    pT = cv(6016, 512, BF16).rearrange("p (s n) -> p s n", s=2)
    ost = cv(6528, 528).rearrange("p (g h n) -> p g h n", g=2, h=2)
    stt = cv(7056, 32).rearrange("p (s n) -> p s n", s=2)
    P.dma("sp", lambda e: e.dma_start(out=cstb, in_=cstb_d), "bi2", writes=["cstb"])
    for v in range(2):
        for hh in range(2):
            P.op("dve", lambda e, v=v, hh=hh: e.tensor_copy(
                out=mask4[:, v, hh * 256:(hh + 1) * 256], in_=cstb[:, 1 + 2 * v:3 + 2 * v, :].rearrange("p a n -> p (a n)")),
                reads=["cstb"], writes=["mask4"])
    P.op("dve", lambda e: e.memset(ost, 0.0), writes=[("ost", 0), ("ost", 1)])

    def unit_stages(it, u, blk, hslot):
        dd = DIL[u // 4]
        nbr = NT // dd
        r, n = blk // nbr, blk % nbr
        first = n == 0
        s2 = it % 2
        s3 = it % 3
        sp3 = hslot if first else (it - 1) % 3
        qsrc = qkv_all.rearrange("(a i d h) c -> d a i h c", i=128, d=dd, h=12)
        qb, tb, ob = 6 + s2, 2 + s2, 4 + s2
        psQ = c.bank[qb][:, :].bitcast(BF16)
        psT = c.bank[tb][:, :].bitcast(BF16)
        st = stt[:, s2, :]
        sk = [("st", s2)]

        def S1():
            if first:
                P.dma("pool", lambda e: e.dma_start(out=raw[:, hslot, :, 128:384], in_=qsrc[r, NT // dd - 1, :, u:u + 2, 128:384]),
                      ("raw", hslot), reads=["qkv_all"], writes=[("raw", hslot)])
            P.dma("pool", lambda e: e.dma_start(out=raw[:, s3, :, :], in_=qsrc[r, NT // dd + n, :, u:u + 2, :]),
                  ("raw", s3), reads=["qkv_all"], writes=[("raw", s3)])

            def tr_qk(e):
                last = None
                for hh in range(2):
                    e.transpose(out=psQ[:, hh * 128:(hh + 1) * 128], in_=raw[:, s3, hh, 0:128], identity=c.ident[:, :])
                    last = e.transpose(out=psQ[:, 256 + hh * 128:256 + (hh + 1) * 128], in_=raw[:, s3, hh, 128:256], identity=c.ident[:, :])
                    if first:
                        last = e.transpose(out=psQ[:, 512 + hh * 128:512 + (hh + 1) * 128], in_=raw[:, hslot, hh, 128:256],
                                           identity=c.ident[:, :])
                return last

            P.op("pe", tr_qk, reads=[("raw", s3), "consts"] + ([("raw", hslot)] if first else []), writes=[("bank", qb)])

        def S1b():
            P.op("act", lambda e: e.copy(out=qT[:, s2, :], in_=psQ[:, 0:256]), reads=[("bank", qb)], writes=[("qT", s2)])
            P.op("act", lambda e: e.copy(out=kT[:, s3, :], in_=psQ[:, 256:512]), reads=[("bank", qb)], writes=[("kT", s3)])
            if first:
                P.op("act", lambda e: e.copy(out=kT[:, hslot, :], in_=psQ[:, 512:768]), reads=[("bank", qb)], writes=[("kT", hslot)])

            def mm_s(e):
                last = None
                for hh in range(2):
                    hs = slice(hh * 128, (hh + 1) * 128)
                    e.matmul(c.bank[s2][:, hh * 256 + 128:hh * 256 + 256], lhsT=qT[:, s2, hs], rhs=kT[:, s3, hs], start=True, stop=True)
                    last = e.matmul(c.bank[s2][:, hh * 256:hh * 256 + 128], lhsT=qT[:, s2, hs], rhs=kT[:, sp3, hs], start=True, stop=True)
                return last

            P.op("pe", mm_s, reads=[("qT", s2), ("kT", s3), ("kT", sp3)], writes=[("bank", s2)])

        def S2():
            P.op("dve", lambda e: e.tensor_tensor(out=sm[:, s2, :], in0=c.bank[s2][:, :], in1=mask4[:, 1 if first else 0, :], op=ALU.add),
                 reads=[("bank", s2), "mask4"], writes=[("sm", s2)])
            P.op("dve", lambda e: e.reduce_max(out=st[:, 0:2], in_=sm[:, s2, :].rearrange("p (h n) -> p h n", h=2), axis=mybir.AxisListType.X),
                 reads=[("sm", s2)], writes=sk)
            P.op("dve", lambda e: e.tensor_scalar(out=st[:, 2:4], in0=st[:, 0:2], scalar1=-1.0, scalar2=0.0, op0=ALU.mult, op1=ALU.add),
                 reads=sk, writes=sk)
            P.op("dve", lambda e: e.memset(st[:, 4:6], 0.0), reads=sk, writes=sk)
            for hh in range(2):
                cs = slice(hh * 256, (hh + 1) * 256)
                P.op("act", lambda e, hh=hh, cs=cs: e.activation(out=pp[:, s2, cs], in_=sm[:, s2, cs], func=AF.Exp, bias=st[:, 2 + hh:3 + hh],
                                                                 accum_out=st[:, 4 + hh:5 + hh]),
                     reads=[("sm", s2)] + sk, writes=[("p", s2)] + sk)

        def S3():
            def tr_p(e):
                last = None
                for j in range(4):
                    last = e.transpose(out=psT[:, j * 128:(j + 1) * 128], in_=pp[:, s2, j * 128:(j + 1) * 128], identity=c.ident[:, :])
                return last

            P.op("pe", tr_p, reads=[("p", s2), "consts"], writes=[("bank", tb)])

        def S3b():
            P.op("act", lambda e: e.copy(out=pT[:, s2, :], in_=psT[:, 0:512]), reads=[("bank", tb)], writes=[("pT", s2)])

            def mm_o(e):
                last = None
                for hh in range(2):
                    oc = slice(hh * 128, (hh + 1) * 128)
                    e.matmul(c.bank[ob][:, oc], lhsT=pT[:, s2, hh * 256 + 128:hh * 256 + 256], rhs=raw[:, s3, hh, 256:384], start=True, stop=False)
                    last = e.matmul(c.bank[ob][:, oc], lhsT=pT[:, s2, hh * 256:hh * 256 + 128], rhs=raw[:, sp3, hh, 256:384], start=False, stop=True)
                return last

            P.op("pe", mm_o, reads=[("pT", s2), ("raw", s3), ("raw", sp3)], writes=[("bank", ob)])

        def S4():
            P.op("dve", lambda e: e.reciprocal(out=st[:, 6:8], in_=st[:, 4:6]), reads=sk, writes=sk)
            for hh in range(2):
                P.op("dve", lambda e, hh=hh: e.tensor_scalar(out=ost[:, s2, hh, 0:128], in0=c.bank[ob][:, hh * 128:(hh + 1) * 128],
                                                             scalar1=st[:, 6 + hh:7 + hh], scalar2=0.0, op0=ALU.mult, op1=ALU.add),
                     reads=[("bank", ob)] + sk, writes=[("ost", s2)])
            P.op("act", lambda e: e.activation(out=st[:, 8:10], in_=st[:, 4:6], func=AF.Ln), reads=sk, writes=sk)
            P.op("dve", lambda e: e.tensor_tensor(out=ost[:, s2, :, 128], in0=st[:, 8:10], in1=st[:, 0:2], op=ALU.add),
                 reads=sk, writes=[("ost", s2)])
            P.dma("sp", lambda e: e.dma_start(out=o_loc.rearrange("(n i d) (h c) -> d n i h c", i=128, d=dd, h=12)[r, n, :, u:u + 2, :],
                                              in_=ost[:, s2, :, :]), ("oo", s2), reads=[("ost", s2)])

        return [S1, S1b, S2, S3, S3b, S4]

    units = []
    it = 0
    nh = 0
    for u in range(0, 12, 2):
        for blk in range(NT):
            nbr = NT // DIL[u // 4]
            hs = 3 + nh % 2
            if blk % nbr == 0:
                nh += 1
            units.append(unit_stages(it, u, blk, hs))
            it += 1
    nu = len(units)
    units[0][0]()
    units[0][1]()
    units[0][2]()
    for k in range(nu):
        nxt = units[k + 1] if k + 1 < nu else None
        if nxt:
            nxt[0]()
        units[k][3]()
        if nxt:
            nxt[1]()
        units[k][4]()
        if nxt:
            nxt[2]()
        units[k][5]()


def emit_C_g(P, c, arC, o_all, gains_d, wo_d, win_d, wd_d, out_d):
    I32 = mybir.dt.int32

    def cv(lo, n, dt=F32):
        a = arC[:, lo:lo + n]
        return a if dt == F32 else a.bitcast(dt)

    idx2 = cv(0, 192, I32)
    oin = cv(192, 3168).rearrange("p (s h n) -> p s h n", s=2, h=12)
    og_ = cv(3360, 768, BF16)
    oT4 = cv(4128, 3072, BF16).rearrange("p (k n) -> p k n", k=12)
    alb = cv(7200, 96).rearrange("p (s n) -> p s n", s=2)
    P.dma("pool", lambda e: e.dma_start(out=c.wo[:, :, :], in_=wo_d), "wo", writes=["wo"])
    for t in range(NT):
        s = t % 2
        al = alb[:, s, :]
        P.dma("sp", lambda e, s=s, t=t: e.dma_start(out=oin[:, s, :, :], in_=o_all[t * 128:(t + 1) * 128, :].rearrange("p (h c) -> p h c", h=12)),
              ("oin", s), reads=["o_all"], writes=[("oin", s)])
        ak = [("al", s)]
        P.op("dve", lambda e, s=s, al=al: e.tensor_copy(out=al[:, 0:12], in_=oin[:, s, :, 128]), reads=[("oin", s)], writes=ak)
        l3 = al[:, 0:12].rearrange("p (g h) -> p g h", g=3)
        e3 = al[:, 12:24].rearrange("p (g h) -> p g h", g=3)
        a3 = al[:, 36:48].rearrange("p (g h) -> p g h", g=3)
        mx = al[:, 24:28]
        sm_ = al[:, 28:32]
        P.op("dve", lambda e, l3=l3, mx=mx: e.tensor_tensor(out=mx, in0=l3[:, 0, :], in1=l3[:, 1, :], op=ALU.max), reads=ak, writes=ak)
        P.op("dve", lambda e, l3=l3, mx=mx: e.tensor_tensor(out=mx, in0=mx, in1=l3[:, 2, :], op=ALU.max), reads=ak, writes=ak)
        for g in range(3):
            P.op("dve", lambda e, l3=l3, e3=e3, mx=mx, g=g: e.tensor_tensor(out=e3[:, g, :], in0=l3[:, g, :], in1=mx, op=ALU.subtract),
                 reads=ak, writes=ak)
        P.op("act", lambda e, al=al: e.activation(out=al[:, 12:24], in_=al[:, 12:24], func=AF.Exp), reads=ak, writes=ak)
        P.op("dve", lambda e, e3=e3, sm_=sm_: e.tensor_tensor(out=sm_, in0=e3[:, 0, :], in1=e3[:, 1, :], op=ALU.add), reads=ak, writes=ak)
        P.op("dve", lambda e, e3=e3, sm_=sm_: e.tensor_tensor(out=sm_, in0=sm_, in1=e3[:, 2, :], op=ALU.add), reads=ak, writes=ak)
        P.op("dve", lambda e, sm_=sm_: e.reciprocal(out=sm_, in_=sm_), reads=ak, writes=ak)
        for g in range(3):
            P.op("dve", lambda e, e3=e3, a3=a3, sm_=sm_, g=g: e.tensor_tensor(out=a3[:, g, :], in0=e3[:, g, :], in1=sm_, op=ALU.mult),
                 reads=ak, writes=ak)
        for hh in range(12):
            eng = "dve" if hh % 2 == 0 else "pool"
            P.op(eng, lambda e, s=s, hh=hh, al=al: e.tensor_scalar(
                out=og_[:, hh * 128:(hh + 1) * 128], in0=oin[:, s, hh, 0:128], scalar1=al[:, 36 + hh:37 + hh],
                scalar2=0.0, op0=ALU.mult, op1=ALU.add), reads=[("oin", s)] + ak, writes=["og"])
        for part, (k0, k1) in enumerate(((0, 8), (8, 12))):
            def tr(e, k0=k0, k1=k1, part=part):
                last = None
                for kc in range(k0, k1):
                    last = e.transpose(out=c.psT[part][:, (kc - k0) * 128:(kc - k0 + 1) * 128], in_=og_[:, kc * 128:(kc + 1) * 128],
                                       identity=c.ident[:, :])
                return last

            P.op("pe", tr, reads=["og", "consts"], writes=[("psT", part)])
            P.op("act", lambda e, k0=k0, k1=k1, part=part, t=t: e.copy(
                out=oT4[:, k0:k1, (t % 4) * 128:(t % 4 + 1) * 128],
                in_=c.psT[part][:, 0:(k1 - k0) * 128].rearrange("p (k n) -> p k n", k=k1 - k0)),
                reads=[("psT", part)], writes=[("oT4", t % 4)])
        if t % 4 == 3:
            for tt in range(t - 3, t + 1):
                for hf in range(2):
                    pb = c.d_ctr % 2
                    c.d_ctr += 1

                    def mmo(e, tt=tt, hf=hf, pb=pb):
                        last = None
                        for kc in range(12):
                            last = e.matmul(c.psD[pb][:, :], lhsT=oT4[:, kc, (tt % 4) * 128:(tt % 4 + 1) * 128],
                                            rhs=c.wo[:, kc, hf * 512:(hf + 1) * 512], start=(kc == 0), stop=(kc == 11))
                        return last

                    P.op("pe", mmo, reads=[("oT4", tt % 4), "wo"], writes=[("psD", pb)])
                    P.op("dve", lambda e, tt=tt, hf=hf, pb=pb: e.tensor_tensor(
                        out=c.h[:, tt, hf * 512:(hf + 1) * 512], in0=c.h[:, tt, hf * 512:(hf + 1) * 512], in1=c.psD[pb][:, :], op=ALU.add),
                        reads=[("psD", pb), ("h", tt)], writes=[("h", tt)])
    P.barrier()
    P.dma("sp", lambda e: e.dma_start(out=c.gcur[:, :], in_=gains_d[:, 3, :]), "cst8", writes=["consts"])
    emit_norm_T(P, c, lambda t: ("h", t), lambda t: c.h[:, t, :], c.gcur[:, :], "f1")
    emit_ffn(P, c, 1, win_d, wd_d)
    P.barrier()
    P.dma("sp", lambda e: e.dma_start(out=c.gcur[:, :], in_=gains_d[:, 4, :]), "cst9", writes=["consts"])
    P.op("dve", lambda e: e.memset(c.ssq[:, :], 0.0), writes=["ssq"])
    for t in range(NT):
        P.op("act", lambda e, t=t: e.activation(out=c.junk[:, :], in_=c.h[:, t, :], func=AF.Square,
                                                accum_out=c.ssq[:, t:t + 1]), reads=[("h", t)], writes=["junk", "ssq"])
    P.op("dve", lambda e: e.tensor_scalar(out=c.rstd[:, :], in0=c.ssq[:, :], scalar1=1.0 / D, scalar2=EPS,
                                          op0=ALU.mult, op1=ALU.add), reads=["ssq"], writes=["rstd"])
    emit_rsqrt(P, c)
    ov = out_d.rearrange("(t p) d -> p t d", p=128)
    for t in range(NT):
        P.op("dve", lambda e, t=t: e.scalar_tensor_tensor(
            out=c.h[:, t, :], in0=c.h[:, t, :], scalar=c.rstd[:, t:t + 1], in1=c.gcur[:, :],
            op0=ALU.mult, op1=ALU.mult), reads=[("h", t), "rstd", "consts"], writes=[("h", t)])
        if t % 4 == 3:
            q = t // 4
            P.dma("sp", lambda e, q=q: e.dma_start(out=ov[:, q * 4:(q + 1) * 4, :], in_=c.h[:, q * 4:(q + 1) * 4, :]),
                  ("o", q), reads=[("h", tt) for tt in range(q * 4, q * 4 + 4)])
    P.wait_dma_done("sp", [("o", q) for q in range(4)])


def _f(a):
    return np.ascontiguousarray(np.asarray(a, dtype=np.float32))


def _consts():
    s = np.arange(128)
    same = (s[:, None] // 64) == (s[None, :] // 64)
    ident = np.eye(128, dtype=np.float32)
    prefT = (same & (s[:, None] <= s[None, :])).astype(np.float32)
    sufT = (same & (s[:, None] > s[None, :])).astype(np.float32)
    ind2 = np.zeros((128, 128), np.float32)
    ind2[:64, 0] = 1.0
    ind2[64:, 1] = 1.0
    return np.ascontiguousarray(np.stack([ident, prefT, sufT, prefT, ind2], axis=1))


def _rope_tables(seg):
    inv_freq = (1.0 / (10000.0 ** (np.arange(0, 128, 2, dtype=np.float32) / np.float32(128)))).astype(np.float32)
    pos = (seg * T + np.arange(T)).astype(np.float32)
    ang = (pos[:, None] * inv_freq[None, :]).astype(np.float32)
    cs = np.stack([np.cos(ang), np.sin(ang)], axis=0).astype(np.float32)
    cs = np.tile(cs.reshape(2, NT, 128, 1, 64), (1, 1, 1, 4, 1)).reshape(2, NT, 128, 256)
    return np.ascontiguousarray(cs.transpose(2, 0, 1, 3))


def prep_common(inputs):
    gains = np.stack([_f(inputs["norm_mix"])[0], _f(inputs["norm_ffn"])[0], _f(inputs["norm_mix"])[1],
                      _f(inputs["norm_ffn"])[1], _f(inputs["final_norm"])], axis=0)
    gains = np.ascontiguousarray(np.broadcast_to(gains[None], (128, 5, D)))
    w_in = _f(inputs["ffn_w_in"])
    g = w_in[:, :, :DFF].reshape(2, 8, 128, NJ, 128)
    u = w_in[:, :, DFF:].reshape(2, 8, 128, NJ, 128)
    win = np.ascontiguousarray(np.concatenate([g, u], axis=-1).transpose(0, 3, 2, 1, 4))
    pk = lambda w: np.ascontiguousarray(w.reshape(w.shape[0] // 128, 128, w.shape[1]).transpose(1, 0, 2))
    wqkv = _f(inputs["attn_w_qkv"])[0].reshape(8, 128, 9, 512).transpose(2, 1, 0, 3)
    return {
        "gains": gains,
        "cst": _consts(),
        "lb_logits": np.ascontiguousarray(np.broadcast_to(_f(inputs["hgrn_lb_logits"])[None], (128, 3, D))),
        "gout": np.ascontiguousarray(np.broadcast_to(np.tile(_f(inputs["hgrn_out_norm"])[0], 8)[None], (128, D))),
        "hgrn_win": pk(_f(inputs["hgrn_w_in"])[0]),
        "hgrn_wout": pk(_f(inputs["hgrn_w_out"])[0]),
        "ffn_win": win,
        "ffn_wd": _f(inputs["ffn_w_down"]),
        "wqkv": np.ascontiguousarray(wqkv),
        "attn_wout": pk(_f(inputs["attn_w_out"])[0]),
    }


_NC_CACHE = {}


def _get(name, builder):
    if name not in _NC_CACHE:
        _NC_CACHE[name] = builder()
    return _NC_CACHE[name]


F_KEYS = ("gains", "cst", "lb_logits", "gout", "hgrn_win", "hgrn_wout", "ffn_win", "ffn_wd", "wqkv", "attn_wout")


def _consts_b():
    i = np.arange(128)
    ident = np.eye(128, dtype=np.float32)
    mprev = np.where(i[None, :] >= i[:, None], 0.0, NEG).astype(np.float32)
    mcur = np.where(i[None, :] <= i[:, None], 0.0, NEG).astype(np.float32)
    return np.ascontiguousarray(np.stack([ident, mprev, mcur], axis=1))


def run_G(inputs, common):
    nc = _get("G", build_G)
    x = _f(inputs["x"])
    cb = _consts_b()
    in_maps = []
    for cidx in range(NCORES):
        b, s = cidx // 4, cidx % 4
        xp = np.zeros((NPREV * 128, D), np.float32)
        n = min(s * T, NPREV * 128)
        if n:
            xp[NPREV * 128 - n:] = x[b, s * T - n:s * T]
        mh = cb[:, 1] if s > 0 else np.full((128, 128), NEG, np.float32)
        cstb = np.ascontiguousarray(np.stack([cb[:, 0], cb[:, 1], cb[:, 2], mh, cb[:, 2]], axis=1))
        rope = np.ascontiguousarray(np.stack([_rope_tables(max(s - 1, 0)), _rope_tables(s)], axis=0))
        m = {k: common[k] for k in F_KEYS}
        m.update(x=np.ascontiguousarray(x[b, s * T:(s + 1) * T]), x_prev=xp, rope=rope, cstb=cstb)
        in_maps.append(m)
    res = run_bass_kernel_spmd(nc, in_maps, core_ids=list(range(NCORES)))
    return np.stack([np.asarray(r["out"]) for r in res.results], axis=0).reshape(2, 4 * T, D).astype(np.float32)


def kernel(**inputs):
    common = prep_common(inputs)
    return run_G(inputs, common)
```
